# Optimizing a Trainium2 kernel written in Bass

```python
import jax, jax.numpy as jnp
from jax import lax
import numpy as np

D_MODEL = 1024
BATCH = 8
SEQ = 4096
DEPTH = 2

CTX_LEN = 256
GRID_W = 64
N_MIXERS = 2
N_HEADS = 8
N_KV_HEADS = 2
HEAD_DIM = D_MODEL // N_HEADS
Q_PER_KV = N_HEADS // N_KV_HEADS
KV_WIDTH = N_KV_HEADS * HEAD_DIM
ROPE_THETA = 10000.0
Q_BLOCK = 128
N_FOURIER_GROUPS = 4
FOURIER_GROUP = D_MODEL // N_FOURIER_GROUPS
D_FF = -(-8 * D_MODEL // (3 * 256)) * 256
N_MOD = 6
N_ATTN_LAYERS = (DEPTH + 1) // 2
N_FOURIER_LAYERS = DEPTH // 2
EPS = 1e-6

kernel_name = "hybrid_attn_fourier_dit_block"


def rms_norm(x, g):
    x32 = x.astype(jnp.float32)
    y = x32 * lax.rsqrt(jnp.mean(x32 * x32, axis=-1, keepdims=True) + EPS)
    return y.astype(x.dtype) * g


def modulate(h, shift, scale):
    return h * (1 + scale) + shift


def axial_rope_angles(rows):
    row = jnp.repeat(jnp.arange(rows), GRID_W).astype(jnp.float32)
    col = jnp.tile(jnp.arange(GRID_W), rows).astype(jnp.float32)
    n_freq = HEAD_DIM // 4
    inv_freq = ROPE_THETA ** (-jnp.arange(n_freq, dtype=jnp.float32) / n_freq)
    ang = jnp.concatenate([row[:, None] * inv_freq, col[:, None] * inv_freq], axis=-1)
    return jnp.cos(ang), jnp.sin(ang)


def apply_rope(x, cos, sin):
    half = HEAD_DIM // 2
    cos = cos[None, :, None, :].astype(x.dtype)
    sin = sin[None, :, None, :].astype(x.dtype)
    x1, x2 = x[..., :half], x[..., half:]
    return jnp.concatenate([x1 * cos - x2 * sin, x1 * sin + x2 * cos], axis=-1)


def gqa_attend(q, k, v):
    s = jnp.einsum('bqkgd,bskd->bkgqs', q, k, preferred_element_type=jnp.float32) * (HEAD_DIM ** -0.5)
    p = jax.nn.softmax(s, axis=-1).astype(v.dtype)
    return jnp.einsum('bkgqs,bskd->bqkgd', p, v)


def qkv_project(h, w_qkv, g_q, g_k):
    b, n, _ = h.shape
    qkv = h @ w_qkv
    q, k, v = jnp.split(qkv, [D_MODEL, D_MODEL + KV_WIDTH], axis=-1)
    q = rms_norm(q.reshape(b, n, N_HEADS, HEAD_DIM), g_q)
    k = rms_norm(k.reshape(b, n, N_KV_HEADS, HEAD_DIM), g_k)
    v = v.reshape(b, n, N_KV_HEADS, HEAD_DIM)
    return q, k, v


def attention_mixer(hx, hc, w_qkv, g_q, g_k, w_o, cos, sin, with_ctx_queries):
    b, s, _ = hx.shape
    n_ctx = hc.shape[1]
    qx, kx, vx = qkv_project(hx, w_qkv, g_q, g_k)
    qx = apply_rope(qx, cos, sin)
    kx = apply_rope(kx, cos, sin)
    qc, kc, vc = qkv_project(hc, w_qkv, g_q, g_k)
    k_all = jnp.concatenate([kx, kc], axis=1)
    v_all = jnp.concatenate([vx, vc], axis=1)
    nb = s // Q_BLOCK
    qb = qx.reshape(b, nb, Q_BLOCK, N_KV_HEADS, Q_PER_KV, HEAD_DIM).transpose(1, 0, 2, 3, 4, 5)
    ox = lax.map(lambda q_blk: gqa_attend(q_blk, k_all, v_all), qb)
    ox = ox.transpose(1, 0, 2, 3, 4, 5).reshape(b, s, D_MODEL) @ w_o
    oc = None
    if with_ctx_queries:
        qc = qc.reshape(b, n_ctx, N_KV_HEADS, Q_PER_KV, HEAD_DIM)
        oc = gqa_attend(qc, kc, vc).reshape(b, n_ctx, D_MODEL) @ w_o
    return ox, oc


def fourier_mixer(h, w_f, b_f):
    b, n, _ = h.shape
    hg = h.astype(jnp.float32).reshape(b, n, N_FOURIER_GROUPS, FOURIER_GROUP)
    f = jnp.fft.fft2(hg, axes=(1, 3), norm='ortho').real
    return f.reshape(b, n, D_MODEL).astype(h.dtype) @ w_f + b_f


def swiglu(h, w_gate_up, w_down):
    g, u = jnp.split(h @ w_gate_up, 2, axis=-1)
    return (jax.nn.silu(g) * u) @ w_down


def setup_inputs(seed: int = 0) -> dict:
    key = jax.random.key(seed)
    ks = jax.random.split(key, 20)
    f32 = jnp.float32
    nrm = lambda k, shape, s: jax.random.normal(k, shape, f32) * s
    D = D_MODEL
    return {
        'x': nrm(ks[0], (BATCH, SEQ, D), 1.0),
        'c': nrm(ks[1], (BATCH, D), 1.0),
        'ctx': nrm(ks[2], (BATCH, CTX_LEN, D), 1.0),
        'c_ctx': nrm(ks[3], (D,), 1.0),
        'w_mod': nrm(ks[4], (DEPTH, D, N_MOD * D), 0.5 * D ** -0.5),
        'b_mod': nrm(ks[5], (DEPTH, N_MOD * D), 0.01),
        'g_mix': 1.0 + nrm(ks[6], (DEPTH, D), 0.05),
        'g_ffn': 1.0 + nrm(ks[7], (DEPTH, D), 0.05),
        'w_qkv': nrm(ks[8], (N_ATTN_LAYERS, D, D + 2 * KV_WIDTH), D ** -0.5),
        'g_q': 1.0 + nrm(ks[9], (N_ATTN_LAYERS, HEAD_DIM), 0.05),
        'g_k': 1.0 + nrm(ks[10], (N_ATTN_LAYERS, HEAD_DIM), 0.05),
        'w_attn_out': nrm(ks[11], (N_ATTN_LAYERS, D, D), D ** -0.5),
        'w_fourier': nrm(ks[12], (N_FOURIER_LAYERS, D, D), D ** -0.5),
        'b_fourier': nrm(ks[13], (N_FOURIER_LAYERS, D), 0.01),
        'w_gate_up': nrm(ks[14], (DEPTH, D, 2 * D_FF), D ** -0.5),
        'w_down': nrm(ks[15], (DEPTH, D_FF, D), D_FF ** -0.5),
        'g_final': 1.0 + nrm(ks[16], (D,), 0.05),
    }


def reference(x, c, ctx, c_ctx, w_mod, b_mod, g_mix, g_ffn, w_qkv, g_q, g_k, w_attn_out,
              w_fourier, b_fourier, w_gate_up, w_down, g_final):
    b, s, d = x.shape
    ROWS = s // GRID_W
    cos, sin = axial_rope_angles(ROWS)
    silu_c = jax.nn.silu(c)
    silu_cc = jax.nn.silu(c_ctx)
    h_ctx = ctx
    for i in range(DEPTH):
        update_ctx = i < DEPTH - 1
        is_attn = i % N_MIXERS == 0
        j = i // N_MIXERS
        mx = (silu_c @ w_mod[i] + b_mod[i]).reshape(b, N_MOD, 1, d)
        mc = (silu_cc @ w_mod[i] + b_mod[i]).reshape(N_MOD, d)
        hx = modulate(rms_norm(x, g_mix[i]), mx[:, 0], mx[:, 1])
        hc = modulate(rms_norm(h_ctx, g_mix[i]), mc[0], mc[1]) if (update_ctx or is_attn) else None
        if is_attn:
            ox, oc = attention_mixer(hx, hc, w_qkv[j], g_q[j], g_k[j], w_attn_out[j], cos, sin, update_ctx)
        else:
            ox = fourier_mixer(hx, w_fourier[j], b_fourier[j])
            oc = fourier_mixer(hc, w_fourier[j], b_fourier[j]) if update_ctx else None
        x = x + mx[:, 2] * ox
        x = x + mx[:, 5] * swiglu(modulate(rms_norm(x, g_ffn[i]), mx[:, 3], mx[:, 4]), w_gate_up[i], w_down[i])
        if update_ctx:
            h_ctx = h_ctx + mc[2] * oc
            h_ctx = h_ctx + mc[5] * swiglu(modulate(rms_norm(h_ctx, g_ffn[i]), mc[3], mc[4]), w_gate_up[i], w_down[i])
    return rms_norm(x, g_final)
```

```python
import math
from contextlib import ExitStack

import numpy as np
import concourse.bass as bass
import concourse.mybir as mybir
from concourse.bass_utils import run_bass_kernel_spmd

F32 = mybir.dt.float32
BF16 = mybir.dt.bfloat16
AF = mybir.ActivationFunctionType
ALU = mybir.AluOpType
AX = mybir.AxisListType

D = 1024
SEQ = 4096
CTX = 256
NH = 8
NKV = 2
HD = 128
DFF = 2816
NFC = DFF // 128
NT = SEQ // 128
NTC = CTX // 128
NKT = NT + NTC
EPS = 1e-6
NCORES = 8


class Prog:
    def __init__(self, nc, stack):
        self.nc = nc
        self.eng = {"pe": nc.tensor, "act": nc.scalar, "dve": nc.vector, "pool": nc.gpsimd, "sp": nc.sync}
        self.semh = {}
        self.cnt = {}
        for e in self.eng:
            self.semh[e] = stack.enter_context(nc.semaphore("sem_" + e))
            self.cnt[e] = 0
        self.dq = {}
        for q, n in (("sp", 12), ("pool", 8), ("act", 4)):
            keys = []
            for i in range(n):
                k = "dma_%s_%d" % (q, i)
                self.semh[k] = stack.enter_context(nc.semaphore(k))
                self.cnt[k] = 0
                keys.append(k)
            self.dq[q] = {"keys": keys, "i": 0}
        self.known = {e: {} for e in self.eng}
        self.tok = {}
        self.ninst = 0

    def _need(self, e, reads, writes):
        need = {}

        def add(ev, same_ok):
            if ev is None:
                return
            sk, v = ev
            if sk == e and same_ok:
                return
            if need.get(sk, 0) < v:
                need[sk] = v

        for t in reads:
            st = self.tok.get(t)
            if st is not None:
                add(st["w"], False)
        for t in writes:
            st = self.tok.get(t)
            if st is not None:
                add(st["w"], True)
                for sk, v in st["r"].items():
                    add((sk, v), True)
        return need

    def _wait(self, e, need):
        eng = self.eng[e]
        kn = self.known[e]
        for sk, v in need.items():
            if kn.get(sk, 0) < v:
                eng.wait_ge(self.semh[sk], v)
                kn[sk] = v
                self.ninst += 1

    def _record(self, ev, reads, writes):
        for t in reads:
            st = self.tok.setdefault(t, {"w": None, "r": {}})
            if st["r"].get(ev[0], 0) < ev[1]:
                st["r"][ev[0]] = ev[1]
        for t in writes:
            self.tok[t] = {"w": ev, "r": {}}

    def op(self, e, fn, reads=(), writes=()):
        self._wait(e, self._need(e, reads, writes))
        inst = fn(self.eng[e])
        self.cnt[e] += 1
        inst.then_inc(self.semh[e], 1)
        self.ninst += 1
        self._record((e, self.cnt[e]), reads, writes)

    def dma(self, q, out, in_, reads=(), writes=(), **kw):
        dq = self.dq[q]
        k = dq["keys"][dq["i"] % len(dq["keys"])]
        dq["i"] += 1
        need = self._need(q, reads, writes)
        if self.cnt[k] > 0 and need.get(k, 0) < self.cnt[k]:
            need[k] = self.cnt[k]
        self._wait(q, need)
        inst = self.eng[q].dma_start(out=out, in_=in_, **kw)
        self.cnt[k] += 16
        inst.then_inc(self.semh[k], 16)
        self.ninst += 1
        self._record((k, self.cnt[k]), reads, writes)

    def barrier(self):
        for e in self.eng:
            need = {sk: v for sk, v in self.cnt.items() if v > 0}
            self._wait(e, need)
        self.tok = {}


class Ring:
    uid = 0

    def __init__(self, stack, nc, name, n, shape, dtype, psum=False):
        self.name = name
        self.n = n
        self.i = -1
        alloc = nc.psum_tensor if psum else nc.sbuf_tensor
        Ring.uid += 1
        self.tiles = [stack.enter_context(alloc("r%d_%s%d" % (Ring.uid, name, i), shape, dtype)) for i in range(n)]

    def next(self):
        self.i += 1
        s = self.i % self.n
        return self.tiles[s], (self.name, s)


class NS:
    pass


SCALE = float(HD) ** -0.5


def build_program(stop_after=None):
    nc = bass.Bass("TRN2", target_bir_lowering=False)
    C = NS()
    C.nc = nc
    C.stop_after = stop_after
    din = lambda name, shape, dt=F32: nc.dram_tensor(name, shape, dt, kind="ExternalInput").ap()
    dscr = lambda name, shape, dt=F32: nc.dram_tensor(name, shape, dt, kind="Internal").ap()
    C.x_d = din("x", [SEQ, D])
    C.ctx_d = din("ctx", [CTX, D])
    C.cpp_d = din("c_pp", [128, 8, 2])
    C.wmod_d = din("w_mod", [2, D, 6 * D])
    C.bmodpp_d = din("b_mod_pp", [2, 128, 48])
    C.bmod_d = din("b_mod", [2, 6 * D])
    C.gmixpp_d = din("g_mix_pp", [2, 128, 8])
    C.gffnpp_d = din("g_ffn_pp", [2, 128, 8])
    C.wqkv_d = din("w_qkv", [D, 1536])
    C.gq_d = din("g_q", [128])
    C.gk_d = din("g_k", [128])
    C.wo_d = din("w_o", [D, D])
    C.cos_d = din("rope_cos", [SEQ, 64])
    C.sin_d = din("rope_sin", [SEQ, 64])
    C.ident_d = din("ident", [128, 128])
    C.out_d = nc.dram_tensor("out", [SEQ, D], F32, kind="ExternalOutput").ap()
    C.QT_d = dscr("QT_scr", [NT, 128, 1024], BF16)
    C.gates_d = dscr("gates_scr", [4, 1024])
    C.wgu_d = din("w_gate_up", [2, D, 2 * DFF])
    C.wd_d = din("w_down", [2, DFF, D])
    C.wf_d = din("w_fourier", [D, D])
    C.bf_d = din("b_fourier", [D])
    C.gfin_d = din("g_final", [D])
    C.Fc_d = din("dft_fc", [128, 2, 512])
    C.M1_d = din("dft_m1", [128, 128])
    C.M2_d = din("dft_m2", [128, 64, 64])
    C.Z_d = dscr("Z_scr", [2, SEQ, D], BF16)
    C.T1_d = dscr("T1_scr", [2, 64, 64, D], BF16)
    C.f_d = dscr("f_scr", [SEQ, D], BF16)
    names = ["x1", "x2", "x3"]
    for i, nm in enumerate(names):
        setattr(C, nm + "_d", C.out_d if stop_after == "p%d" % (i + 2) and False else dscr(nm + "_scr", [SEQ, D]))
    if stop_after == "p2":
        C.x1_d = C.out_d
    if stop_after == "p3":
        C.x2_d = C.out_d
    if stop_after == "p6":
        C.x3_d = C.out_d

    with ExitStack() as gs:
        P = Prog(nc, gs)
        C.P = P
        C.gs = gs
        uid = [0]

        def _alloc(fn, pre, name, shape, dt, st):
            uid[0] += 1
            return st.enter_context(fn("%s%d_%s" % (pre, uid[0], name), shape, dt))

        C.sb = lambda name, shape, dt=F32, st=gs: _alloc(nc.sbuf_tensor, "sb", name, shape, dt, st)
        C.ps = lambda name, shape, dt=F32, st=gs: _alloc(nc.psum_tensor, "ps", name, shape, dt, st)
        sb = C.sb
        C.modpp = sb("modpp", [128, 2, 4, 8, 2])
        C.gmix = sb("gmix", [128, 2, 8])
        C.gffn = sb("gffn", [128, 2, 8])
        C.Amod = sb("Amod", [128, 5, 8])
        C.Bmod = sb("Bmod", [128, 5, 8])
        C.ident_f = sb("ident_f", [128, 128])
        C.ident = sb("ident", [128, 128], BF16)
        C.ones = sb("ones", [128, 128], BF16)
        P.dma("sp", C.ident_f[:], C.ident_d, writes=["ident_f"])
        P.op("dve", lambda e: e.tensor_copy(out=C.ident[:], in_=C.ident_f[:]), reads=["ident_f"], writes=["ident"])
        P.op("dve", lambda e: e.memset(C.ones[:], 1.0), writes=["ones"])
        C.eps = sb("eps", [128, 1])
        P.op("dve", lambda e: e.memset(C.eps[:], EPS), writes=["eps"])

        phase0(C)
        if stop_after == "p0":
            return nc
        with ExitStack() as st12:
            C.KT = sb("KT", [128, NKV, NKT * 128], BF16, st=st12)
            C.Vs = sb("Vs", [128, NKT, NKV * HD], BF16, st=st12)
            phase1(C)
            phase2(C)
        if stop_after == "p2":
            return nc
        ffn_phase(C, 0, C.x1_d, C.x2_d)
        if stop_after == "p3":
            return nc
        phase4(C)
        phase5(C)
        phase6(C)
        if stop_after == "p6":
            return nc
        ffn_phase(C, 1, C.x3_d, C.out_d, final=True)
        print("instructions:", P.ninst)
    return nc


def phase0(C):
    nc, P, sb, ps = C.nc, C.P, C.sb, C.ps
    modpp, gmix, gffn, Amod, Bmod = C.modpp, C.gmix, C.gffn, C.Amod, C.Bmod
    with ExitStack() as st:
        gates = sb("gates", [128, 4, 1024], st=st)
        cpp = sb("cpp", [128, 8, 2], st=st)
        sc = sb("sc", [128, 8, 2], st=st)
        sig = sb("sig", [128, 8, 2], st=st)
        scb = sb("scb", [128, 8, 128], st=st)
        bpp = sb("bpp", [128, 2, 48], st=st)
        brow = sb("brow", [128, 4, 1024], st=st)
        wring = Ring(st, nc, "wm", 2, [128, 8, 1024], F32)
        pp_ps = ps("pp_ps", [128, 512], st=st)
        row_ps = Ring(st, nc, "row_ps", 2, [128, 512], F32, psum=True)

        P.dma("sp", cpp[:], C.cpp_d, writes=["cpp"])
        P.dma("sp", bpp[:], C.bmodpp_d.rearrange("l p f -> p l f"), writes=["bpp"])
        P.dma("sp", gmix[:], C.gmixpp_d.rearrange("l p f -> p l f"), writes=["gmix"])
        P.dma("sp", gffn[:], C.gffnpp_d.rearrange("l p f -> p l f"), writes=["gffn"])
        for l in range(2):
            for gi, m in enumerate((2, 5)):
                P.dma("sp", brow[:, l * 2 + gi, :],
                      C.bmod_d[l, m * 1024:(m + 1) * 1024].partition_broadcast(128),
                      writes=[("brow", l * 2 + gi)])
        P.op("act", lambda e: e.activation(out=sig[:], in_=cpp[:], func=AF.Sigmoid), reads=["cpp"], writes=["sig"])
        P.op("dve", lambda e: e.tensor_tensor(out=sc[:], in0=cpp[:], in1=sig[:], op=ALU.mult),
             reads=["cpp", "sig"], writes=["sc"])
        P.op("dve", lambda e: e.tensor_copy(out=scb[:], in_=sc[:, :, 0:1].to_broadcast([128, 8, 128])),
             reads=["sc"], writes=["scb"])
        wv = C.wmod_d.rearrange("l (kc p) f -> l p kc f", p=128)
        for l in range(2):
            for m in range(6):
                wt, wtok = wring.next()
                P.dma("sp", wt[:], wv[l, :, :, m * 1024:(m + 1) * 1024], writes=[wtok])
                if m in (2, 5):
                    gi = l * 2 + (0 if m == 2 else 1)
                    for h in range(2):
                        rp, rtok = row_ps.next()
                        for kc in range(8):
                            P.op("pe", lambda e, kc=kc, rp=rp, wt=wt, h=h: e.matmul(
                                rp[:], lhsT=scb[:, kc, :], rhs=wt[:, kc, h * 512:(h + 1) * 512],
                                start=(kc == 0), stop=(kc == 7)),
                                reads=["scb", wtok], writes=[rtok])
                        P.op("dve", lambda e, rp=rp, gi=gi, h=h: e.tensor_tensor(
                            out=gates[:, gi, h * 512:(h + 1) * 512], in0=rp[:],
                            in1=brow[:, gi, h * 512:(h + 1) * 512], op=ALU.add),
                            reads=[rtok, ("brow", gi)], writes=[("gates", gi, h)])
                        if h == 1:
                            P.dma("sp", C.gates_d[gi:gi + 1, :], gates[0:1, gi, :],
                                  reads=[("gates", gi, 0), ("gates", gi, 1)], writes=[("gates_d", gi)])
                else:
                    mi = {0: 0, 1: 1, 3: 2, 4: 3}[m]
                    for fc in range(8):
                        o = ((l * 4 + mi) * 8 + fc) * 2
                        for kc in range(8):
                            P.op("pe", lambda e, kc=kc, fc=fc, wt=wt, o=o: e.matmul(
                                pp_ps[:, o:o + 2], lhsT=wt[:, kc, fc * 128:(fc + 1) * 128], rhs=sc[:, kc, :],
                                start=(kc == 0), stop=(kc == 7)),
                                reads=["sc", wtok], writes=["pp_ps"])
                    P.op("dve", lambda e, l=l, mi=mi, m=m: e.tensor_tensor(
                        out=modpp[:, l, mi, :, :],
                        in0=pp_ps[:, (l * 4 + mi) * 16:(l * 4 + mi + 1) * 16].rearrange("p (f t) -> p f t", t=2),
                        in1=bpp[:, l, m * 8:(m + 1) * 8].unsqueeze(2).to_broadcast([128, 8, 2]), op=ALU.add),
                        reads=["pp_ps", "bpp"], writes=[("modpp", l, mi)])
        combos = [(0, 0, 1, 0, gmix, 0), (1, 0, 1, 0, gmix, 1), (2, 0, 3, 2, gffn, 0),
                  (3, 1, 1, 0, gmix, 0), (4, 1, 3, 2, gffn, 0)]
        for idx, l, m_sc, m_sh, g, col in combos:
            P.op("dve", lambda e, idx=idx, l=l, m_sc=m_sc, g=g, col=col: e.scalar_tensor_tensor(
                out=Amod[:, idx, :], in0=modpp[:, l, m_sc, :, col], scalar=1.0, in1=g[:, l, :],
                op0=ALU.add, op1=ALU.mult),
                reads=[("modpp", l, m_sc), "gmix", "gffn"], writes=[("Amod", idx)])
            P.op("dve", lambda e, idx=idx, l=l, m_sh=m_sh, col=col: e.tensor_copy(
                out=Bmod[:, idx, :], in_=modpp[:, l, m_sh, :, col]),
                reads=[("modpp", l, m_sh)], writes=[("Bmod", idx)])
        P.barrier()


def rms_stage(C, R, src_ap, a_idx, hT, hTtok, col0=0):
    P = C.P
    xt, xtok = R["xt"].next()
    P.dma("sp", xt[:], src_ap, writes=[xtok])
    rms_from_tile(C, R, xt, xtok, a_idx, hT, hTtok, col0)
    return xt, xtok


def rms_from_tile(C, R, xt, xtok, a_idx, hT, hTtok, col0=0):
    P = C.P
    ss, sstok = R["ss"].next()
    xn, xntok = R["xn"].next()
    tp, tptok = R["tp"].next()
    junk = R["junk"]
    P.op("act", lambda e: e.activation(out=junk[:], in_=xt[:], func=AF.Square, accum_out=ss[:, 0:1]),
         reads=[xtok], writes=["junk", (sstok, 0)])
    P.op("act", lambda e: e.activation(out=ss[:, 1:2], in_=ss[:, 0:1], func=AF.Sqrt, scale=1.0 / D, bias=C.eps[:, 0:1]),
         reads=[(sstok, 0), "eps"], writes=[(sstok, 1)])
    P.op("dve", lambda e: e.reciprocal(out=ss[:, 2:3], in_=ss[:, 1:2]), reads=[(sstok, 1)], writes=[(sstok, 2)])
    P.op("act", lambda e: e.activation(out=xn[:], in_=xt[:], func=AF.Identity, scale=ss[:, 2:3]),
         reads=[xtok, (sstok, 2)], writes=[xntok])
    for kc in range(8):
        P.op("pe", lambda e, kc=kc: e.transpose(out=tp[:, kc * 128:(kc + 1) * 128], in_=xn[:, kc * 128:(kc + 1) * 128],
                                                 identity=C.ident[:]),
             reads=[xntok, "ident"], writes=[tptok])
    for kc in range(8):
        P.op("dve", lambda e, kc=kc: e.tensor_scalar(
            out=hT[:, kc, col0:col0 + 128], in0=tp[:, kc * 128:(kc + 1) * 128],
            scalar1=C.Amod[:, a_idx, kc:kc + 1], scalar2=C.Bmod[:, a_idx, kc:kc + 1], op0=ALU.mult, op1=ALU.add),
            reads=[tptok, ("Amod", a_idx), ("Bmod", a_idx)], writes=[hTtok])


def load_weight_bf16(C, st, name, dst, dst_tok_fn, src_view, nchunks, chunk_shape, engines=("dve", "pool")):
    P, nc = C.P, C.nc
    ring = Ring(st, nc, name + "_stg", 2, chunk_shape, F32)
    for i in range(nchunks):
        t, tok = ring.next()
        P.dma("sp", t[:], src_view(i), writes=[tok])
        eng = engines[i % len(engines)]
        P.op(eng, lambda e, t=t, i=i: e.tensor_copy(out=dst(i), in_=t[:]), reads=[tok], writes=[dst_tok_fn(i)])


def phase1(C):
    nc, P, sb, ps = C.nc, C.P, C.sb, C.ps
    with ExitStack() as st:
        wqkv = sb("wqkv", [128, 8, 1536], BF16, st=st)
        cos = sb("cos", [128, NT, 64], st=st)
        sin = sb("sin", [128, NT, 64], st=st)
        gqk = sb("gqk", [128, 10, 128], st=st)
        P.dma("sp", cos[:], C.cos_d.rearrange("(t p) f -> p t f", p=128), writes=["cos"])
        P.dma("sp", sin[:], C.sin_d.rearrange("(t p) f -> p t f", p=128), writes=["sin"])
        for h in range(10):
            P.dma("sp", gqk[:, h, :], (C.gq_d if h < 8 else C.gk_d).partition_broadcast(128), writes=["gqk"])
        wv = C.wqkv_d.rearrange("(kc p) f -> p kc f", p=128)
        load_weight_bf16(C, st, "wqkv", lambda i: wqkv[:, i, :], lambda i: ("wqkv", i),
                         lambda i: wv[:, i, :], 8, [128, 1536])
        R = {
            "xt": Ring(st, nc, "xt", 3, [128, D], F32),
            "ss": Ring(st, nc, "ss", 3, [128, 4], F32),
            "xn": Ring(st, nc, "xn", 2, [128, D], BF16),
            "tp": Ring(st, nc, "tp", 2, [128, D], BF16, psum=True),
            "junk": sb("junk", [128, D], BF16, st=st),
        }
        hTr = Ring(st, nc, "hT", 2, [128, 8, 128], BF16)
        qkv_ps = [ps("qkv_ps%d" % i, [128, 512], st=st) for i in range(3)]
        qkvr = Ring(st, nc, "qkv_sb", 3, [128, 1536], F32)
        sqb = sb("sqb", [128, 1280], st=st)
        ssq = Ring(st, nc, "ssq", 2, [128, 3, 10], F32)
        qn = sb("qn", [128, 10, 128], st=st)
        qg = sb("qg", [128, 10, 128], st=st)
        t1 = sb("t1", [128, 10, 64], st=st)
        t2 = sb("t2", [128, 10, 64], st=st)
        t3 = sb("t3", [128, 10, 64], st=st)
        t4 = sb("t4", [128, 10, 64], st=st)
        qrr = Ring(st, nc, "qr", 2, [128, 10, 128], BF16)
        qT_ps = ps("qT_ps", [128, 1024], BF16, st=st)
        kT_ps = ps("kT_ps", [128, 1024], BF16, st=st)
        qTr = Ring(st, nc, "qT_sb", 2, [128, 1024], BF16)

        def stage1(T):
            lat = T < NT
            src = C.x_d[T * 128:(T + 1) * 128, :] if lat else C.ctx_d[(T - NT) * 128:(T - NT + 1) * 128, :]
            hT, hTtok = hTr.next()
            rms_stage(C, R, src, 0 if lat else 1, hT, hTtok)
            banks = (0, 1, 2) if lat else (2,)
            for nb in banks:
                for kc in range(8):
                    P.op("pe", lambda e, nb=nb, kc=kc: e.matmul(
                        qkv_ps[nb][:], lhsT=hT[:, kc, :], rhs=wqkv[:, kc, nb * 512:(nb + 1) * 512],
                        start=(kc == 0), stop=(kc == 7)),
                        reads=[hTtok, ("wqkv", kc)], writes=[("qkv_ps", nb)])
            qs, qstok = qkvr.next()
            for nb in banks:
                P.op("act", lambda e, nb=nb: e.activation(out=qs[:, nb * 512:(nb + 1) * 512], in_=qkv_ps[nb][:],
                                                          func=AF.Identity),
                     reads=[("qkv_ps", nb)], writes=[(qstok, nb)])
            return qs, qstok

        def stage2(T, qs, qstok):
            lat = T < NT
            h0 = 0 if lat else 8
            nh = 10 - h0
            lo = h0 * 128
            rd = [(qstok, nb) for nb in ((0, 1, 2) if lat else (2,))]
            sq3, sqtok = ssq.next()
            P.op("dve", lambda e: e.tensor_tensor(out=sqb[:, lo:1280], in0=qs[:, lo:1280], in1=qs[:, lo:1280], op=ALU.mult),
                 reads=rd, writes=["sqb"])
            P.op("dve", lambda e: e.tensor_reduce(out=sq3[:, 0, h0:10],
                                                  in_=sqb[:, lo:1280].rearrange("p (h d) -> p h d", d=128),
                                                  axis=AX.X, op=ALU.add),
                 reads=["sqb"], writes=[(sqtok, 0)])
            P.op("act", lambda e: e.activation(out=sq3[:, 1, h0:10], in_=sq3[:, 0, h0:10], func=AF.Sqrt,
                                               scale=1.0 / HD, bias=C.eps[:, 0:1]),
                 reads=[(sqtok, 0), "eps"], writes=[(sqtok, 1)])
            P.op("dve", lambda e: e.reciprocal(out=sq3[:, 2, h0:10], in_=sq3[:, 1, h0:10]),
                 reads=[(sqtok, 1)], writes=[(sqtok, 2)])
            P.op("dve", lambda e: e.tensor_tensor(
                out=qn[:, h0:10, :], in0=qs[:, lo:1280].rearrange("p (h d) -> p h d", d=128),
                in1=sq3[:, 2, h0:10].unsqueeze(2).to_broadcast([128, nh, 128]), op=ALU.mult),
                reads=rd + [(sqtok, 2)], writes=["qn"])
            P.op("pool", lambda e: e.tensor_tensor(out=qg[:, h0:10, :], in0=qn[:, h0:10, :], in1=gqk[:, h0:10, :], op=ALU.mult),
                 reads=["qn", "gqk"], writes=["qg"])
            qr, qrtok = qrr.next()
            if lat:
                cb = cos[:, T, :].unsqueeze(1).to_broadcast([128, 10, 64])
                sbb = sin[:, T, :].unsqueeze(1).to_broadcast([128, 10, 64])
                x1 = qg[:, :, 0:64]
                x2 = qg[:, :, 64:128]
                P.op("dve", lambda e: e.tensor_tensor(out=t1[:], in0=x1, in1=cb, op=ALU.mult), reads=["qg", "cos"], writes=["t1"])
                P.op("pool", lambda e: e.tensor_tensor(out=t2[:], in0=x2, in1=sbb, op=ALU.mult), reads=["qg", "sin"], writes=["t2"])
                P.op("pool", lambda e: e.tensor_tensor(out=t3[:], in0=x1, in1=sbb, op=ALU.mult), reads=["qg", "sin"], writes=["t3"])
                P.op("dve", lambda e: e.tensor_tensor(out=t4[:], in0=x2, in1=cb, op=ALU.mult), reads=["qg", "cos"], writes=["t4"])
                P.op("dve", lambda e: e.tensor_tensor(out=qr[:, :, 0:64], in0=t1[:], in1=t2[:], op=ALU.subtract),
                     reads=["t1", "t2"], writes=[(qrtok, 0)])
                P.op("pool", lambda e: e.tensor_tensor(out=qr[:, :, 64:128], in0=t3[:], in1=t4[:], op=ALU.add),
                     reads=["t3", "t4"], writes=[(qrtok, 1)])
            else:
                P.op("dve", lambda e: e.tensor_copy(out=qr[:, 8:10, :], in_=qg[:, 8:10, :]), reads=["qg"],
                     writes=[(qrtok, 0), (qrtok, 1)])
            if lat:
                for h in range(8):
                    P.op("pe", lambda e, h=h: e.transpose(out=qT_ps[:, h * 128:(h + 1) * 128], in_=qr[:, h, :],
                                                           identity=C.ident[:]),
                         reads=[(qrtok, 0), (qrtok, 1), "ident"], writes=["qT_ps"])
            for j in range(2):
                P.op("pe", lambda e, j=j: e.transpose(out=kT_ps[:, j * 128:(j + 1) * 128], in_=qr[:, 8 + j, :],
                                                       identity=C.ident[:]),
                     reads=[(qrtok, 0), (qrtok, 1), "ident"], writes=["kT_ps"])
            if lat:
                qT, qTtok = qTr.next()
                P.op("act", lambda e: e.activation(out=qT[:], in_=qT_ps[:], func=AF.Identity), reads=["qT_ps"], writes=[qTtok])
                P.dma("pool", C.QT_d[T], qT[:], reads=[qTtok], writes=[("QT_d", T)])
            P.op("dve", lambda e: e.tensor_copy(
                out=C.KT[:, :, T * 128:(T + 1) * 128], in_=kT_ps[:, 0:256].rearrange("p (j t) -> p j t", t=128)),
                reads=["kT_ps"], writes=[("KT", T)])
            P.op("pool", lambda e: e.tensor_copy(out=C.Vs[:, T, :], in_=qs[:, 1280:1536]), reads=[(qstok, 2)],
                 writes=[("V", T)])

        prev = None
        for T in range(NKT):
            cur = (T,) + stage1(T)
            if prev is not None:
                stage2(*prev)
            prev = cur
        stage2(*prev)
        P.barrier()


def phase2(C):
    nc, P, sb, ps = C.nc, C.P, C.sb, C.ps
    with ExitStack() as st:
        wo = sb("wo", [128, 8, 1024], BF16, st=st)
        wv = C.wo_d.rearrange("(h p) f -> p h f", p=128)
        load_weight_bf16(C, st, "wo", lambda i: wo[:, i, :], lambda i: ("wo", i), lambda i: wv[:, i, :], 8, [128, 1024])
        qTr = Ring(st, nc, "qT2", 2, [128, 1024], BF16)
        xtr = Ring(st, nc, "xt2", 3, [128, D], F32)
        PTr = Ring(st, nc, "PT", 4, [128, 512], BF16)
        Sr = Ring(st, nc, "S_ps", 2, [128, 512], F32, psum=True)
        accr = Ring(st, nc, "acc_ps", 2, [128, 512], F32, psum=True)
        denr = Ring(st, nc, "den_ps", 2, [128, 512], F32, psum=True)
        wo_ps = [ps("wo_ps%d" % i, [128, 512], st=st) for i in range(2)]
        recr = Ring(st, nc, "rec", 2, [128, 512], F32)
        OTr = Ring(st, nc, "OT", 2, [128, 8, 128], BF16)
        x1r = Ring(st, nc, "x1t", 2, [128, D], F32)
        gate = sb("gate_p2", [128, D], st=st)
        P.dma("sp", gate[:], C.gates_d[0].partition_broadcast(128), writes=["gate"])

        steps = [(T, kvh, kt) for T in range(NT) for kvh in range(NKV) for kt in range(NKT)]
        state = {}
        pending = []

        def load_tile(T):
            qT, qTtok = qTr.next()
            xt, xtok = xtr.next()
            P.dma("sp", qT[:], C.QT_d[T], reads=[("QT_d", T)], writes=[qTtok])
            P.dma("sp", xt[:], C.x_d[T * 128:(T + 1) * 128, :], writes=[xtok])
            OT, OTtok = OTr.next()
            state[T] = dict(qT=qT, qTtok=qTtok, xt=xt, xtok=xtok, OT=OT, OTtok=OTtok)

        def emit_S(i):
            T, kvh, kt = steps[i]
            if kvh == 0 and kt == 0:
                if T not in state:
                    load_tile(T)
                if T + 1 < NT and (T + 1) not in state:
                    load_tile(T + 1)
            s = state[T]
            S, Stok = Sr.next()
            P.op("pe", lambda e: e.matmul(S[:], lhsT=C.KT[:, kvh, kt * 128:(kt + 1) * 128],
                                           rhs=s["qT"][:, kvh * 512:(kvh + 1) * 512], start=True, stop=True),
                 reads=[("KT", kt), s["qTtok"]], writes=[Stok])
            PT, PTtok = PTr.next()
            P.op("act", lambda e: e.activation(out=PT[:], in_=S[:], func=AF.Exp, scale=SCALE), reads=[Stok], writes=[PTtok])
            return PT, PTtok

        def emit_PV(i, PT, PTtok):
            T, kvh, kt = steps[i]
            s = state[T]
            if kt == 0:
                s["acc"], s["acctok"] = accr.next()
                s["den"], s["dentok"] = denr.next()
            acc, den = s["acc"], s["den"]
            P.op("pe", lambda e: e.matmul(acc[:], lhsT=C.Vs[:, kt, kvh * 128:(kvh + 1) * 128], rhs=PT[:],
                                           start=(kt == 0), stop=(kt == NKT - 1)),
                 reads=[("V", kt), PTtok], writes=[s["acctok"]])
            P.op("pe", lambda e: e.matmul(den[:], lhsT=C.ones[:], rhs=PT[:], start=(kt == 0), stop=(kt == NKT - 1)),
                 reads=["ones", PTtok], writes=[s["dentok"]])
            if kt == NKT - 1:
                rec, rectok = recr.next()
                OT, OTtok = s["OT"], s["OTtok"]
                P.op("dve", lambda e: e.reciprocal(out=rec[:], in_=den[:]), reads=[s["dentok"]], writes=[rectok])
                P.op("dve", lambda e: e.tensor_tensor(
                    out=OT[:, kvh * 4:(kvh + 1) * 4, :].rearrange("p h q -> p (h q)"), in0=acc[:], in1=rec[:], op=ALU.mult),
                    reads=[s["acctok"], rectok], writes=[(OTtok, kvh)])
                if kvh == NKV - 1:
                    pending.append([2, lambda T=T: emit_wo(T)])

        def emit_wo(T):
            s = state[T]
            OT, OTtok = s["OT"], s["OTtok"]
            x1t, x1tok = x1r.next()
            for half in range(2):
                for h in range(8):
                    P.op("pe", lambda e, half=half, h=h: e.matmul(
                        wo_ps[half][:], lhsT=OT[:, h, :], rhs=wo[:, h, half * 512:(half + 1) * 512],
                        start=(h == 0), stop=(h == 7)),
                        reads=[(OTtok, h // 4), ("wo", h)], writes=[("wo_ps", half)])
            for half in range(2):
                sl = slice(half * 512, (half + 1) * 512)
                P.op("dve", lambda e, half=half, sl=sl: e.tensor_tensor(
                    out=x1t[:, sl], in0=wo_ps[half][:], in1=gate[:, sl], op=ALU.mult),
                    reads=[("wo_ps", half), "gate"], writes=[(x1tok, half)])
                P.op("pool", lambda e, sl=sl: e.tensor_tensor(out=x1t[:, sl], in0=x1t[:, sl], in1=s["xt"][:, sl], op=ALU.add),
                     reads=[(x1tok, half), s["xtok"]], writes=[(x1tok, half)])
            P.dma("pool", C.x1_d[T * 128:(T + 1) * 128, :], x1t[:], reads=[(x1tok, 0), (x1tok, 1)], writes=[("x1_d", T)])
            del state[T]

        def tick():
            for p in list(pending):
                p[0] -= 1
                if p[0] <= 0:
                    pending.remove(p)
                    p[1]()

        cur = emit_S(0)
        for i in range(len(steps)):
            nxt = emit_S(i + 1) if i + 1 < len(steps) else None
            emit_PV(i, *cur)
            tick()
            cur = nxt
        while pending:
            tick()
        P.barrier()


def ffn_phase(C, layer, src_d, dst_d, final=False):
    nc, P, sb, ps = C.nc, C.P, C.sb, C.ps
    a_idx = 2 if layer == 0 else 4
    gi = layer * 2 + 1
    with ExitStack() as st:
        wgu = sb("wgu", [128, 8, 2 * DFF], BF16, st=st)
        wd = sb("wd", [128, NFC, D], BF16, st=st)
        wguv = C.wgu_d[layer].rearrange("(kc p) f -> p kc f", p=128)
        wdv = C.wd_d[layer].rearrange("(fc p) d -> p fc d", p=128)
        for kc in range(8):
            for hh in range(2):
                P.dma("pool", wgu[:, kc, hh * DFF:(hh + 1) * DFF], wguv[:, kc, hh * DFF:(hh + 1) * DFF],
                      writes=[("wgu", kc, hh)])
        for fc in range(NFC):
            P.dma("pool", wd[:, fc, :], wdv[:, fc, :], writes=[("wd", fc)])
        gate = sb("gate_ffn", [128, D], st=st)
        P.dma("sp", gate[:], C.gates_d[gi].partition_broadcast(128), reads=[("gates_d", gi)], writes=["gate"])
        if final:
            gfin = sb("gfin", [128, D], st=st)
            P.dma("sp", gfin[:], C.gfin_d.partition_broadcast(128), writes=["gfin"])
        R = {
            "xt": Ring(st, nc, "fxt", 2, [128, D], F32),
            "ss": Ring(st, nc, "fss", 3, [128, 4], F32),
            "xn": Ring(st, nc, "fxn", 2, [128, D], BF16),
            "tp": Ring(st, nc, "ftp", 2, [128, D], BF16, psum=True),
            "junk": sb("fjunk", [128, D], BF16, st=st),
        }
        hT2 = sb("hT2", [128, 8, 512], BF16, st=st)
        aT = sb("aT", [128, NFC, 512], BF16, st=st)
        Gr = Ring(st, nc, "G_ps", 2, [128, 512], F32, psum=True)
        Ur = Ring(st, nc, "U_ps", 2, [128, 512], F32, psum=True)
        Dr = Ring(st, nc, "D_ps", 2, [128, 512], F32, psum=True)
        sgr = Ring(st, nc, "sg", 2, [128, 512], F32)
        xrr = Ring(st, nc, "xres", 2, [128, D], F32)
        xor_ = Ring(st, nc, "xo", 2, [128, D], F32)
        fss = Ring(st, nc, "finss", 2, [128, 4], F32)
        print("ffn sbuf remaining", nc.sbuf_bytes_remaining)
        NB = NT // 4

        def emit_rms(bk):
            for j in range(4):
                T = bk * 4 + j
                rms_stage(C, R, src_d[T * 128:(T + 1) * 128, :], a_idx, hT2, ("hT2", j), col0=j * 128)

        def emit_gu(bk):
            for fc in range(NFC):
                G, Gtok = Gr.next()
                U, Utok = Ur.next()
                for (ps_t, ps_tok, c0) in ((G, Gtok, fc * 128), (U, Utok, DFF + fc * 128)):
                    hh = 0 if c0 < DFF else 1
                    for kc in range(8):
                        P.op("pe", lambda e, ps_t=ps_t, c0=c0, kc=kc: e.matmul(
                            ps_t[:], lhsT=wgu[:, kc, c0:c0 + 128], rhs=hT2[:, kc, :], start=(kc == 0), stop=(kc == 7)),
                            reads=[("wgu", kc, hh)] + [("hT2", j) for j in range(4)], writes=[ps_tok])
                sg, sgtok = sgr.next()
                P.op("act", lambda e, sg=sg, G=G: e.activation(out=sg[:], in_=G[:], func=AF.Silu), reads=[Gtok], writes=[sgtok])
                P.op("dve", lambda e, sg=sg, U=U, fc=fc: e.tensor_tensor(out=aT[:, fc, :], in0=U[:], in1=sg[:], op=ALU.mult),
                     reads=[Utok, sgtok], writes=[("aT", fc)])

        def emit_down(bk):
            for j in range(4):
                T = bk * 4 + j
                xr, xrtok = xrr.next()
                P.dma("sp", xr[:], src_d[T * 128:(T + 1) * 128, :], writes=[xrtok])
                xo, xotok = xor_.next()
                for half in range(2):
                    Dp, Dtok = Dr.next()
                    sl = slice(half * 512, (half + 1) * 512)
                    for fc in range(NFC):
                        P.op("pe", lambda e, Dp=Dp, fc=fc, sl=sl, j=j: e.matmul(
                            Dp[:], lhsT=aT[:, fc, j * 128:(j + 1) * 128], rhs=wd[:, fc, sl],
                            start=(fc == 0), stop=(fc == NFC - 1)),
                            reads=[("aT", fc), ("wd", fc)], writes=[Dtok])
                    P.op("dve", lambda e, Dp=Dp, sl=sl, xo=xo: e.tensor_tensor(out=xo[:, sl], in0=Dp[:], in1=gate[:, sl], op=ALU.mult),
                         reads=[Dtok, "gate"], writes=[(xotok, half)])
                    P.op("pool", lambda e, sl=sl, xo=xo, xr=xr: e.tensor_tensor(out=xo[:, sl], in0=xo[:, sl], in1=xr[:, sl], op=ALU.add),
                         reads=[(xotok, half), xrtok], writes=[(xotok, half)])
                if final:
                    fs, fstok = fss.next()
                    P.op("act", lambda e, xo=xo, fs=fs: e.activation(out=R["junk"][:], in_=xo[:], func=AF.Square, accum_out=fs[:, 0:1]),
                         reads=[(xotok, 0), (xotok, 1)], writes=["junk", (fstok, 0)])
                    P.op("act", lambda e, fs=fs: e.activation(out=fs[:, 1:2], in_=fs[:, 0:1], func=AF.Sqrt, scale=1.0 / D, bias=C.eps[:, 0:1]),
                         reads=[(fstok, 0), "eps"], writes=[(fstok, 1)])
                    P.op("dve", lambda e, fs=fs: e.reciprocal(out=fs[:, 2:3], in_=fs[:, 1:2]), reads=[(fstok, 1)], writes=[(fstok, 2)])
                    P.op("dve", lambda e, xo=xo, fs=fs: e.scalar_tensor_tensor(
                        out=xo[:], in0=xo[:], scalar=fs[:, 2:3], in1=gfin[:], op0=ALU.mult, op1=ALU.mult),
                        reads=[(xotok, 0), (xotok, 1), (fstok, 2), "gfin"], writes=[(xotok, 0), (xotok, 1)])
                P.dma("pool", dst_d[T * 128:(T + 1) * 128, :], xo[:], reads=[(xotok, 0), (xotok, 1)], writes=[("dst", T)])

        emit_rms(0)
        for bk in range(NB):
            emit_gu(bk)
            if bk + 1 < NB:
                emit_rms(bk + 1)
            emit_down(bk)
        P.barrier()


def phase4(C):
    nc, P, sb, ps = C.nc, C.P, C.sb, C.ps
    with ExitStack() as st:
        Fc = sb("Fc", [128, 2, 512], BF16, st=st)
        P.dma("pool", Fc[:], C.Fc_d, writes=["Fc"])
        R = {
            "xt": Ring(st, nc, "p4xt", 3, [128, D], F32),
            "ss": Ring(st, nc, "p4ss", 3, [128, 4], F32),
            "xn": Ring(st, nc, "p4xn", 2, [128, D], BF16),
            "tp": Ring(st, nc, "p4tp", 2, [128, D], BF16, psum=True),
            "junk": sb("p4junk", [128, D], BF16, st=st),
        }
        hTr = Ring(st, nc, "p4hT", 2, [128, 8, 128], BF16)
        Zr = Ring(st, nc, "Z_ps", 4, [128, 512], F32, psum=True)
        zsr = Ring(st, nc, "zs", 3, [128, 2, D], BF16)
        for T in range(NT):
            hT, hTtok = hTr.next()
            rms_stage(C, R, C.x2_d[T * 128:(T + 1) * 128, :], 3, hT, hTtok)
            zs, zstok = zsr.next()
            for g in range(4):
                Z, Ztok = Zr.next()
                for cc in range(2):
                    P.op("pe", lambda e, Z=Z, g=g, cc=cc: e.matmul(Z[:], lhsT=hT[:, 2 * g + cc, :], rhs=Fc[:, cc, :],
                                                                    start=(cc == 0), stop=(cc == 1)),
                         reads=[hTtok, "Fc"], writes=[Ztok])
                eng = "act" if g % 2 == 0 else "dve"
                if eng == "act":
                    P.op("act", lambda e, Z=Z, g=g, zs=zs: e.activation(
                        out=zs[:, :, g * 256:(g + 1) * 256], in_=Z[:].rearrange("p (r c) -> p r c", r=2), func=AF.Identity),
                        reads=[Ztok], writes=[(zstok, g)])
                else:
                    P.op("dve", lambda e, Z=Z, g=g, zs=zs: e.tensor_copy(
                        out=zs[:, :, g * 256:(g + 1) * 256], in_=Z[:].rearrange("p (r c) -> p r c", r=2)),
                        reads=[Ztok], writes=[(zstok, g)])
            P.dma("pool", C.Z_d[:, T * 128:(T + 1) * 128, :].rearrange("r t c -> t r c"), zs[:],
                  reads=[(zstok, g) for g in range(4)], writes=[("Z_d", T)])
        P.barrier()


def phase5(C):
    nc, P, sb, ps = C.nc, C.P, C.sb, C.ps
    NB2 = 8
    with ExitStack() as st:
        M1 = sb("M1", [128, 128], BF16, st=st)
        P.dma("pool", M1[:], C.M1_d, writes=["M1"])
        ztr = Ring(st, nc, "zt", 2, [128, NB2, D], BF16)
        t1r = Ring(st, nc, "t1s", 2, [128, NB2, D], BF16)
        Tr = Ring(st, nc, "T1_ps", 4, [128, 512], F32, psum=True)
        zv = C.Z_d.rearrange("r (n1 n2) c -> (r n1) n2 c", n2=64)
        k = 0
        for blk in range(64 // NB2):
            zt, zttok = ztr.next()
            P.dma("sp", zt[:], zv[:, blk * NB2:(blk + 1) * NB2, :], writes=[zttok])
            t1s, t1tok = t1r.next()
            for i in range(NB2):
                for half in range(2):
                    Tp, Ttok = Tr.next()
                    sl = slice(half * 512, (half + 1) * 512)
                    P.op("pe", lambda e, Tp=Tp, i=i, sl=sl: e.matmul(Tp[:], lhsT=M1[:], rhs=zt[:, i, sl], start=True, stop=True),
                         reads=["M1", zttok], writes=[Ttok])
                    if k % 2 == 0:
                        P.op("act", lambda e, Tp=Tp, i=i, sl=sl: e.activation(out=t1s[:, i, sl], in_=Tp[:], func=AF.Identity),
                             reads=[Ttok], writes=[(t1tok, i, half)])
                    else:
                        P.op("dve", lambda e, Tp=Tp, i=i, sl=sl: e.tensor_copy(out=t1s[:, i, sl], in_=Tp[:]),
                             reads=[Ttok], writes=[(t1tok, i, half)])
                    k += 1
            rd = [(t1tok, i, h) for i in range(NB2) for h in range(2)]
            for r in range(2):
                P.dma("pool", C.T1_d[r, blk * NB2:(blk + 1) * NB2, :, :].rearrange("n2 k1 c -> k1 n2 c"),
                      t1s[r * 64:(r + 1) * 64, :, :], reads=rd, writes=[("T1_d", blk, r)])
        P.barrier()


def phase6(C):
    nc, P, sb, ps = C.nc, C.P, C.sb, C.ps
    NB1 = 8
    with ExitStack() as st:
        M2 = sb("M2", [128, 64, 64], BF16, st=st)
        P.dma("pool", M2[:], C.M2_d, writes=["M2"])
        ttr = Ring(st, nc, "tt", 2, [128, NB1, D], BF16)
        fsr = Ring(st, nc, "fs", 2, [64, NB1, D], BF16)
        Fr = Ring(st, nc, "f_ps", 4, [64, 512], F32, psum=True)
        tv = C.T1_d.rearrange("r n2 k1 c -> (r n2) k1 c")
        fv = C.f_d.rearrange("(k2 k1) c -> k2 k1 c", k1=64)
        k = 0
        for blk in range(64 // NB1):
            tt, tttok = ttr.next()
            P.dma("sp", tt[:], tv[:, blk * NB1:(blk + 1) * NB1, :], writes=[tttok])
            fs, fstok = fsr.next()
            for i in range(NB1):
                k1 = blk * NB1 + i
                for half in range(2):
                    Fp, Ftok = Fr.next()
                    sl = slice(half * 512, (half + 1) * 512)
                    P.op("pe", lambda e, Fp=Fp, i=i, sl=sl, k1=k1: e.matmul(Fp[:], lhsT=M2[:, k1, :], rhs=tt[:, i, sl],
                                                                        start=True, stop=True),
                         reads=["M2", tttok], writes=[Ftok])
                    if k % 2 == 0:
                        P.op("act", lambda e, Fp=Fp, i=i, sl=sl: e.activation(out=fs[:, i, sl], in_=Fp[:], func=AF.Identity),
                             reads=[Ftok], writes=[(fstok, i, half)])
                    else:
                        P.op("dve", lambda e, Fp=Fp, i=i, sl=sl: e.tensor_copy(out=fs[:, i, sl], in_=Fp[:]),
                             reads=[Ftok], writes=[(fstok, i, half)])
                    k += 1
            P.dma("pool", fv[:, blk * NB1:(blk + 1) * NB1, :], fs[:],
                  reads=[(fstok, i, h) for i in range(NB1) for h in range(2)], writes=[("f_d", blk)])
        P.barrier()
    with ExitStack() as st:
        wf = sb("wf", [128, 8, D], BF16, st=st)
        wfv = C.wf_d.rearrange("(kc p) f -> p kc f", p=128)
        for kc in range(8):
            P.dma("pool", wf[:, kc, :], wfv[:, kc, :], writes=[("wf", kc)])
        gate = sb("gate_p6", [128, D], st=st)
        bfr = sb("bf_row", [128, D], st=st)
        P.dma("sp", gate[:], C.gates_d[2].partition_broadcast(128), writes=["gate"])
        P.dma("sp", bfr[:], C.bf_d.partition_broadcast(128), writes=["bfr"])
        ftr = Ring(st, nc, "ft", 3, [128, D], BF16)
        tpr = Ring(st, nc, "p6tp", 2, [128, D], BF16, psum=True)
        fTr = Ring(st, nc, "fT", 2, [128, 8, 128], BF16)
        xtr = Ring(st, nc, "p6xt", 3, [128, D], F32)
        xor_ = Ring(st, nc, "p6xo", 2, [128, D], F32)
        Wr = Ring(st, nc, "wf_ps", 4, [128, 512], F32, psum=True)
        for T in range(NT):
            ft, fttok = ftr.next()
            P.dma("sp", ft[:], C.f_d[T * 128:(T + 1) * 128, :], writes=[fttok])
            xt, xtok = xtr.next()
            P.dma("sp", xt[:], C.x2_d[T * 128:(T + 1) * 128, :], writes=[xtok])
            tp, tptok = tpr.next()
            for kc in range(8):
                P.op("pe", lambda e, kc=kc: e.transpose(out=tp[:, kc * 128:(kc + 1) * 128], in_=ft[:, kc * 128:(kc + 1) * 128],
                                                         identity=C.ident[:]),
                     reads=[fttok, "ident"], writes=[tptok])
            fT, fTtok = fTr.next()
            P.op("act", lambda e: e.activation(out=fT[:].rearrange("p k t -> p (k t)"), in_=tp[:], func=AF.Identity),
                 reads=[tptok], writes=[fTtok])
            xo, xotok = xor_.next()
            for half in range(2):
                Wp, Wtok = Wr.next()
                sl = slice(half * 512, (half + 1) * 512)
                for kc in range(8):
                    P.op("pe", lambda e, Wp=Wp, kc=kc, sl=sl: e.matmul(Wp[:], lhsT=fT[:, kc, :], rhs=wf[:, kc, sl],
                                                                       start=(kc == 0), stop=(kc == 7)),
                         reads=[fTtok, ("wf", kc)], writes=[Wtok])
                P.op("dve", lambda e, Wp=Wp, sl=sl: e.tensor_tensor(out=xo[:, sl], in0=Wp[:], in1=bfr[:, sl], op=ALU.add),
                     reads=[Wtok, "bfr"], writes=[(xotok, half)])
                P.op("pool", lambda e, sl=sl: e.tensor_tensor(out=xo[:, sl], in0=xo[:, sl], in1=gate[:, sl], op=ALU.mult),
                     reads=[(xotok, half), "gate"], writes=[(xotok, half)])
                P.op("pool", lambda e, sl=sl: e.tensor_tensor(out=xo[:, sl], in0=xo[:, sl], in1=xt[:, sl], op=ALU.add),
                     reads=[(xotok, half), xtok], writes=[(xotok, half)])
            P.dma("pool", C.x3_d[T * 128:(T + 1) * 128, :], xo[:], reads=[(xotok, 0), (xotok, 1)], writes=[("x3_d", T)])
        P.barrier()


def _dft_tables():
    c = np.arange(256, dtype=np.float64)
    ang = 2 * np.pi * np.outer(c, c) / 256.0
    Cc, Sc = np.cos(ang) / 16.0, np.sin(ang) / 16.0
    fc = np.concatenate([Cc, -Sc], axis=1)
    fc = fc.reshape(2, 128, 512).transpose(1, 0, 2)
    n = np.arange(64, dtype=np.float64)
    a1 = 2 * np.pi * np.outer(n, n) / 64.0
    C1, S1 = np.cos(a1) / 8.0, np.sin(a1) / 8.0
    m1 = np.block([[C1, -S1], [S1, C1]])
    n2 = n[:, None, None]; k1 = n[None, :, None]; k2 = n[None, None, :]
    th = 2 * np.pi * (n2 * k2 / 64.0 + n2 * k1 / 4096.0)
    m2 = np.concatenate([np.cos(th), np.sin(th)], axis=0) / 8.0
    f32 = lambda a: np.ascontiguousarray(a.astype(np.float32))
    return f32(fc), f32(m1), f32(m2)


def _rope_tables():
    rows = SEQ // 64
    row = np.repeat(np.arange(rows), 64).astype(np.float32)
    col = np.tile(np.arange(64), rows).astype(np.float32)
    inv_freq = (np.float32(10000.0) ** (-np.arange(32, dtype=np.float32) / np.float32(32))).astype(np.float32)
    ang = np.concatenate([row[:, None] * inv_freq, col[:, None] * inv_freq], axis=-1).astype(np.float32)
    return np.cos(ang).astype(np.float32), np.sin(ang).astype(np.float32)


def _host_inputs(inputs):
    f = lambda a: np.ascontiguousarray(np.asarray(a, dtype=np.float32))
    x = f(inputs["x"]); c = f(inputs["c"]); ctx = f(inputs["ctx"]); c_ctx = f(inputs["c_ctx"])
    w_mod = f(inputs["w_mod"]); b_mod = f(inputs["b_mod"])
    pp = lambda v: np.ascontiguousarray(v.reshape(-1, 128).T)
    b_mod_pp = np.stack([pp(b_mod[l]) for l in range(2)])
    g_mix_pp = np.stack([pp(f(inputs["g_mix"])[l]) for l in range(2)])
    g_ffn_pp = np.stack([pp(f(inputs["g_ffn"])[l]) for l in range(2)])
    cos, sin = _rope_tables()
    shared = {
        "w_mod": w_mod, "b_mod_pp": b_mod_pp, "b_mod": b_mod, "g_mix_pp": g_mix_pp, "g_ffn_pp": g_ffn_pp,
        "w_qkv": f(inputs["w_qkv"])[0], "g_q": f(inputs["g_q"])[0], "g_k": f(inputs["g_k"])[0],
        "w_o": f(inputs["w_attn_out"])[0], "rope_cos": cos, "rope_sin": sin,
        "ident": np.eye(128, dtype=np.float32),
        "w_gate_up": f(inputs["w_gate_up"]), "w_down": f(inputs["w_down"]),
        "w_fourier": f(inputs["w_fourier"])[0], "b_fourier": f(inputs["b_fourier"])[0],
        "g_final": f(inputs["g_final"]),
    }
    shared["dft_fc"], shared["dft_m1"], shared["dft_m2"] = _dft_tables()
    maps = []
    for b in range(NCORES):
        c_pp = np.ascontiguousarray(np.stack([pp(c[b]), pp(c_ctx)], axis=-1))
        m = {"x": x[b], "ctx": ctx[b], "c_pp": c_pp}
        m.update(shared)
        maps.append(m)
    return maps


def kernel(**inputs):
    nc = build_program()
    maps = _host_inputs(inputs)
    res = run_bass_kernel_spmd(nc, maps, core_ids=list(range(NCORES)))
    return np.stack([np.asarray(r["out"], dtype=np.float32) for r in res.results], axis=0)
```

```python
import math
from contextlib import ExitStack

import numpy as np
import concourse.bass as bass
import concourse.mybir as mybir
from concourse.bass_utils import run_bass_kernel_spmd

F32 = mybir.dt.float32
BF16 = mybir.dt.bfloat16
AF = mybir.ActivationFunctionType
ALU = mybir.AluOpType
AX = mybir.AxisListType

D = 1024
SEQ = 4096
CTX = 256
NH = 8
NKV = 2
HD = 128
DFF = 2816
NFC = DFF // 128
NT = SEQ // 128
NTC = CTX // 128
NKT = NT + NTC
EPS = 1e-6
NCORES = 8


class Prog:
    def __init__(self, nc, stack):
        self.nc = nc
        self.eng = {"pe": nc.tensor, "act": nc.scalar, "dve": nc.vector, "pool": nc.gpsimd, "sp": nc.sync}
        self.semh = {}
        self.cnt = {}
        for e in self.eng:
            self.semh[e] = stack.enter_context(nc.semaphore("sem_" + e))
            self.cnt[e] = 0
        self.dq = {}
        for q, n in (("sp", 12), ("pool", 8), ("act", 4)):
            keys = []
            for i in range(n):
                k = "dma_%s_%d" % (q, i)
                self.semh[k] = stack.enter_context(nc.semaphore(k))
                self.cnt[k] = 0
                keys.append(k)
            self.dq[q] = {"keys": keys, "i": 0}
        self.known = {e: {} for e in self.eng}
        self.tok = {}
        self.ninst = 0

    def _need(self, e, reads, writes):
        need = {}

        def add(ev, same_ok):
            if ev is None:
                return
            sk, v = ev
            if sk == e and same_ok:
                return
            if need.get(sk, 0) < v:
                need[sk] = v

        for t in reads:
            st = self.tok.get(t)
            if st is not None:
                add(st["w"], False)
        for t in writes:
            st = self.tok.get(t)
            if st is not None:
                add(st["w"], True)
                for sk, v in st["r"].items():
                    add((sk, v), True)
        return need

    def _wait(self, e, need):
        eng = self.eng[e]
        kn = self.known[e]
        for sk, v in need.items():
            if kn.get(sk, 0) < v:
                eng.wait_ge(self.semh[sk], v)
                kn[sk] = v
                self.ninst += 1

    def _record(self, ev, reads, writes):
        for t in reads:
            st = self.tok.setdefault(t, {"w": None, "r": {}})
            if st["r"].get(ev[0], 0) < ev[1]:
                st["r"][ev[0]] = ev[1]
        for t in writes:
            self.tok[t] = {"w": ev, "r": {}}

    def op(self, e, fn, reads=(), writes=()):
        self._wait(e, self._need(e, reads, writes))
        inst = fn(self.eng[e])
        self.cnt[e] += 1
        inst.then_inc(self.semh[e], 1)
        self.ninst += 1
        self._record((e, self.cnt[e]), reads, writes)

    def dma(self, q, out, in_, reads=(), writes=(), **kw):
        dq = self.dq[q]
        k = dq["keys"][dq["i"] % len(dq["keys"])]
        dq["i"] += 1
        need = self._need(q, reads, writes)
        if self.cnt[k] > 0 and need.get(k, 0) < self.cnt[k]:
            need[k] = self.cnt[k]
        self._wait(q, need)
        inst = self.eng[q].dma_start(out=out, in_=in_, **kw)
        self.cnt[k] += 16
        inst.then_inc(self.semh[k], 16)
        self.ninst += 1
        self._record((k, self.cnt[k]), reads, writes)

    def barrier(self):
        for e in self.eng:
            need = {sk: v for sk, v in self.cnt.items() if v > 0}
            self._wait(e, need)
        self.tok = {}


class Ring:
    uid = 0

    def __init__(self, stack, nc, name, n, shape, dtype, psum=False):
        self.name = name
        self.n = n
        self.i = -1
        alloc = nc.psum_tensor if psum else nc.sbuf_tensor
        Ring.uid += 1
        self.tiles = [stack.enter_context(alloc("r%d_%s%d" % (Ring.uid, name, i), shape, dtype)) for i in range(n)]

    def next(self):
        self.i += 1
        s = self.i % self.n
        return self.tiles[s], (self.name, s)


class NS:
    pass


SCALE = float(HD) ** -0.5


def build_program(stop_after=None):
    nc = bass.Bass("TRN2", target_bir_lowering=False)
    C = NS()
    C.nc = nc
    C.stop_after = stop_after
    din = lambda name, shape, dt=F32: nc.dram_tensor(name, shape, dt, kind="ExternalInput").ap()
    dscr = lambda name, shape, dt=F32: nc.dram_tensor(name, shape, dt, kind="Internal").ap()
    C.x_d = din("x", [SEQ, D])
    C.ctx_d = din("ctx", [CTX, D])
    C.cpp_d = din("c_pp", [128, 8, 2])
    C.wmod_d = din("w_mod", [2, D, 6 * D])
    C.bmodpp_d = din("b_mod_pp", [2, 128, 48])
    C.bmod_d = din("b_mod", [2, 6 * D])
    C.gmixpp_d = din("g_mix_pp", [2, 128, 8])
    C.gffnpp_d = din("g_ffn_pp", [2, 128, 8])
    C.wqkv_d = din("w_qkv", [D, 1536])
    C.gq_d = din("g_q", [128])
    C.gk_d = din("g_k", [128])
    C.wo_d = din("w_o", [D, D])
    C.cos_d = din("rope_cos", [SEQ, 64])
    C.sin_d = din("rope_sin", [SEQ, 64])
    C.ident_d = din("ident", [128, 128])
    C.out_d = nc.dram_tensor("out", [SEQ, D], F32, kind="ExternalOutput").ap()
    C.QT_d = dscr("QT_scr", [NT, 128, 1024], BF16)
    C.gates_d = dscr("gates_scr", [4, 1024])
    C.wgu_d = din("w_gate_up", [2, D, 2 * DFF])
    C.wd_d = din("w_down", [2, DFF, D])
    C.wf_d = din("w_fourier", [D, D])
    C.bf_d = din("b_fourier", [D])
    C.gfin_d = din("g_final", [D])
    C.Fc_d = din("dft_fc", [128, 2, 512])
    C.M1_d = din("dft_m1", [128, 128])
    C.M2_d = din("dft_m2", [128, 64, 64])
    C.Z_d = dscr("Z_scr", [2, SEQ, D], BF16)
    C.T1_d = dscr("T1_scr", [2, 64, 64, D], BF16)
    C.f_d = dscr("f_scr", [SEQ, D], BF16)
    names = ["x1", "x2", "x3"]
    for i, nm in enumerate(names):
        setattr(C, nm + "_d", C.out_d if stop_after == "p%d" % (i + 2) and False else dscr(nm + "_scr", [SEQ, D]))
    if stop_after == "p2":
        C.x1_d = C.out_d
    if stop_after == "p3":
        C.x2_d = C.out_d
    if stop_after == "p6":
        C.x3_d = C.out_d

    with ExitStack() as gs:
        P = Prog(nc, gs)
        C.P = P
        C.gs = gs
        uid = [0]

        def _alloc(fn, pre, name, shape, dt, st):
            uid[0] += 1
            return st.enter_context(fn("%s%d_%s" % (pre, uid[0], name), shape, dt))

        C.sb = lambda name, shape, dt=F32, st=gs: _alloc(nc.sbuf_tensor, "sb", name, shape, dt, st)
        C.ps = lambda name, shape, dt=F32, st=gs: _alloc(nc.psum_tensor, "ps", name, shape, dt, st)
        sb = C.sb
        C.modpp = sb("modpp", [128, 2, 4, 8, 2])
        C.gmix = sb("gmix", [128, 2, 8])
        C.gffn = sb("gffn", [128, 2, 8])
        C.Amod = sb("Amod", [128, 5, 8])
        C.Bmod = sb("Bmod", [128, 5, 8])
        C.ident_f = sb("ident_f", [128, 128])
        C.ident = sb("ident", [128, 128], BF16)
        C.ones = sb("ones", [128, 128], BF16)
        P.dma("sp", C.ident_f[:], C.ident_d, writes=["ident_f"])
        P.op("dve", lambda e: e.tensor_copy(out=C.ident[:], in_=C.ident_f[:]), reads=["ident_f"], writes=["ident"])
        P.op("dve", lambda e: e.memset(C.ones[:], 1.0), writes=["ones"])
        C.eps = sb("eps", [128, 1])
        P.op("dve", lambda e: e.memset(C.eps[:], EPS), writes=["eps"])

        phase0(C)
        if stop_after == "p0":
            return nc
        with ExitStack() as st12:
            C.KT = sb("KT", [128, NKV, NKT * 128], BF16, st=st12)
            C.Vs = sb("Vs", [128, NKT, NKV * HD], BF16, st=st12)
            phase1(C)
            phase2(C)
        if stop_after == "p2":
            return nc
        ffn_phase(C, 0, C.x1_d, C.x2_d)
        if stop_after == "p3":
            return nc
        phase4(C)
        phase5(C)
        phase6(C)
        if stop_after == "p6":
            return nc
        ffn_phase(C, 1, C.x3_d, C.out_d, final=True)
        print("instructions:", P.ninst)
    return nc


def phase0(C):
    nc, P, sb, ps = C.nc, C.P, C.sb, C.ps
    modpp, gmix, gffn, Amod, Bmod = C.modpp, C.gmix, C.gffn, C.Amod, C.Bmod
    with ExitStack() as st:
        gates = sb("gates", [128, 4, 1024], st=st)
        cpp = sb("cpp", [128, 8, 2], st=st)
        sc = sb("sc", [128, 8, 2], st=st)
        sig = sb("sig", [128, 8, 2], st=st)
        scb = sb("scb", [128, 8, 128], st=st)
        bpp = sb("bpp", [128, 2, 48], st=st)
        brow = sb("brow", [128, 4, 1024], st=st)
        wring = Ring(st, nc, "wm", 2, [128, 8, 1024], F32)
        pp_ps = ps("pp_ps", [128, 512], st=st)
        row_ps = Ring(st, nc, "row_ps", 2, [128, 512], F32, psum=True)

        P.dma("sp", cpp[:], C.cpp_d, writes=["cpp"])
        P.dma("sp", bpp[:], C.bmodpp_d.rearrange("l p f -> p l f"), writes=["bpp"])
        P.dma("sp", gmix[:], C.gmixpp_d.rearrange("l p f -> p l f"), writes=["gmix"])
        P.dma("sp", gffn[:], C.gffnpp_d.rearrange("l p f -> p l f"), writes=["gffn"])
        for l in range(2):
            for gi, m in enumerate((2, 5)):
                P.dma("sp", brow[:, l * 2 + gi, :],
                      C.bmod_d[l, m * 1024:(m + 1) * 1024].partition_broadcast(128),
                      writes=[("brow", l * 2 + gi)])
        P.op("act", lambda e: e.activation(out=sig[:], in_=cpp[:], func=AF.Sigmoid), reads=["cpp"], writes=["sig"])
        P.op("dve", lambda e: e.tensor_tensor(out=sc[:], in0=cpp[:], in1=sig[:], op=ALU.mult),
             reads=["cpp", "sig"], writes=["sc"])
        P.op("dve", lambda e: e.tensor_copy(out=scb[:], in_=sc[:, :, 0:1].to_broadcast([128, 8, 128])),
             reads=["sc"], writes=["scb"])
        wv = C.wmod_d.rearrange("l (kc p) f -> l p kc f", p=128)
        for l in range(2):
            for m in range(6):
                wt, wtok = wring.next()
                P.dma("sp", wt[:], wv[l, :, :, m * 1024:(m + 1) * 1024], writes=[wtok])
                if m in (2, 5):
                    gi = l * 2 + (0 if m == 2 else 1)
                    for h in range(2):
                        rp, rtok = row_ps.next()
                        for kc in range(8):
                            P.op("pe", lambda e, kc=kc, rp=rp, wt=wt, h=h: e.matmul(
                                rp[:], lhsT=scb[:, kc, :], rhs=wt[:, kc, h * 512:(h + 1) * 512],
                                start=(kc == 0), stop=(kc == 7)),
                                reads=["scb", wtok], writes=[rtok])
                        P.op("dve", lambda e, rp=rp, gi=gi, h=h: e.tensor_tensor(
                            out=gates[:, gi, h * 512:(h + 1) * 512], in0=rp[:],
                            in1=brow[:, gi, h * 512:(h + 1) * 512], op=ALU.add),
                            reads=[rtok, ("brow", gi)], writes=[("gates", gi, h)])
                        if h == 1:
                            P.dma("sp", C.gates_d[gi:gi + 1, :], gates[0:1, gi, :],
                                  reads=[("gates", gi, 0), ("gates", gi, 1)], writes=[("gates_d", gi)])
                else:
                    mi = {0: 0, 1: 1, 3: 2, 4: 3}[m]
                    for fc in range(8):
                        o = ((l * 4 + mi) * 8 + fc) * 2
                        for kc in range(8):
                            P.op("pe", lambda e, kc=kc, fc=fc, wt=wt, o=o: e.matmul(
                                pp_ps[:, o:o + 2], lhsT=wt[:, kc, fc * 128:(fc + 1) * 128], rhs=sc[:, kc, :],
                                start=(kc == 0), stop=(kc == 7)),
                                reads=["sc", wtok], writes=["pp_ps"])
                    P.op("dve", lambda e, l=l, mi=mi, m=m: e.tensor_tensor(
                        out=modpp[:, l, mi, :, :],
                        in0=pp_ps[:, (l * 4 + mi) * 16:(l * 4 + mi + 1) * 16].rearrange("p (f t) -> p f t", t=2),
                        in1=bpp[:, l, m * 8:(m + 1) * 8].unsqueeze(2).to_broadcast([128, 8, 2]), op=ALU.add),
                        reads=["pp_ps", "bpp"], writes=[("modpp", l, mi)])
        combos = [(0, 0, 1, 0, gmix, 0), (1, 0, 1, 0, gmix, 1), (2, 0, 3, 2, gffn, 0),
                  (3, 1, 1, 0, gmix, 0), (4, 1, 3, 2, gffn, 0)]
        for idx, l, m_sc, m_sh, g, col in combos:
            P.op("dve", lambda e, idx=idx, l=l, m_sc=m_sc, g=g, col=col: e.scalar_tensor_tensor(
                out=Amod[:, idx, :], in0=modpp[:, l, m_sc, :, col], scalar=1.0, in1=g[:, l, :],
                op0=ALU.add, op1=ALU.mult),
                reads=[("modpp", l, m_sc), "gmix", "gffn"], writes=[("Amod", idx)])
            P.op("dve", lambda e, idx=idx, l=l, m_sh=m_sh, col=col: e.tensor_copy(
                out=Bmod[:, idx, :], in_=modpp[:, l, m_sh, :, col]),
                reads=[("modpp", l, m_sh)], writes=[("Bmod", idx)])
        P.barrier()


def rms_stage(C, R, src_ap, a_idx, hT, hTtok, col0=0):
    P = C.P
    xt, xtok = R["xt"].next()
    P.dma("sp", xt[:], src_ap, writes=[xtok])
    rms_from_tile(C, R, xt, xtok, a_idx, hT, hTtok, col0)
    return xt, xtok


def rms_from_tile(C, R, xt, xtok, a_idx, hT, hTtok, col0=0):
    P = C.P
    ss, sstok = R["ss"].next()
    xn, xntok = R["xn"].next()
    tp, tptok = R["tp"].next()
    junk = R["junk"]
    P.op("act", lambda e: e.activation(out=junk[:], in_=xt[:], func=AF.Square, accum_out=ss[:, 0:1]),
         reads=[xtok], writes=["junk", (sstok, 0)])
    P.op("act", lambda e: e.activation(out=ss[:, 1:2], in_=ss[:, 0:1], func=AF.Sqrt, scale=1.0 / D, bias=C.eps[:, 0:1]),
         reads=[(sstok, 0), "eps"], writes=[(sstok, 1)])
    P.op("dve", lambda e: e.reciprocal(out=ss[:, 2:3], in_=ss[:, 1:2]), reads=[(sstok, 1)], writes=[(sstok, 2)])
    P.op("act", lambda e: e.activation(out=xn[:], in_=xt[:], func=AF.Identity, scale=ss[:, 2:3]),
         reads=[xtok, (sstok, 2)], writes=[xntok])
    for kc in range(8):
        P.op("pe", lambda e, kc=kc: e.transpose(out=tp[:, kc * 128:(kc + 1) * 128], in_=xn[:, kc * 128:(kc + 1) * 128],
                                                 identity=C.ident[:]),
             reads=[xntok, "ident"], writes=[tptok])
    for kc in range(8):
        P.op("dve", lambda e, kc=kc: e.tensor_scalar(
            out=hT[:, kc, col0:col0 + 128], in0=tp[:, kc * 128:(kc + 1) * 128],
            scalar1=C.Amod[:, a_idx, kc:kc + 1], scalar2=C.Bmod[:, a_idx, kc:kc + 1], op0=ALU.mult, op1=ALU.add),
            reads=[tptok, ("Amod", a_idx), ("Bmod", a_idx)], writes=[hTtok])


def load_weight_bf16(C, st, name, dst, dst_tok_fn, src_view, nchunks, chunk_shape, engines=("dve", "pool")):
    P, nc = C.P, C.nc
    ring = Ring(st, nc, name + "_stg", 2, chunk_shape, F32)
    for i in range(nchunks):
        t, tok = ring.next()
        P.dma("sp", t[:], src_view(i), writes=[tok])
        eng = engines[i % len(engines)]
        P.op(eng, lambda e, t=t, i=i: e.tensor_copy(out=dst(i), in_=t[:]), reads=[tok], writes=[dst_tok_fn(i)])


def phase1(C):
    nc, P, sb, ps = C.nc, C.P, C.sb, C.ps
    with ExitStack() as st:
        wqkv = sb("wqkv", [128, 8, 1536], BF16, st=st)
        cosb = sb("cosb", [128, NT, 64], BF16, st=st)
        sinb = sb("sinb", [128, NT, 64], BF16, st=st)
        gqk = sb("gqk", [128, 10, 128], st=st)
        P.dma("pool", cosb[:], C.cos_d.rearrange("(t p) f -> p t f", p=128), writes=["cos"])
        P.dma("pool", sinb[:], C.sin_d.rearrange("(t p) f -> p t f", p=128), writes=["sin"])
        for h in range(10):
            P.dma("sp", gqk[:, h, :], (C.gq_d if h < 8 else C.gk_d).partition_broadcast(128), writes=["gqk"])
        wv = C.wqkv_d.rearrange("(kc p) f -> p kc f", p=128)
        for kc in range(8):
            P.dma("pool", wqkv[:, kc, :], wv[:, kc, :], writes=[("wqkv", kc)])
        R = {
            "xt": Ring(st, nc, "xt", 3, [128, D], F32),
            "ss": Ring(st, nc, "ss", 3, [128, 4], F32),
            "xn": Ring(st, nc, "xn", 2, [128, D], BF16),
            "tp": Ring(st, nc, "tp", 2, [128, D], BF16, psum=True),
            "junk": sb("junk", [128, D], BF16, st=st),
        }
        hTr = Ring(st, nc, "hT", 2, [128, 8, 128], BF16)
        qkv_ps = [ps("qkv_ps%d" % i, [128, 512], st=st) for i in range(3)]
        qkvr = Ring(st, nc, "qkv_sb", 3, [128, 1536], F32)
        sqb = sb("sqb", [128, 1280], BF16, st=st)
        ssq = Ring(st, nc, "ssq", 2, [128, 3, 10], F32)
        qgr = Ring(st, nc, "qg", 2, [128, 10, 128], BF16)
        t1 = sb("t1", [128, 10, 64], BF16, st=st)
        t2 = sb("t2", [128, 10, 64], BF16, st=st)
        t3 = sb("t3", [128, 10, 64], BF16, st=st)
        t4 = sb("t4", [128, 10, 64], BF16, st=st)
        qrr = Ring(st, nc, "qr", 2, [128, 10, 128], BF16)
        qT_ps = ps("qT_ps", [128, 1024], BF16, st=st)
        kT_ps = ps("kT_ps", [128, 1024], BF16, st=st)
        qTr = Ring(st, nc, "qT_sb", 2, [128, 1024], BF16)
        info = {}

        def stage1a(T):
            lat = T < NT
            src = C.x_d[T * 128:(T + 1) * 128, :] if lat else C.ctx_d[(T - NT) * 128:(T - NT + 1) * 128, :]
            hT, hTtok = hTr.next()
            rms_stage(C, R, src, 0 if lat else 1, hT, hTtok)
            info[T] = dict(hT=hT, hTtok=hTtok)

        def stage1b(T):
            lat = T < NT
            hT, hTtok = info[T]["hT"], info[T]["hTtok"]
            banks = (0, 1, 2) if lat else (2,)
            for nb in banks:
                for kc in range(8):
                    P.op("pe", lambda e, nb=nb, kc=kc: e.matmul(
                        qkv_ps[nb][:], lhsT=hT[:, kc, :], rhs=wqkv[:, kc, nb * 512:(nb + 1) * 512],
                        start=(kc == 0), stop=(kc == 7)),
                        reads=[hTtok, ("wqkv", kc)], writes=[("qkv_ps", nb)])
            qs, qstok = qkvr.next()
            for nb in banks:
                P.op("act", lambda e, nb=nb: e.activation(out=qs[:, nb * 512:(nb + 1) * 512], in_=qkv_ps[nb][:],
                                                          func=AF.Identity),
                     reads=[("qkv_ps", nb)], writes=[(qstok, nb)])
            info[T].update(qs=qs, qstok=qstok)

        def stage2(T):
            lat = T < NT
            qs, qstok = info[T]["qs"], info[T]["qstok"]
            h0 = 0 if lat else 8
            nh = 10 - h0
            lo = h0 * 128
            rd = [(qstok, nb) for nb in ((0, 1, 2) if lat else (2,))]
            sq3, sqtok = ssq.next()
            P.op("act", lambda e: e.activation(out=sqb[:, lo:1280], in_=qs[:, lo:1280], func=AF.Square),
                 reads=rd, writes=["sqb"])
            P.op("dve", lambda e: e.tensor_reduce(out=sq3[:, 0, h0:10],
                                                  in_=sqb[:, lo:1280].rearrange("p (h d) -> p h d", d=128),
                                                  axis=AX.X, op=ALU.add),
                 reads=["sqb"], writes=[(sqtok, 0)])
            P.op("act", lambda e: e.activation(out=sq3[:, 1, h0:10], in_=sq3[:, 0, h0:10], func=AF.Sqrt,
                                               scale=1.0 / HD, bias=C.eps[:, 0:1]),
                 reads=[(sqtok, 0), "eps"], writes=[(sqtok, 1)])
            P.op("dve", lambda e: e.reciprocal(out=sq3[:, 2, h0:10], in_=sq3[:, 1, h0:10]),
                 reads=[(sqtok, 1)], writes=[(sqtok, 2)])
            qg, qgtok = qgr.next()
            for h in range(h0, 10):
                P.op("dve", lambda e, h=h: e.scalar_tensor_tensor(
                    out=qg[:, h, :], in0=qs[:, h * 128:(h + 1) * 128], scalar=sq3[:, 2, h:h + 1], in1=gqk[:, h, :],
                    op0=ALU.mult, op1=ALU.mult),
                    reads=rd + [(sqtok, 2), "gqk"], writes=[(qgtok, h)])
            if lat:
                qr, qrtok = qrr.next()
                cb = cosb[:, T, :].unsqueeze(1).to_broadcast([128, 10, 64])
                sbb = sinb[:, T, :].unsqueeze(1).to_broadcast([128, 10, 64])
                x1 = qg[:, :, 0:64]
                x2 = qg[:, :, 64:128]
                qall = [(qgtok, h) for h in range(10)]
                P.op("dve", lambda e: e.tensor_tensor(out=t1[:], in0=x1, in1=cb, op=ALU.mult), reads=qall + ["cos"], writes=["t1"])
                P.op("dve", lambda e: e.tensor_tensor(out=t2[:], in0=x2, in1=sbb, op=ALU.mult), reads=qall + ["sin"], writes=["t2"])
                P.op("dve", lambda e: e.tensor_tensor(out=t3[:], in0=x1, in1=sbb, op=ALU.mult), reads=qall + ["sin"], writes=["t3"])
                P.op("dve", lambda e: e.tensor_tensor(out=t4[:], in0=x2, in1=cb, op=ALU.mult), reads=qall + ["cos"], writes=["t4"])
                P.op("dve", lambda e: e.tensor_tensor(out=qr[:, :, 0:64], in0=t1[:], in1=t2[:], op=ALU.subtract),
                     reads=["t1", "t2"], writes=[(qrtok, 0)])
                P.op("dve", lambda e: e.tensor_tensor(out=qr[:, :, 64:128], in0=t3[:], in1=t4[:], op=ALU.add),
                     reads=["t3", "t4"], writes=[(qrtok, 1)])
                src_t, src_rd = qr, [(qrtok, 0), (qrtok, 1)]
            else:
                src_t, src_rd = qg, [(qgtok, 8), (qgtok, 9)]
            if lat:
                for h in range(8):
                    P.op("pe", lambda e, h=h: e.transpose(out=qT_ps[:, h * 128:(h + 1) * 128], in_=src_t[:, h, :],
                                                           identity=C.ident[:]),
                         reads=src_rd + ["ident"], writes=["qT_ps"])
            for j in range(2):
                P.op("pe", lambda e, j=j: e.transpose(out=kT_ps[:, j * 128:(j + 1) * 128], in_=src_t[:, 8 + j, :],
                                                       identity=C.ident[:]),
                     reads=src_rd + ["ident"], writes=["kT_ps"])
            if lat:
                qT, qTtok = qTr.next()
                P.op("act", lambda e: e.activation(out=qT[:], in_=qT_ps[:], func=AF.Identity), reads=["qT_ps"], writes=[qTtok])
                P.dma("pool", C.QT_d[T], qT[:], reads=[qTtok], writes=[("QT_d", T)])
            P.op("dve", lambda e: e.tensor_copy(
                out=C.KT[:, :, T * 128:(T + 1) * 128], in_=kT_ps[:, 0:256].rearrange("p (j t) -> p j t", t=128)),
                reads=["kT_ps"], writes=[("KT", T)])
            P.op("pool", lambda e: e.tensor_copy(out=C.Vs[:, T, :], in_=qs[:, 1280:1536]), reads=[(qstok, 2)],
                 writes=[("V", T)])
            del info[T]

        for T in range(NKT + 2):
            if T < NKT:
                stage1a(T)
            if 0 <= T - 1 < NKT:
                stage1b(T - 1)
            if 0 <= T - 2 < NKT:
                stage2(T - 2)
        P.barrier()


def phase2(C):
    nc, P, sb, ps = C.nc, C.P, C.sb, C.ps
    with ExitStack() as st:
        wo = sb("wo", [128, 8, 1024], BF16, st=st)
        wv = C.wo_d.rearrange("(h p) f -> p h f", p=128)
        for h in range(8):
            P.dma("pool", wo[:, h, :], wv[:, h, :], writes=[("wo", h)])
        wavg = sb("wavg", [128, 128], st=st)
        P.op("dve", lambda e: e.memset(wavg[:], 1.0 / 32.0), writes=["wavg"])
        qTr = Ring(st, nc, "qT2", 2, [128, 1024], BF16)
        xtr = Ring(st, nc, "xt2", 3, [128, D], F32)
        PTr = Ring(st, nc, "PT", 8, [128, 512], BF16)
        Sr = Ring(st, nc, "S_ps", 2, [128, 512], F32, psum=True)
        accr = Ring(st, nc, "acc_ps", 2, [128, 512], F32, psum=True)
        denr = Ring(st, nc, "den_ps", 2, [128, 512], F32, psum=True)
        wo_ps = [ps("wo_ps%d" % i, [128, 512], st=st) for i in range(2)]
        densr = Ring(st, nc, "den_sb", 2, [128, 512], F32)
        recr = Ring(st, nc, "rec", 2, [128, 512], F32)
        OTr = Ring(st, nc, "OT", 2, [128, 8, 128], BF16)
        x1r = Ring(st, nc, "x1t", 2, [128, D], F32)
        gate = sb("gate_p2", [128, D], st=st)
        P.dma("sp", gate[:], C.gates_d[0].partition_broadcast(128), writes=["gate"])

        steps = [(T, kvh, kt) for T in range(NT) for kvh in range(NKV) for kt in range(NKT)]
        state = {}
        pending = []

        def load_tile(T):
            qT, qTtok = qTr.next()
            xt, xtok = xtr.next()
            P.dma("sp", qT[:], C.QT_d[T], writes=[qTtok])
            P.dma("sp", xt[:], C.x_d[T * 128:(T + 1) * 128, :], writes=[xtok])
            OT, OTtok = OTr.next()
            state[T] = dict(qT=qT, qTtok=qTtok, xt=xt, xtok=xtok, OT=OT, OTtok=OTtok, batch=[])

        def emit_S(i):
            T, kvh, kt = steps[i]
            if kvh == 0 and kt == 0:
                if T not in state:
                    load_tile(T)
                if T + 1 < NT and (T + 1) not in state:
                    load_tile(T + 1)
            s = state[T]
            S, Stok = Sr.next()
            P.op("pe", lambda e: e.matmul(S[:], lhsT=C.KT[:, kvh, kt * 128:(kt + 1) * 128],
                                           rhs=s["qT"][:, kvh * 512:(kvh + 1) * 512], start=True, stop=True),
                 reads=[("KT", kt), s["qTtok"]], writes=[Stok])
            PT, PTtok = PTr.next()
            P.op("act", lambda e: e.activation(out=PT[:], in_=S[:], func=AF.Exp, scale=SCALE), reads=[Stok], writes=[PTtok])
            return PT, PTtok

        def emit_PV(i, PT, PTtok):
            T, kvh, kt = steps[i]
            s = state[T]
            if kt == 0:
                s["acc"], s["acctok"] = accr.next()
                s["den"], s["dentok"] = denr.next()
                s["batch"] = []
            acc, den, dentok, acctok = s["acc"], s["den"], s["dentok"], s["acctok"]
            P.op("pe", lambda e: e.matmul(acc[:], lhsT=C.Vs[:, kt, kvh * 128:(kvh + 1) * 128], rhs=PT[:],
                                           start=(kt == 0), stop=(kt == NKT - 1)),
                 reads=[("V", kt), PTtok], writes=[acctok])
            s["batch"].append((kt, PT, PTtok))
            if kt % 4 == 3 or kt == NKT - 1:
                for (k2, PT2, PTtok2) in s["batch"]:
                    j = k2 % 4
                    P.op("pe", lambda e, j=j, PT2=PT2, k2=k2: e.matmul(
                        den[32 * j:32 * j + 32, :], lhsT=C.ones[:, 0:32], rhs=PT2[:],
                        start=(k2 < 4), stop=(k2 >= NKT - 4), tile_position=(0, 32 * j)),
                        reads=["ones", PTtok2], writes=[dentok])
                s["batch"] = []
            if kt == NKT - 1:
                dens, denstok = densr.next()
                P.op("dve", lambda e: e.tensor_copy(out=dens[:], in_=den[:]), reads=[dentok], writes=[denstok])
                OT, OTtok = s["OT"], s["OTtok"]

                def fin(T=T, kvh=kvh, den=den, dentok=dentok, acc=acc, acctok=acctok, dens=dens, denstok=denstok,
                        OT=OT, OTtok=OTtok):
                    P.op("pe", lambda e: e.matmul(den[:], lhsT=wavg[:], rhs=dens[:], start=True, stop=True),
                         reads=["wavg", denstok], writes=[dentok])
                    rec, rectok = recr.next()
                    P.op("dve", lambda e: e.reciprocal(out=rec[:], in_=den[:]), reads=[dentok], writes=[rectok])
                    P.op("dve", lambda e: e.tensor_tensor(
                        out=OT[:, kvh * 4:(kvh + 1) * 4, :].rearrange("p h q -> p (h q)"), in0=acc[:], in1=rec[:], op=ALU.mult),
                        reads=[acctok, rectok], writes=[(OTtok, kvh)])
                    if kvh == NKV - 1:
                        pending.append([3, lambda T=T: emit_wo(T)])

                pending.append([2, fin])

        def emit_wo(T):
            s = state[T]
            OT, OTtok = s["OT"], s["OTtok"]
            x1t, x1tok = x1r.next()
            for half in range(2):
                for h in range(8):
                    P.op("pe", lambda e, half=half, h=h: e.matmul(
                        wo_ps[half][:], lhsT=OT[:, h, :], rhs=wo[:, h, half * 512:(half + 1) * 512],
                        start=(h == 0), stop=(h == 7)),
                        reads=[(OTtok, h // 4), ("wo", h)], writes=[("wo_ps", half)])
            for half in range(2):
                sl = slice(half * 512, (half + 1) * 512)
                P.op("dve", lambda e, half=half, sl=sl: e.tensor_tensor(
                    out=x1t[:, sl], in0=wo_ps[half][:], in1=gate[:, sl], op=ALU.mult),
                    reads=[("wo_ps", half), "gate"], writes=[(x1tok, half)])
                P.op("pool", lambda e, sl=sl: e.tensor_tensor(out=x1t[:, sl], in0=x1t[:, sl], in1=s["xt"][:, sl], op=ALU.add),
                     reads=[(x1tok, half), s["xtok"]], writes=[(x1tok, half)])
            P.dma("pool", C.x1_d[T * 128:(T + 1) * 128, :], x1t[:], reads=[(x1tok, 0), (x1tok, 1)], writes=[("x1_d", T)])
            del state[T]

        def tick():
            for p in list(pending):
                p[0] -= 1
                if p[0] <= 0:
                    pending.remove(p)
                    p[1]()

        cur = emit_S(0)
        for i in range(len(steps)):
            nxt = emit_S(i + 1) if i + 1 < len(steps) else None
            emit_PV(i, *cur)
            tick()
            cur = nxt
        while pending:
            tick()
        P.barrier()


def ffn_phase(C, layer, src_d, dst_d, final=False):
    nc, P, sb, ps = C.nc, C.P, C.sb, C.ps
    a_idx = 2 if layer == 0 else 4
    gi = layer * 2 + 1
    with ExitStack() as st:
        wgu = sb("wgu", [128, 8, 2 * DFF], BF16, st=st)
        wd = sb("wd", [128, NFC, D], BF16, st=st)
        wguv = C.wgu_d[layer].rearrange("(kc p) f -> p kc f", p=128)
        wdv = C.wd_d[layer].rearrange("(fc p) d -> p fc d", p=128)
        for kc in range(8):
            for hh in range(2):
                P.dma("pool", wgu[:, kc, hh * DFF:(hh + 1) * DFF], wguv[:, kc, hh * DFF:(hh + 1) * DFF],
                      writes=[("wgu", kc, hh)])
        for fc in range(NFC):
            P.dma("pool", wd[:, fc, :], wdv[:, fc, :], writes=[("wd", fc)])
        gate = sb("gate_ffn", [128, D], st=st)
        P.dma("sp", gate[:], C.gates_d[gi].partition_broadcast(128), reads=[("gates_d", gi)], writes=["gate"])
        if final:
            gfin = sb("gfin", [128, D], st=st)
            P.dma("sp", gfin[:], C.gfin_d.partition_broadcast(128), writes=["gfin"])
        R = {
            "xt": Ring(st, nc, "fxt", 2, [128, D], F32),
            "ss": Ring(st, nc, "fss", 3, [128, 4], F32),
            "xn": Ring(st, nc, "fxn", 2, [128, D], BF16),
            "tp": Ring(st, nc, "ftp", 2, [128, D], BF16, psum=True),
            "junk": sb("fjunk", [128, D], BF16, st=st),
        }
        hT2 = sb("hT2", [128, 8, 512], BF16, st=st)
        aT = sb("aT", [128, NFC, 512], BF16, st=st)
        Gr = Ring(st, nc, "G_ps", 2, [128, 512], F32, psum=True)
        Ur = Ring(st, nc, "U_ps", 2, [128, 512], F32, psum=True)
        Dr = Ring(st, nc, "D_ps", 2, [128, 512], F32, psum=True)
        sgr = Ring(st, nc, "sg", 2, [128, 512], F32)
        xrr = Ring(st, nc, "xres", 2, [128, D], F32)
        xor_ = Ring(st, nc, "xo", 2, [128, D], F32)
        fss = Ring(st, nc, "finss", 2, [128, 4], F32)
        print("ffn sbuf remaining", nc.sbuf_bytes_remaining)
        NB = NT // 4

        def emit_rms(bk):
            for j in range(4):
                T = bk * 4 + j
                rms_stage(C, R, src_d[T * 128:(T + 1) * 128, :], a_idx, hT2, ("hT2", j), col0=j * 128)

        def emit_gu(bk):
            for fc in range(NFC):
                G, Gtok = Gr.next()
                U, Utok = Ur.next()
                for (ps_t, ps_tok, c0) in ((G, Gtok, fc * 128), (U, Utok, DFF + fc * 128)):
                    hh = 0 if c0 < DFF else 1
                    for kc in range(8):
                        P.op("pe", lambda e, ps_t=ps_t, c0=c0, kc=kc: e.matmul(
                            ps_t[:], lhsT=wgu[:, kc, c0:c0 + 128], rhs=hT2[:, kc, :], start=(kc == 0), stop=(kc == 7)),
                            reads=[("wgu", kc, hh)] + [("hT2", j) for j in range(4)], writes=[ps_tok])
                sg, sgtok = sgr.next()
                P.op("act", lambda e, sg=sg, G=G: e.activation(out=sg[:], in_=G[:], func=AF.Silu), reads=[Gtok], writes=[sgtok])
                P.op("dve", lambda e, sg=sg, U=U, fc=fc: e.tensor_tensor(out=aT[:, fc, :], in0=U[:], in1=sg[:], op=ALU.mult),
                     reads=[Utok, sgtok], writes=[("aT", fc)])

        def emit_down(bk):
            for j in range(4):
                T = bk * 4 + j
                xr, xrtok = xrr.next()
                P.dma("sp", xr[:], src_d[T * 128:(T + 1) * 128, :], writes=[xrtok])
                xo, xotok = xor_.next()
                for half in range(2):
                    Dp, Dtok = Dr.next()
                    sl = slice(half * 512, (half + 1) * 512)
                    for fc in range(NFC):
                        P.op("pe", lambda e, Dp=Dp, fc=fc, sl=sl, j=j: e.matmul(
                            Dp[:], lhsT=aT[:, fc, j * 128:(j + 1) * 128], rhs=wd[:, fc, sl],
                            start=(fc == 0), stop=(fc == NFC - 1)),
                            reads=[("aT", fc), ("wd", fc)], writes=[Dtok])
                    P.op("dve", lambda e, Dp=Dp, sl=sl, xo=xo: e.tensor_tensor(out=xo[:, sl], in0=Dp[:], in1=gate[:, sl], op=ALU.mult),
                         reads=[Dtok, "gate"], writes=[(xotok, half)])
                    P.op("pool", lambda e, sl=sl, xo=xo, xr=xr: e.tensor_tensor(out=xo[:, sl], in0=xo[:, sl], in1=xr[:, sl], op=ALU.add),
                         reads=[(xotok, half), xrtok], writes=[(xotok, half)])
                if final:
                    fs, fstok = fss.next()
                    P.op("act", lambda e, xo=xo, fs=fs: e.activation(out=R["junk"][:], in_=xo[:], func=AF.Square, accum_out=fs[:, 0:1]),
                         reads=[(xotok, 0), (xotok, 1)], writes=["junk", (fstok, 0)])
                    P.op("act", lambda e, fs=fs: e.activation(out=fs[:, 1:2], in_=fs[:, 0:1], func=AF.Sqrt, scale=1.0 / D, bias=C.eps[:, 0:1]),
                         reads=[(fstok, 0), "eps"], writes=[(fstok, 1)])
                    P.op("dve", lambda e, fs=fs: e.reciprocal(out=fs[:, 2:3], in_=fs[:, 1:2]), reads=[(fstok, 1)], writes=[(fstok, 2)])
                    P.op("dve", lambda e, xo=xo, fs=fs: e.scalar_tensor_tensor(
                        out=xo[:], in0=xo[:], scalar=fs[:, 2:3], in1=gfin[:], op0=ALU.mult, op1=ALU.mult),
                        reads=[(xotok, 0), (xotok, 1), (fstok, 2), "gfin"], writes=[(xotok, 0), (xotok, 1)])
                P.dma("pool", dst_d[T * 128:(T + 1) * 128, :], xo[:], reads=[(xotok, 0), (xotok, 1)], writes=[("dst", T)])

        emit_rms(0)
        for bk in range(NB):
            emit_gu(bk)
            if bk + 1 < NB:
                emit_rms(bk + 1)
            emit_down(bk)
        P.barrier()


def phase4(C):
    nc, P, sb, ps = C.nc, C.P, C.sb, C.ps
    with ExitStack() as st:
        Fc = sb("Fc", [128, 2, 512], BF16, st=st)
        P.dma("pool", Fc[:], C.Fc_d, writes=["Fc"])
        R = {
            "xt": Ring(st, nc, "p4xt", 3, [128, D], F32),
            "ss": Ring(st, nc, "p4ss", 3, [128, 4], F32),
            "xn": Ring(st, nc, "p4xn", 2, [128, D], BF16),
            "tp": Ring(st, nc, "p4tp", 2, [128, D], BF16, psum=True),
            "junk": sb("p4junk", [128, D], BF16, st=st),
        }
        hTr = Ring(st, nc, "p4hT", 2, [128, 8, 128], BF16)
        Zr = Ring(st, nc, "Z_ps", 4, [128, 512], F32, psum=True)
        zsr = Ring(st, nc, "zs", 3, [128, 2, D], BF16)
        for T in range(NT):
            hT, hTtok = hTr.next()
            rms_stage(C, R, C.x2_d[T * 128:(T + 1) * 128, :], 3, hT, hTtok)
            zs, zstok = zsr.next()
            for g in range(4):
                Z, Ztok = Zr.next()
                for cc in range(2):
                    P.op("pe", lambda e, Z=Z, g=g, cc=cc: e.matmul(Z[:], lhsT=hT[:, 2 * g + cc, :], rhs=Fc[:, cc, :],
                                                                    start=(cc == 0), stop=(cc == 1)),
                         reads=[hTtok, "Fc"], writes=[Ztok])
                eng = "act" if g % 2 == 0 else "dve"
                if eng == "act":
                    P.op("act", lambda e, Z=Z, g=g, zs=zs: e.activation(
                        out=zs[:, :, g * 256:(g + 1) * 256], in_=Z[:].rearrange("p (r c) -> p r c", r=2), func=AF.Identity),
                        reads=[Ztok], writes=[(zstok, g)])
                else:
                    P.op("dve", lambda e, Z=Z, g=g, zs=zs: e.tensor_copy(
                        out=zs[:, :, g * 256:(g + 1) * 256], in_=Z[:].rearrange("p (r c) -> p r c", r=2)),
                        reads=[Ztok], writes=[(zstok, g)])
            P.dma("pool", C.Z_d[:, T * 128:(T + 1) * 128, :].rearrange("r t c -> t r c"), zs[:],
                  reads=[(zstok, g) for g in range(4)], writes=[("Z_d", T)])
        P.barrier()


def phase5(C):
    nc, P, sb, ps = C.nc, C.P, C.sb, C.ps
    NB2 = 8
    with ExitStack() as st:
        M1 = sb("M1", [128, 128], BF16, st=st)
        P.dma("pool", M1[:], C.M1_d, writes=["M1"])
        ztr = Ring(st, nc, "zt", 2, [128, NB2, D], BF16)
        t1r = Ring(st, nc, "t1s", 2, [128, NB2, D], BF16)
        Tr = Ring(st, nc, "T1_ps", 4, [128, 512], F32, psum=True)
        zv = C.Z_d.rearrange("r (n1 n2) c -> (r n1) n2 c", n2=64)
        k = 0
        for blk in range(64 // NB2):
            zt, zttok = ztr.next()
            P.dma("sp", zt[:], zv[:, blk * NB2:(blk + 1) * NB2, :], writes=[zttok])
            t1s, t1tok = t1r.next()
            for i in range(NB2):
                for half in range(2):
                    Tp, Ttok = Tr.next()
                    sl = slice(half * 512, (half + 1) * 512)
                    P.op("pe", lambda e, Tp=Tp, i=i, sl=sl: e.matmul(Tp[:], lhsT=M1[:], rhs=zt[:, i, sl], start=True, stop=True),
                         reads=["M1", zttok], writes=[Ttok])
                    if k % 2 == 0:
                        P.op("act", lambda e, Tp=Tp, i=i, sl=sl: e.activation(out=t1s[:, i, sl], in_=Tp[:], func=AF.Identity),
                             reads=[Ttok], writes=[(t1tok, i, half)])
                    else:
                        P.op("dve", lambda e, Tp=Tp, i=i, sl=sl: e.tensor_copy(out=t1s[:, i, sl], in_=Tp[:]),
                             reads=[Ttok], writes=[(t1tok, i, half)])
                    k += 1
            rd = [(t1tok, i, h) for i in range(NB2) for h in range(2)]
            for r in range(2):
                P.dma("pool", C.T1_d[r, blk * NB2:(blk + 1) * NB2, :, :].rearrange("n2 k1 c -> k1 n2 c"),
                      t1s[r * 64:(r + 1) * 64, :, :], reads=rd, writes=[("T1_d", blk, r)])
        P.barrier()


def phase6(C):
    nc, P, sb, ps = C.nc, C.P, C.sb, C.ps
    NB1 = 8
    with ExitStack() as st:
        M2 = sb("M2", [128, 64, 64], BF16, st=st)
        P.dma("pool", M2[:], C.M2_d, writes=["M2"])
        ttr = Ring(st, nc, "tt", 2, [128, NB1, D], BF16)
        fsr = Ring(st, nc, "fs", 2, [64, NB1, D], BF16)
        Fr = Ring(st, nc, "f_ps", 4, [64, 512], F32, psum=True)
        tv = C.T1_d.rearrange("r n2 k1 c -> (r n2) k1 c")
        fv = C.f_d.rearrange("(k2 k1) c -> k2 k1 c", k1=64)
        k = 0
        for blk in range(64 // NB1):
            tt, tttok = ttr.next()
            P.dma("sp", tt[:], tv[:, blk * NB1:(blk + 1) * NB1, :], writes=[tttok])
            fs, fstok = fsr.next()
            for i in range(NB1):
                k1 = blk * NB1 + i
                for half in range(2):
                    Fp, Ftok = Fr.next()
                    sl = slice(half * 512, (half + 1) * 512)
                    P.op("pe", lambda e, Fp=Fp, i=i, sl=sl, k1=k1: e.matmul(Fp[:], lhsT=M2[:, k1, :], rhs=tt[:, i, sl],
                                                                        start=True, stop=True),
                         reads=["M2", tttok], writes=[Ftok])
                    if k % 2 == 0:
                        P.op("act", lambda e, Fp=Fp, i=i, sl=sl: e.activation(out=fs[:, i, sl], in_=Fp[:], func=AF.Identity),
                             reads=[Ftok], writes=[(fstok, i, half)])
                    else:
                        P.op("dve", lambda e, Fp=Fp, i=i, sl=sl: e.tensor_copy(out=fs[:, i, sl], in_=Fp[:]),
                             reads=[Ftok], writes=[(fstok, i, half)])
                    k += 1
            P.dma("pool", fv[:, blk * NB1:(blk + 1) * NB1, :], fs[:],
                  reads=[(fstok, i, h) for i in range(NB1) for h in range(2)], writes=[("f_d", blk)])
        P.barrier()
    with ExitStack() as st:
        wf = sb("wf", [128, 8, D], BF16, st=st)
        wfv = C.wf_d.rearrange("(kc p) f -> p kc f", p=128)
        for kc in range(8):
            P.dma("pool", wf[:, kc, :], wfv[:, kc, :], writes=[("wf", kc)])
        gate = sb("gate_p6", [128, D], st=st)
        bfr = sb("bf_row", [128, D], st=st)
        P.dma("sp", gate[:], C.gates_d[2].partition_broadcast(128), writes=["gate"])
        P.dma("sp", bfr[:], C.bf_d.partition_broadcast(128), writes=["bfr"])
        ftr = Ring(st, nc, "ft", 3, [128, D], BF16)
        tpr = Ring(st, nc, "p6tp", 2, [128, D], BF16, psum=True)
        fTr = Ring(st, nc, "fT", 2, [128, 8, 128], BF16)
        xtr = Ring(st, nc, "p6xt", 3, [128, D], F32)
        xor_ = Ring(st, nc, "p6xo", 2, [128, D], F32)
        Wr = Ring(st, nc, "wf_ps", 4, [128, 512], F32, psum=True)
        for T in range(NT):
            ft, fttok = ftr.next()
            P.dma("sp", ft[:], C.f_d[T * 128:(T + 1) * 128, :], writes=[fttok])
            xt, xtok = xtr.next()
            P.dma("sp", xt[:], C.x2_d[T * 128:(T + 1) * 128, :], writes=[xtok])
            tp, tptok = tpr.next()
            for kc in range(8):
                P.op("pe", lambda e, kc=kc: e.transpose(out=tp[:, kc * 128:(kc + 1) * 128], in_=ft[:, kc * 128:(kc + 1) * 128],
                                                         identity=C.ident[:]),
                     reads=[fttok, "ident"], writes=[tptok])
            fT, fTtok = fTr.next()
            P.op("act", lambda e: e.activation(out=fT[:].rearrange("p k t -> p (k t)"), in_=tp[:], func=AF.Identity),
                 reads=[tptok], writes=[fTtok])
            xo, xotok = xor_.next()
            for half in range(2):
                Wp, Wtok = Wr.next()
                sl = slice(half * 512, (half + 1) * 512)
                for kc in range(8):
                    P.op("pe", lambda e, Wp=Wp, kc=kc, sl=sl: e.matmul(Wp[:], lhsT=fT[:, kc, :], rhs=wf[:, kc, sl],
                                                                       start=(kc == 0), stop=(kc == 7)),
                         reads=[fTtok, ("wf", kc)], writes=[Wtok])
                P.op("dve", lambda e, Wp=Wp, sl=sl: e.tensor_tensor(out=xo[:, sl], in0=Wp[:], in1=bfr[:, sl], op=ALU.add),
                     reads=[Wtok, "bfr"], writes=[(xotok, half)])
                P.op("pool", lambda e, sl=sl: e.tensor_tensor(out=xo[:, sl], in0=xo[:, sl], in1=gate[:, sl], op=ALU.mult),
                     reads=[(xotok, half), "gate"], writes=[(xotok, half)])
                P.op("pool", lambda e, sl=sl: e.tensor_tensor(out=xo[:, sl], in0=xo[:, sl], in1=xt[:, sl], op=ALU.add),
                     reads=[(xotok, half), xtok], writes=[(xotok, half)])
            P.dma("pool", C.x3_d[T * 128:(T + 1) * 128, :], xo[:], reads=[(xotok, 0), (xotok, 1)], writes=[("x3_d", T)])
        P.barrier()


def _dft_tables():
    c = np.arange(256, dtype=np.float64)
    ang = 2 * np.pi * np.outer(c, c) / 256.0
    Cc, Sc = np.cos(ang) / 16.0, np.sin(ang) / 16.0
    fc = np.concatenate([Cc, -Sc], axis=1)
    fc = fc.reshape(2, 128, 512).transpose(1, 0, 2)
    n = np.arange(64, dtype=np.float64)
    a1 = 2 * np.pi * np.outer(n, n) / 64.0
    C1, S1 = np.cos(a1) / 8.0, np.sin(a1) / 8.0
    m1 = np.block([[C1, -S1], [S1, C1]])
    n2 = n[:, None, None]; k1 = n[None, :, None]; k2 = n[None, None, :]
    th = 2 * np.pi * (n2 * k2 / 64.0 + n2 * k1 / 4096.0)
    m2 = np.concatenate([np.cos(th), np.sin(th)], axis=0) / 8.0
    f32 = lambda a: np.ascontiguousarray(a.astype(np.float32))
    return f32(fc), f32(m1), f32(m2)


def _rope_tables():
    rows = SEQ // 64
    row = np.repeat(np.arange(rows), 64).astype(np.float32)
    col = np.tile(np.arange(64), rows).astype(np.float32)
    inv_freq = (np.float32(10000.0) ** (-np.arange(32, dtype=np.float32) / np.float32(32))).astype(np.float32)
    ang = np.concatenate([row[:, None] * inv_freq, col[:, None] * inv_freq], axis=-1).astype(np.float32)
    return np.cos(ang).astype(np.float32), np.sin(ang).astype(np.float32)


def _host_inputs(inputs):
    f = lambda a: np.ascontiguousarray(np.asarray(a, dtype=np.float32))
    x = f(inputs["x"]); c = f(inputs["c"]); ctx = f(inputs["ctx"]); c_ctx = f(inputs["c_ctx"])
    w_mod = f(inputs["w_mod"]); b_mod = f(inputs["b_mod"])
    pp = lambda v: np.ascontiguousarray(v.reshape(-1, 128).T)
    b_mod_pp = np.stack([pp(b_mod[l]) for l in range(2)])
    g_mix_pp = np.stack([pp(f(inputs["g_mix"])[l]) for l in range(2)])
    g_ffn_pp = np.stack([pp(f(inputs["g_ffn"])[l]) for l in range(2)])
    cos, sin = _rope_tables()
    shared = {
        "w_mod": w_mod, "b_mod_pp": b_mod_pp, "b_mod": b_mod, "g_mix_pp": g_mix_pp, "g_ffn_pp": g_ffn_pp,
        "w_qkv": f(inputs["w_qkv"])[0], "g_q": f(inputs["g_q"])[0], "g_k": f(inputs["g_k"])[0],
        "w_o": f(inputs["w_attn_out"])[0], "rope_cos": cos, "rope_sin": sin,
        "ident": np.eye(128, dtype=np.float32),
        "w_gate_up": f(inputs["w_gate_up"]), "w_down": f(inputs["w_down"]),
        "w_fourier": f(inputs["w_fourier"])[0], "b_fourier": f(inputs["b_fourier"])[0],
        "g_final": f(inputs["g_final"]),
    }
    shared["dft_fc"], shared["dft_m1"], shared["dft_m2"] = _dft_tables()
    maps = []
    for b in range(NCORES):
        c_pp = np.ascontiguousarray(np.stack([pp(c[b]), pp(c_ctx)], axis=-1))
        m = {"x": x[b], "ctx": ctx[b], "c_pp": c_pp}
        m.update(shared)
        maps.append(m)
    return maps


def kernel(**inputs):
    nc = build_program()
    maps = _host_inputs(inputs)
    res = run_bass_kernel_spmd(nc, maps, core_ids=list(range(NCORES)))
    return np.stack([np.asarray(r["out"], dtype=np.float32) for r in res.results], axis=0)
```

```python
import math
from contextlib import ExitStack

import numpy as np
import concourse.bass as bass
import concourse.mybir as mybir
from concourse.bass_utils import run_bass_kernel_spmd

F32 = mybir.dt.float32
BF16 = mybir.dt.bfloat16
AF = mybir.ActivationFunctionType
ALU = mybir.AluOpType
AX = mybir.AxisListType

D = 1024
SEQ = 4096
CTX = 256
NH = 8
NKV = 2
HD = 128
DFF = 2816
NFC = DFF // 128
NT = SEQ // 128
NTC = CTX // 128
NKT = NT + NTC
EPS = 1e-6
NCORES = 8


class Prog:
    def __init__(self, nc, stack):
        self.nc = nc
        self.eng = {"pe": nc.tensor, "act": nc.scalar, "dve": nc.vector, "pool": nc.gpsimd, "sp": nc.sync}
        self.semh = {}
        self.cnt = {}
        for e in self.eng:
            self.semh[e] = stack.enter_context(nc.semaphore("sem_" + e))
            self.cnt[e] = 0
        self.dq = {}
        for q, n in (("sp", 12), ("pool", 8), ("act", 4)):
            keys = []
            for i in range(n):
                k = "dma_%s_%d" % (q, i)
                self.semh[k] = stack.enter_context(nc.semaphore(k))
                self.cnt[k] = 0
                keys.append(k)
            self.dq[q] = {"keys": keys, "i": 0}
        self.known = {e: {} for e in self.eng}
        self.tok = {}
        self.ninst = 0

    def _need(self, e, reads, writes):
        need = {}

        def add(ev, same_ok):
            if ev is None:
                return
            sk, v = ev
            if sk == e and same_ok:
                return
            if need.get(sk, 0) < v:
                need[sk] = v

        for t in reads:
            st = self.tok.get(t)
            if st is not None:
                add(st["w"], False)
        for t in writes:
            st = self.tok.get(t)
            if st is not None:
                add(st["w"], True)
                for sk, v in st["r"].items():
                    add((sk, v), True)
        return need

    def _wait(self, e, need):
        eng = self.eng[e]
        kn = self.known[e]
        for sk, v in need.items():
            if kn.get(sk, 0) < v:
                eng.wait_ge(self.semh[sk], v)
                kn[sk] = v
                self.ninst += 1

    def _record(self, ev, reads, writes):
        for t in reads:
            st = self.tok.setdefault(t, {"w": None, "r": {}})
            if st["r"].get(ev[0], 0) < ev[1]:
                st["r"][ev[0]] = ev[1]
        for t in writes:
            self.tok[t] = {"w": ev, "r": {}}

    def op(self, e, fn, reads=(), writes=()):
        self._wait(e, self._need(e, reads, writes))
        inst = fn(self.eng[e])
        self.cnt[e] += 1
        inst.then_inc(self.semh[e], 1)
        self.ninst += 1
        self._record((e, self.cnt[e]), reads, writes)

    def dma(self, q, out, in_, reads=(), writes=(), **kw):
        dq = self.dq[q]
        k = dq["keys"][dq["i"] % len(dq["keys"])]
        dq["i"] += 1
        need = self._need(q, reads, writes)
        if self.cnt[k] > 0 and need.get(k, 0) < self.cnt[k]:
            need[k] = self.cnt[k]
        self._wait(q, need)
        inst = self.eng[q].dma_start(out=out, in_=in_, **kw)
        self.cnt[k] += 16
        inst.then_inc(self.semh[k], 16)
        self.ninst += 1
        self._record((k, self.cnt[k]), reads, writes)

    def barrier(self):
        for e in self.eng:
            need = {sk: v for sk, v in self.cnt.items() if v > 0}
            self._wait(e, need)
        self.tok = {}


class Ring:
    uid = 0

    def __init__(self, stack, nc, name, n, shape, dtype, psum=False):
        self.name = name
        self.n = n
        self.i = -1
        alloc = nc.psum_tensor if psum else nc.sbuf_tensor
        Ring.uid += 1
        self.tiles = [stack.enter_context(alloc("r%d_%s%d" % (Ring.uid, name, i), shape, dtype)) for i in range(n)]

    def next(self):
        self.i += 1
        s = self.i % self.n
        return self.tiles[s], (self.name, s)


class NS:
    pass


SCALE = float(HD) ** -0.5


def build_program(stop_after=None):
    nc = bass.Bass("TRN2", target_bir_lowering=False)
    C = NS()
    C.nc = nc
    C.stop_after = stop_after
    din = lambda name, shape, dt=F32: nc.dram_tensor(name, shape, dt, kind="ExternalInput").ap()
    dscr = lambda name, shape, dt=F32: nc.dram_tensor(name, shape, dt, kind="Internal").ap()
    C.x_d = din("x", [SEQ, D])
    C.ctx_d = din("ctx", [CTX, D])
    C.cpp_d = din("c_pp", [128, 8, 2])
    C.wmod_d = din("w_mod", [2, D, 6 * D])
    C.bmodpp_d = din("b_mod_pp", [2, 128, 48])
    C.bmod_d = din("b_mod", [2, 6 * D])
    C.gmixpp_d = din("g_mix_pp", [2, 128, 8])
    C.gffnpp_d = din("g_ffn_pp", [2, 128, 8])
    C.wqkv_d = din("w_qkv", [D, 1536])
    C.gq_d = din("g_q", [128])
    C.gk_d = din("g_k", [128])
    C.wo_d = din("w_o", [D, D])
    C.cos_d = din("rope_cos", [SEQ, 64])
    C.sin_d = din("rope_sin", [SEQ, 64])
    C.ident_d = din("ident", [128, 128])
    C.out_d = nc.dram_tensor("out", [SEQ, D], F32, kind="ExternalOutput").ap()
    C.QT_d = dscr("QT_scr", [NT, 128, 1024], BF16)
    C.gates_d = dscr("gates_scr", [4, 1024])
    C.wgu_d = din("w_gate_up", [2, D, 2 * DFF])
    C.wd_d = din("w_down", [2, DFF, D])
    C.wf_d = din("w_fourier", [D, D])
    C.bf_d = din("b_fourier", [D])
    C.gfin_d = din("g_final", [D])
    C.Fc_d = din("dft_fc", [128, 2, 512])
    C.M1_d = din("dft_m1", [128, 128])
    C.M2_d = din("dft_m2", [128, 64, 64])
    C.Z_d = dscr("Z_scr", [2, SEQ, D], BF16)
    C.T1_d = dscr("T1_scr", [2, 64, 64, D], BF16)
    C.f_d = dscr("f_scr", [SEQ, D], BF16)
    names = ["x1", "x2", "x3"]
    for i, nm in enumerate(names):
        setattr(C, nm + "_d", C.out_d if stop_after == "p%d" % (i + 2) and False else dscr(nm + "_scr", [SEQ, D]))
    if stop_after == "p2":
        C.x1_d = C.out_d
    if stop_after == "p3":
        C.x2_d = C.out_d
    if stop_after == "p6":
        C.x3_d = C.out_d

    with ExitStack() as gs:
        P = Prog(nc, gs)
        C.P = P
        C.gs = gs
        uid = [0]

        def _alloc(fn, pre, name, shape, dt, st):
            uid[0] += 1
            return st.enter_context(fn("%s%d_%s" % (pre, uid[0], name), shape, dt))

        C.sb = lambda name, shape, dt=F32, st=gs: _alloc(nc.sbuf_tensor, "sb", name, shape, dt, st)
        C.ps = lambda name, shape, dt=F32, st=gs: _alloc(nc.psum_tensor, "ps", name, shape, dt, st)
        sb = C.sb
        C.modpp = sb("modpp", [128, 2, 4, 8, 2])
        C.gmix = sb("gmix", [128, 2, 8])
        C.gffn = sb("gffn", [128, 2, 8])
        C.Amod = sb("Amod", [128, 5, 8])
        C.Bmod = sb("Bmod", [128, 5, 8])
        C.ident_f = sb("ident_f", [128, 128])
        C.ident = sb("ident", [128, 128], BF16)
        C.ones = sb("ones", [128, 128], BF16)
        P.dma("sp", C.ident_f[:], C.ident_d, writes=["ident_f"])
        P.op("dve", lambda e: e.tensor_copy(out=C.ident[:], in_=C.ident_f[:]), reads=["ident_f"], writes=["ident"])
        P.op("dve", lambda e: e.memset(C.ones[:], 1.0), writes=["ones"])
        C.eps = sb("eps", [128, 1])
        P.op("dve", lambda e: e.memset(C.eps[:], EPS), writes=["eps"])

        phase0(C)
        if stop_after == "p0":
            return nc
        with ExitStack() as st12:
            C.KT = sb("KT", [128, NKV, NKT * 128], BF16, st=st12)
            C.Vs = sb("Vs", [128, NKT, NKV * HD], BF16, st=st12)
            phase1(C)
            phase2(C)
        if stop_after == "p2":
            return nc
        ffn_phase(C, 0, C.x1_d, C.x2_d)
        if stop_after == "p3":
            return nc
        phase4(C)
        phase5(C)
        phase6(C)
        if stop_after == "p6":
            return nc
        ffn_phase(C, 1, C.x3_d, C.out_d, final=True)
        print("instructions:", P.ninst)
    return nc


def phase0(C):
    nc, P, sb, ps = C.nc, C.P, C.sb, C.ps
    modpp, gmix, gffn, Amod, Bmod = C.modpp, C.gmix, C.gffn, C.Amod, C.Bmod
    with ExitStack() as st:
        gates = sb("gates", [128, 4, 1024], st=st)
        cpp = sb("cpp", [128, 8, 2], st=st)
        sc = sb("sc", [128, 8, 2], st=st)
        sig = sb("sig", [128, 8, 2], st=st)
        scb = sb("scb", [128, 8, 128], st=st)
        bpp = sb("bpp", [128, 2, 48], st=st)
        brow = sb("brow", [128, 4, 1024], st=st)
        wring = Ring(st, nc, "wm", 2, [128, 8, 1024], F32)
        pp_ps = ps("pp_ps", [128, 512], st=st)
        row_ps = Ring(st, nc, "row_ps", 2, [128, 512], F32, psum=True)

        P.dma("sp", cpp[:], C.cpp_d, writes=["cpp"])
        P.dma("sp", bpp[:], C.bmodpp_d.rearrange("l p f -> p l f"), writes=["bpp"])
        P.dma("sp", gmix[:], C.gmixpp_d.rearrange("l p f -> p l f"), writes=["gmix"])
        P.dma("sp", gffn[:], C.gffnpp_d.rearrange("l p f -> p l f"), writes=["gffn"])
        for l in range(2):
            for gi, m in enumerate((2, 5)):
                P.dma("sp", brow[:, l * 2 + gi, :],
                      C.bmod_d[l, m * 1024:(m + 1) * 1024].partition_broadcast(128),
                      writes=[("brow", l * 2 + gi)])
        P.op("act", lambda e: e.activation(out=sig[:], in_=cpp[:], func=AF.Sigmoid), reads=["cpp"], writes=["sig"])
        P.op("dve", lambda e: e.tensor_tensor(out=sc[:], in0=cpp[:], in1=sig[:], op=ALU.mult),
             reads=["cpp", "sig"], writes=["sc"])
        P.op("dve", lambda e: e.tensor_copy(out=scb[:], in_=sc[:, :, 0:1].to_broadcast([128, 8, 128])),
             reads=["sc"], writes=["scb"])
        wv = C.wmod_d.rearrange("l (kc p) f -> l p kc f", p=128)
        for l in range(2):
            for m in range(6):
                wt, wtok = wring.next()
                P.dma("sp", wt[:], wv[l, :, :, m * 1024:(m + 1) * 1024], writes=[wtok])
                if m in (2, 5):
                    gi = l * 2 + (0 if m == 2 else 1)
                    for h in range(2):
                        rp, rtok = row_ps.next()
                        for kc in range(8):
                            P.op("pe", lambda e, kc=kc, rp=rp, wt=wt, h=h: e.matmul(
                                rp[:], lhsT=scb[:, kc, :], rhs=wt[:, kc, h * 512:(h + 1) * 512],
                                start=(kc == 0), stop=(kc == 7)),
                                reads=["scb", wtok], writes=[rtok])
                        P.op("dve", lambda e, rp=rp, gi=gi, h=h: e.tensor_tensor(
                            out=gates[:, gi, h * 512:(h + 1) * 512], in0=rp[:],
                            in1=brow[:, gi, h * 512:(h + 1) * 512], op=ALU.add),
                            reads=[rtok, ("brow", gi)], writes=[("gates", gi, h)])
                        if h == 1:
                            P.dma("sp", C.gates_d[gi:gi + 1, :], gates[0:1, gi, :],
                                  reads=[("gates", gi, 0), ("gates", gi, 1)], writes=[("gates_d", gi)])
                else:
                    mi = {0: 0, 1: 1, 3: 2, 4: 3}[m]
                    for fc in range(8):
                        o = ((l * 4 + mi) * 8 + fc) * 2
                        for kc in range(8):
                            P.op("pe", lambda e, kc=kc, fc=fc, wt=wt, o=o: e.matmul(
                                pp_ps[:, o:o + 2], lhsT=wt[:, kc, fc * 128:(fc + 1) * 128], rhs=sc[:, kc, :],
                                start=(kc == 0), stop=(kc == 7)),
                                reads=["sc", wtok], writes=["pp_ps"])
                    P.op("dve", lambda e, l=l, mi=mi, m=m: e.tensor_tensor(
                        out=modpp[:, l, mi, :, :],
                        in0=pp_ps[:, (l * 4 + mi) * 16:(l * 4 + mi + 1) * 16].rearrange("p (f t) -> p f t", t=2),
                        in1=bpp[:, l, m * 8:(m + 1) * 8].unsqueeze(2).to_broadcast([128, 8, 2]), op=ALU.add),
                        reads=["pp_ps", "bpp"], writes=[("modpp", l, mi)])
        combos = [(0, 0, 1, 0, gmix, 0), (1, 0, 1, 0, gmix, 1), (2, 0, 3, 2, gffn, 0),
                  (3, 1, 1, 0, gmix, 0), (4, 1, 3, 2, gffn, 0)]
        for idx, l, m_sc, m_sh, g, col in combos:
            P.op("dve", lambda e, idx=idx, l=l, m_sc=m_sc, g=g, col=col: e.scalar_tensor_tensor(
                out=Amod[:, idx, :], in0=modpp[:, l, m_sc, :, col], scalar=1.0, in1=g[:, l, :],
                op0=ALU.add, op1=ALU.mult),
                reads=[("modpp", l, m_sc), "gmix", "gffn"], writes=[("Amod", idx)])
            P.op("dve", lambda e, idx=idx, l=l, m_sh=m_sh, col=col: e.tensor_copy(
                out=Bmod[:, idx, :], in_=modpp[:, l, m_sh, :, col]),
                reads=[("modpp", l, m_sh)], writes=[("Bmod", idx)])
        P.barrier()


def rms_stage(C, R, src_ap, a_idx, hT, hTtok, col0=0):
    P = C.P
    xt, xtok = R["xt"].next()
    P.dma("sp", xt[:], src_ap, writes=[xtok])
    rms_from_tile(C, R, xt, xtok, a_idx, hT, hTtok, col0)
    return xt, xtok


def rms_from_tile(C, R, xt, xtok, a_idx, hT, hTtok, col0=0):
    P = C.P
    ss, sstok = R["ss"].next()
    xn, xntok = R["xn"].next()
    tp, tptok = R["tp"].next()
    junk = R["junk"]
    P.op("act", lambda e: e.activation(out=junk[:], in_=xt[:], func=AF.Square, accum_out=ss[:, 0:1]),
         reads=[xtok], writes=["junk", (sstok, 0)])
    P.op("act", lambda e: e.activation(out=ss[:, 1:2], in_=ss[:, 0:1], func=AF.Sqrt, scale=1.0 / D, bias=C.eps[:, 0:1]),
         reads=[(sstok, 0), "eps"], writes=[(sstok, 1)])
    P.op("dve", lambda e: e.reciprocal(out=ss[:, 2:3], in_=ss[:, 1:2]), reads=[(sstok, 1)], writes=[(sstok, 2)])
    P.op("act", lambda e: e.activation(out=xn[:], in_=xt[:], func=AF.Identity, scale=ss[:, 2:3]),
         reads=[xtok, (sstok, 2)], writes=[xntok])
    for kc in range(8):
        P.op("pe", lambda e, kc=kc: e.transpose(out=tp[:, kc * 128:(kc + 1) * 128], in_=xn[:, kc * 128:(kc + 1) * 128],
                                                 identity=C.ident[:]),
             reads=[xntok, "ident"], writes=[tptok])
    for kc in range(8):
        P.op("dve", lambda e, kc=kc: e.tensor_scalar(
            out=hT[:, kc, col0:col0 + 128], in0=tp[:, kc * 128:(kc + 1) * 128],
            scalar1=C.Amod[:, a_idx, kc:kc + 1], scalar2=C.Bmod[:, a_idx, kc:kc + 1], op0=ALU.mult, op1=ALU.add),
            reads=[tptok, ("Amod", a_idx), ("Bmod", a_idx)], writes=[hTtok])


def load_weight_bf16(C, st, name, dst, dst_tok_fn, src_view, nchunks, chunk_shape, engines=("dve", "pool")):
    P, nc = C.P, C.nc
    ring = Ring(st, nc, name + "_stg", 2, chunk_shape, F32)
    for i in range(nchunks):
        t, tok = ring.next()
        P.dma("sp", t[:], src_view(i), writes=[tok])
        eng = engines[i % len(engines)]
        P.op(eng, lambda e, t=t, i=i: e.tensor_copy(out=dst(i), in_=t[:]), reads=[tok], writes=[dst_tok_fn(i)])


def phase1(C):
    nc, P, sb, ps = C.nc, C.P, C.sb, C.ps
    with ExitStack() as st:
        wqkv = sb("wqkv", [128, 8, 1536], BF16, st=st)
        cosb = sb("cosb", [128, NT, 64], BF16, st=st)
        sinb = sb("sinb", [128, NT, 64], BF16, st=st)
        gqk = sb("gqk", [128, 10, 128], st=st)
        P.dma("pool", cosb[:], C.cos_d.rearrange("(t p) f -> p t f", p=128), writes=["cos"])
        P.dma("pool", sinb[:], C.sin_d.rearrange("(t p) f -> p t f", p=128), writes=["sin"])
        for h in range(10):
            P.dma("sp", gqk[:, h, :], (C.gq_d if h < 8 else C.gk_d).partition_broadcast(128), writes=["gqk"])
        wv = C.wqkv_d.rearrange("(kc p) f -> p kc f", p=128)
        for kc in range(8):
            P.dma("pool", wqkv[:, kc, :], wv[:, kc, :], writes=[("wqkv", kc)])
        R = {
            "xt": Ring(st, nc, "xt", 3, [128, D], F32),
            "ss": Ring(st, nc, "ss", 3, [128, 4], F32),
            "xn": Ring(st, nc, "xn", 2, [128, D], BF16),
            "tp": Ring(st, nc, "tp", 2, [128, D], BF16, psum=True),
            "junk": sb("junk", [128, D], BF16, st=st),
        }
        hTr = Ring(st, nc, "hT", 2, [128, 8, 128], BF16)
        qkv_ps = [ps("qkv_ps%d" % i, [128, 512], st=st) for i in range(3)]
        qkvr = Ring(st, nc, "qkv_sb", 3, [128, 1536], F32)
        sqb = sb("sqb", [128, 1280], BF16, st=st)
        ssq = Ring(st, nc, "ssq", 2, [128, 3, 10], F32)
        qgr = Ring(st, nc, "qg", 2, [128, 10, 128], BF16)
        t1 = sb("t1", [128, 10, 64], BF16, st=st)
        t2 = sb("t2", [128, 10, 64], BF16, st=st)
        t3 = sb("t3", [128, 10, 64], BF16, st=st)
        t4 = sb("t4", [128, 10, 64], BF16, st=st)
        qrr = Ring(st, nc, "qr", 2, [128, 10, 128], BF16)
        qT_ps = ps("qT_ps", [128, 1024], BF16, st=st)
        kT_ps = ps("kT_ps", [128, 1024], BF16, st=st)
        qTr = Ring(st, nc, "qT_sb", 2, [128, 1024], BF16)
        info = {}

        def stage1a(T):
            lat = T < NT
            src = C.x_d[T * 128:(T + 1) * 128, :] if lat else C.ctx_d[(T - NT) * 128:(T - NT + 1) * 128, :]
            hT, hTtok = hTr.next()
            rms_stage(C, R, src, 0 if lat else 1, hT, hTtok)
            info[T] = dict(hT=hT, hTtok=hTtok)

        def stage1b(T):
            lat = T < NT
            hT, hTtok = info[T]["hT"], info[T]["hTtok"]
            banks = (0, 1, 2) if lat else (2,)
            for nb in banks:
                for kc in range(8):
                    P.op("pe", lambda e, nb=nb, kc=kc: e.matmul(
                        qkv_ps[nb][:], lhsT=hT[:, kc, :], rhs=wqkv[:, kc, nb * 512:(nb + 1) * 512],
                        start=(kc == 0), stop=(kc == 7)),
                        reads=[hTtok, ("wqkv", kc)], writes=[("qkv_ps", nb)])
            qs, qstok = qkvr.next()
            for nb in banks:
                P.op("act", lambda e, nb=nb: e.activation(out=qs[:, nb * 512:(nb + 1) * 512], in_=qkv_ps[nb][:],
                                                          func=AF.Identity),
                     reads=[("qkv_ps", nb)], writes=[(qstok, nb)])
            info[T].update(qs=qs, qstok=qstok)

        def stage2(T):
            lat = T < NT
            qs, qstok = info[T]["qs"], info[T]["qstok"]
            h0 = 0 if lat else 8
            nh = 10 - h0
            lo = h0 * 128
            rd = [(qstok, nb) for nb in ((0, 1, 2) if lat else (2,))]
            sq3, sqtok = ssq.next()
            P.op("act", lambda e: e.activation(out=sqb[:, lo:1280], in_=qs[:, lo:1280], func=AF.Square),
                 reads=rd, writes=["sqb"])
            P.op("dve", lambda e: e.tensor_reduce(out=sq3[:, 0, h0:10],
                                                  in_=sqb[:, lo:1280].rearrange("p (h d) -> p h d", d=128),
                                                  axis=AX.X, op=ALU.add),
                 reads=["sqb"], writes=[(sqtok, 0)])
            P.op("act", lambda e: e.activation(out=sq3[:, 1, h0:10], in_=sq3[:, 0, h0:10], func=AF.Sqrt,
                                               scale=1.0 / HD, bias=C.eps[:, 0:1]),
                 reads=[(sqtok, 0), "eps"], writes=[(sqtok, 1)])
            P.op("dve", lambda e: e.reciprocal(out=sq3[:, 2, h0:10], in_=sq3[:, 1, h0:10]),
                 reads=[(sqtok, 1)], writes=[(sqtok, 2)])
            qg, qgtok = qgr.next()
            for h in range(h0, 10):
                P.op("dve", lambda e, h=h: e.scalar_tensor_tensor(
                    out=qg[:, h, :], in0=qs[:, h * 128:(h + 1) * 128], scalar=sq3[:, 2, h:h + 1], in1=gqk[:, h, :],
                    op0=ALU.mult, op1=ALU.mult),
                    reads=rd + [(sqtok, 2), "gqk"], writes=[(qgtok, h)])
            if lat:
                qr, qrtok = qrr.next()
                cb = cosb[:, T, :].unsqueeze(1).to_broadcast([128, 10, 64])
                sbb = sinb[:, T, :].unsqueeze(1).to_broadcast([128, 10, 64])
                x1 = qg[:, :, 0:64]
                x2 = qg[:, :, 64:128]
                qall = [(qgtok, h) for h in range(10)]
                P.op("dve", lambda e: e.tensor_tensor(out=t1[:], in0=x1, in1=cb, op=ALU.mult), reads=qall + ["cos"], writes=["t1"])
                P.op("dve", lambda e: e.tensor_tensor(out=t2[:], in0=x2, in1=sbb, op=ALU.mult), reads=qall + ["sin"], writes=["t2"])
                P.op("dve", lambda e: e.tensor_tensor(out=t3[:], in0=x1, in1=sbb, op=ALU.mult), reads=qall + ["sin"], writes=["t3"])
                P.op("dve", lambda e: e.tensor_tensor(out=t4[:], in0=x2, in1=cb, op=ALU.mult), reads=qall + ["cos"], writes=["t4"])
                P.op("dve", lambda e: e.tensor_tensor(out=qr[:, :, 0:64], in0=t1[:], in1=t2[:], op=ALU.subtract),
                     reads=["t1", "t2"], writes=[(qrtok, 0)])
                P.op("dve", lambda e: e.tensor_tensor(out=qr[:, :, 64:128], in0=t3[:], in1=t4[:], op=ALU.add),
                     reads=["t3", "t4"], writes=[(qrtok, 1)])
                src_t, src_rd = qr, [(qrtok, 0), (qrtok, 1)]
            else:
                src_t, src_rd = qg, [(qgtok, 8), (qgtok, 9)]
            if lat:
                for h in range(8):
                    P.op("pe", lambda e, h=h: e.transpose(out=qT_ps[:, h * 128:(h + 1) * 128], in_=src_t[:, h, :],
                                                           identity=C.ident[:]),
                         reads=src_rd + ["ident"], writes=["qT_ps"])
            for j in range(2):
                P.op("pe", lambda e, j=j: e.transpose(out=kT_ps[:, j * 128:(j + 1) * 128], in_=src_t[:, 8 + j, :],
                                                       identity=C.ident[:]),
                     reads=src_rd + ["ident"], writes=["kT_ps"])
            if lat:
                qT, qTtok = qTr.next()
                P.op("act", lambda e: e.activation(out=qT[:], in_=qT_ps[:], func=AF.Identity), reads=["qT_ps"], writes=[qTtok])
                P.dma("pool", C.QT_d[T], qT[:], reads=[qTtok], writes=[("QT_d", T)])
            P.op("dve", lambda e: e.tensor_copy(
                out=C.KT[:, :, T * 128:(T + 1) * 128], in_=kT_ps[:, 0:256].rearrange("p (j t) -> p j t", t=128)),
                reads=["kT_ps"], writes=[("KT", T)])
            P.op("pool", lambda e: e.tensor_copy(out=C.Vs[:, T, :], in_=qs[:, 1280:1536]), reads=[(qstok, 2)],
                 writes=[("V", T)])
            del info[T]

        for T in range(NKT + 2):
            if T < NKT:
                stage1a(T)
            if 0 <= T - 1 < NKT:
                stage1b(T - 1)
            if 0 <= T - 2 < NKT:
                stage2(T - 2)
        P.barrier()


def phase2(C):
    nc, P, sb, ps = C.nc, C.P, C.sb, C.ps
    with ExitStack() as st:
        wo = sb("wo", [128, 8, 1024], BF16, st=st)
        wv = C.wo_d.rearrange("(h p) f -> p h f", p=128)
        for h in range(8):
            P.dma("pool", wo[:, h, :], wv[:, h, :], writes=[("wo", h)])
        wavg = sb("wavg", [128, 128], st=st)
        P.op("dve", lambda e: e.memset(wavg[:], 1.0 / 32.0), writes=["wavg"])
        qTr = Ring(st, nc, "qT2", 2, [128, 1024], BF16)
        xtr = Ring(st, nc, "xt2", 3, [128, D], F32)
        PTr = Ring(st, nc, "PT", 10, [128, 512], BF16)
        Sr = Ring(st, nc, "S_ps", 3, [128, 512], F32, psum=True)
        accr = Ring(st, nc, "acc_ps", 2, [128, 512], F32, psum=True)
        denr = Ring(st, nc, "den_ps", 2, [128, 512], F32, psum=True)
        wo_ps1 = ps("wo_ps", [128, 512], st=st)
        wo_ps = [wo_ps1, wo_ps1]
        densr = Ring(st, nc, "den_sb", 2, [128, 512], F32)
        recr = Ring(st, nc, "rec", 2, [128, 512], F32)
        OTr = Ring(st, nc, "OT", 2, [128, 8, 128], BF16)
        x1r = Ring(st, nc, "x1t", 2, [128, D], F32)
        gate = sb("gate_p2", [128, D], st=st)
        P.dma("sp", gate[:], C.gates_d[0].partition_broadcast(128), writes=["gate"])

        steps = [(T, kvh, kt) for T in range(NT) for kvh in range(NKV) for kt in range(NKT)]
        state = {}
        pending = []

        def load_tile(T):
            qT, qTtok = qTr.next()
            xt, xtok = xtr.next()
            P.dma("sp", qT[:], C.QT_d[T], writes=[qTtok])
            P.dma("sp", xt[:], C.x_d[T * 128:(T + 1) * 128, :], writes=[xtok])
            OT, OTtok = OTr.next()
            state[T] = dict(qT=qT, qTtok=qTtok, xt=xt, xtok=xtok, OT=OT, OTtok=OTtok, batch=[])

        def emit_S(i):
            T, kvh, kt = steps[i]
            if kvh == 0 and kt == 0:
                if T not in state:
                    load_tile(T)
                if T + 1 < NT and (T + 1) not in state:
                    load_tile(T + 1)
            s = state[T]
            S, Stok = Sr.next()
            P.op("pe", lambda e: e.matmul(S[:], lhsT=C.KT[:, kvh, kt * 128:(kt + 1) * 128],
                                           rhs=s["qT"][:, kvh * 512:(kvh + 1) * 512], start=True, stop=True),
                 reads=[("KT", kt), s["qTtok"]], writes=[Stok])
            PT, PTtok = PTr.next()
            P.op("act", lambda e: e.activation(out=PT[:], in_=S[:], func=AF.Exp, scale=SCALE), reads=[Stok], writes=[PTtok])
            return PT, PTtok

        def emit_PV(i, PT, PTtok):
            T, kvh, kt = steps[i]
            s = state[T]
            if kt == 0:
                s["acc"], s["acctok"] = accr.next()
                s["den"], s["dentok"] = denr.next()
                s["batch"] = []
            acc, den, dentok, acctok = s["acc"], s["den"], s["dentok"], s["acctok"]
            P.op("pe", lambda e: e.matmul(acc[:], lhsT=C.Vs[:, kt, kvh * 128:(kvh + 1) * 128], rhs=PT[:],
                                           start=(kt == 0), stop=(kt == NKT - 1)),
                 reads=[("V", kt), PTtok], writes=[acctok])
            s["batch"].append((kt, PT, PTtok))
            if kt % 4 == 3 or kt == NKT - 1:
                for (k2, PT2, PTtok2) in s["batch"]:
                    j = k2 % 4
                    P.op("pe", lambda e, j=j, PT2=PT2, k2=k2: e.matmul(
                        den[32 * j:32 * j + 32, :], lhsT=C.ones[:, 0:32], rhs=PT2[:],
                        start=(k2 < 4), stop=(k2 >= NKT - 4), tile_position=(0, 32 * j)),
                        reads=["ones", PTtok2], writes=[dentok])
                s["batch"] = []
            if kt == NKT - 1:
                dens, denstok = densr.next()
                P.op("dve", lambda e: e.tensor_copy(out=dens[:], in_=den[:]), reads=[dentok], writes=[denstok])
                OT, OTtok = s["OT"], s["OTtok"]

                def fin(T=T, kvh=kvh, den=den, dentok=dentok, acc=acc, acctok=acctok, dens=dens, denstok=denstok,
                        OT=OT, OTtok=OTtok):
                    P.op("pe", lambda e: e.matmul(den[:], lhsT=wavg[:], rhs=dens[:], start=True, stop=True),
                         reads=["wavg", denstok], writes=[dentok])
                    rec, rectok = recr.next()
                    P.op("dve", lambda e: e.reciprocal(out=rec[:], in_=den[:]), reads=[dentok], writes=[rectok])
                    P.op("dve", lambda e: e.tensor_tensor(
                        out=OT[:, kvh * 4:(kvh + 1) * 4, :].rearrange("p h q -> p (h q)"), in0=acc[:], in1=rec[:], op=ALU.mult),
                        reads=[acctok, rectok], writes=[(OTtok, kvh)])
                    if kvh == NKV - 1:
                        pending.append([3, lambda T=T: emit_wo(T)])

                pending.append([2, fin])

        def emit_wo(T):
            s = state[T]
            OT, OTtok = s["OT"], s["OTtok"]
            x1t, x1tok = x1r.next()
            for half in range(2):
                sl = slice(half * 512, (half + 1) * 512)
                for h in range(8):
                    P.op("pe", lambda e, half=half, h=h: e.matmul(
                        wo_ps[half][:], lhsT=OT[:, h, :], rhs=wo[:, h, half * 512:(half + 1) * 512],
                        start=(h == 0), stop=(h == 7)),
                        reads=[(OTtok, h // 4), ("wo", h)], writes=["wo_ps"])
                P.op("dve", lambda e, half=half, sl=sl: e.tensor_tensor(
                    out=x1t[:, sl], in0=wo_ps[half][:], in1=gate[:, sl], op=ALU.mult),
                    reads=["wo_ps", "gate"], writes=[(x1tok, half)])
                P.op("pool", lambda e, sl=sl: e.tensor_tensor(out=x1t[:, sl], in0=x1t[:, sl], in1=s["xt"][:, sl], op=ALU.add),
                     reads=[(x1tok, half), s["xtok"]], writes=[(x1tok, half)])
            P.dma("pool", C.x1_d[T * 128:(T + 1) * 128, :], x1t[:], reads=[(x1tok, 0), (x1tok, 1)], writes=[("x1_d", T)])
            del state[T]

        def tick():
            for p in list(pending):
                p[0] -= 1
                if p[0] <= 0:
                    pending.remove(p)
                    p[1]()

        LA = 2
        q = [emit_S(i) for i in range(LA)]
        for i in range(len(steps)):
            if i + LA < len(steps):
                q.append(emit_S(i + LA))
            emit_PV(i, *q.pop(0))
            tick()
        while pending:
            tick()
        P.barrier()


def ffn_phase(C, layer, src_d, dst_d, final=False):
    nc, P, sb, ps = C.nc, C.P, C.sb, C.ps
    a_idx = 2 if layer == 0 else 4
    gi = layer * 2 + 1
    with ExitStack() as st:
        wgu = sb("wgu", [128, 8, 2 * DFF], BF16, st=st)
        wd = sb("wd", [128, NFC, D], BF16, st=st)
        wguv = C.wgu_d[layer].rearrange("(kc p) f -> p kc f", p=128)
        wdv = C.wd_d[layer].rearrange("(fc p) d -> p fc d", p=128)
        for kc in range(8):
            for hh in range(2):
                P.dma("pool", wgu[:, kc, hh * DFF:(hh + 1) * DFF], wguv[:, kc, hh * DFF:(hh + 1) * DFF],
                      writes=[("wgu", kc, hh)])
        for fc in range(NFC):
            P.dma("pool", wd[:, fc, :], wdv[:, fc, :], writes=[("wd", fc)])
        gate = sb("gate_ffn", [128, D], st=st)
        P.dma("sp", gate[:], C.gates_d[gi].partition_broadcast(128), reads=[("gates_d", gi)], writes=["gate"])
        if final:
            gfin = sb("gfin", [128, D], st=st)
            P.dma("sp", gfin[:], C.gfin_d.partition_broadcast(128), writes=["gfin"])
        R = {
            "xt": Ring(st, nc, "fxt", 2, [128, D], F32),
            "ss": Ring(st, nc, "fss", 3, [128, 4], F32),
            "xn": Ring(st, nc, "fxn", 2, [128, D], BF16),
            "tp": Ring(st, nc, "ftp", 2, [128, D], BF16, psum=True),
            "junk": sb("fjunk", [128, D], BF16, st=st),
        }
        hT2 = sb("hT2", [128, 8, 512], BF16, st=st)
        aT = sb("aT", [128, NFC, 512], BF16, st=st)
        Gr = Ring(st, nc, "G_ps", 2, [128, 512], F32, psum=True)
        Ur = Ring(st, nc, "U_ps", 2, [128, 512], F32, psum=True)
        Dr = Ring(st, nc, "D_ps", 2, [128, 512], F32, psum=True)
        sgr = Ring(st, nc, "sg", 2, [128, 512], F32)
        xrr = Ring(st, nc, "xres", 2, [128, D], F32)
        xor_ = Ring(st, nc, "xo", 2, [128, D], F32)
        fss = Ring(st, nc, "finss", 2, [128, 4], F32)
        print("ffn sbuf remaining", nc.sbuf_bytes_remaining)
        NB = NT // 4

        def emit_rms(bk):
            for j in range(4):
                T = bk * 4 + j
                rms_stage(C, R, src_d[T * 128:(T + 1) * 128, :], a_idx, hT2, ("hT2", j), col0=j * 128)

        def emit_gu(bk):
            for fc in range(NFC):
                G, Gtok = Gr.next()
                U, Utok = Ur.next()
                for (ps_t, ps_tok, c0) in ((G, Gtok, fc * 128), (U, Utok, DFF + fc * 128)):
                    hh = 0 if c0 < DFF else 1
                    for kc in range(8):
                        P.op("pe", lambda e, ps_t=ps_t, c0=c0, kc=kc: e.matmul(
                            ps_t[:], lhsT=wgu[:, kc, c0:c0 + 128], rhs=hT2[:, kc, :], start=(kc == 0), stop=(kc == 7)),
                            reads=[("wgu", kc, hh)] + [("hT2", j) for j in range(4)], writes=[ps_tok])
                sg, sgtok = sgr.next()
                P.op("act", lambda e, sg=sg, G=G: e.activation(out=sg[:], in_=G[:], func=AF.Silu), reads=[Gtok], writes=[sgtok])
                P.op("dve", lambda e, sg=sg, U=U, fc=fc: e.tensor_tensor(out=aT[:, fc, :], in0=U[:], in1=sg[:], op=ALU.mult),
                     reads=[Utok, sgtok], writes=[("aT", fc)])

        def emit_down(bk):
            for j in range(4):
                T = bk * 4 + j
                xr, xrtok = xrr.next()
                P.dma("sp", xr[:], src_d[T * 128:(T + 1) * 128, :], writes=[xrtok])
                xo, xotok = xor_.next()
                for half in range(2):
                    Dp, Dtok = Dr.next()
                    sl = slice(half * 512, (half + 1) * 512)
                    for fc in range(NFC):
                        P.op("pe", lambda e, Dp=Dp, fc=fc, sl=sl, j=j: e.matmul(
                            Dp[:], lhsT=aT[:, fc, j * 128:(j + 1) * 128], rhs=wd[:, fc, sl],
                            start=(fc == 0), stop=(fc == NFC - 1)),
                            reads=[("aT", fc), ("wd", fc)], writes=[Dtok])
                    P.op("dve", lambda e, Dp=Dp, sl=sl, xo=xo: e.tensor_tensor(out=xo[:, sl], in0=Dp[:], in1=gate[:, sl], op=ALU.mult),
                         reads=[Dtok, "gate"], writes=[(xotok, half)])
                    P.op("pool", lambda e, sl=sl, xo=xo, xr=xr: e.tensor_tensor(out=xo[:, sl], in0=xo[:, sl], in1=xr[:, sl], op=ALU.add),
                         reads=[(xotok, half), xrtok], writes=[(xotok, half)])
                if final:
                    fs, fstok = fss.next()
                    P.op("act", lambda e, xo=xo, fs=fs: e.activation(out=R["junk"][:], in_=xo[:], func=AF.Square, accum_out=fs[:, 0:1]),
                         reads=[(xotok, 0), (xotok, 1)], writes=["junk", (fstok, 0)])
                    P.op("act", lambda e, fs=fs: e.activation(out=fs[:, 1:2], in_=fs[:, 0:1], func=AF.Sqrt, scale=1.0 / D, bias=C.eps[:, 0:1]),
                         reads=[(fstok, 0), "eps"], writes=[(fstok, 1)])
                    P.op("dve", lambda e, fs=fs: e.reciprocal(out=fs[:, 2:3], in_=fs[:, 1:2]), reads=[(fstok, 1)], writes=[(fstok, 2)])
                    P.op("dve", lambda e, xo=xo, fs=fs: e.scalar_tensor_tensor(
                        out=xo[:], in0=xo[:], scalar=fs[:, 2:3], in1=gfin[:], op0=ALU.mult, op1=ALU.mult),
                        reads=[(xotok, 0), (xotok, 1), (fstok, 2), "gfin"], writes=[(xotok, 0), (xotok, 1)])
                P.dma("pool", dst_d[T * 128:(T + 1) * 128, :], xo[:], reads=[(xotok, 0), (xotok, 1)], writes=[("dst", T)])

        emit_rms(0)
        for bk in range(NB):
            emit_gu(bk)
            if bk + 1 < NB:
                emit_rms(bk + 1)
            emit_down(bk)
        P.barrier()


def phase4(C):
    nc, P, sb, ps = C.nc, C.P, C.sb, C.ps
    with ExitStack() as st:
        Fc = sb("Fc", [128, 2, 512], BF16, st=st)
        P.dma("pool", Fc[:], C.Fc_d, writes=["Fc"])
        R = {
            "xt": Ring(st, nc, "p4xt", 3, [128, D], F32),
            "ss": Ring(st, nc, "p4ss", 3, [128, 4], F32),
            "xn": Ring(st, nc, "p4xn", 2, [128, D], BF16),
            "tp": Ring(st, nc, "p4tp", 2, [128, D], BF16, psum=True),
            "junk": sb("p4junk", [128, D], BF16, st=st),
        }
        hTr = Ring(st, nc, "p4hT", 2, [128, 8, 128], BF16)
        Zr = Ring(st, nc, "Z_ps", 4, [128, 512], F32, psum=True)
        zsr = Ring(st, nc, "zs", 3, [128, 2, D], BF16)
        for T in range(NT):
            hT, hTtok = hTr.next()
            rms_stage(C, R, C.x2_d[T * 128:(T + 1) * 128, :], 3, hT, hTtok)
            zs, zstok = zsr.next()
            for g in range(4):
                Z, Ztok = Zr.next()
                for cc in range(2):
                    P.op("pe", lambda e, Z=Z, g=g, cc=cc: e.matmul(Z[:], lhsT=hT[:, 2 * g + cc, :], rhs=Fc[:, cc, :],
                                                                    start=(cc == 0), stop=(cc == 1)),
                         reads=[hTtok, "Fc"], writes=[Ztok])
                eng = "act" if g % 2 == 0 else "dve"
                if eng == "act":
                    P.op("act", lambda e, Z=Z, g=g, zs=zs: e.activation(
                        out=zs[:, :, g * 256:(g + 1) * 256], in_=Z[:].rearrange("p (r c) -> p r c", r=2), func=AF.Identity),
                        reads=[Ztok], writes=[(zstok, g)])
                else:
                    P.op("dve", lambda e, Z=Z, g=g, zs=zs: e.tensor_copy(
                        out=zs[:, :, g * 256:(g + 1) * 256], in_=Z[:].rearrange("p (r c) -> p r c", r=2)),
                        reads=[Ztok], writes=[(zstok, g)])
            P.dma("pool", C.Z_d[:, T * 128:(T + 1) * 128, :].rearrange("r t c -> t r c"), zs[:],
                  reads=[(zstok, g) for g in range(4)], writes=[("Z_d", T)])
        P.barrier()


def phase5(C):
    nc, P, sb, ps = C.nc, C.P, C.sb, C.ps
    NB2 = 8
    with ExitStack() as st:
        M1 = sb("M1", [128, 128], BF16, st=st)
        P.dma("pool", M1[:], C.M1_d, writes=["M1"])
        ztr = Ring(st, nc, "zt", 2, [128, NB2, D], BF16)
        t1r = Ring(st, nc, "t1s", 2, [128, NB2, D], BF16)
        Tr = Ring(st, nc, "T1_ps", 4, [128, 512], F32, psum=True)
        zv = C.Z_d.rearrange("r (n1 n2) c -> (r n1) n2 c", n2=64)
        k = 0
        for blk in range(64 // NB2):
            zt, zttok = ztr.next()
            P.dma("sp", zt[:], zv[:, blk * NB2:(blk + 1) * NB2, :], writes=[zttok])
            t1s, t1tok = t1r.next()
            for i in range(NB2):
                for half in range(2):
                    Tp, Ttok = Tr.next()
                    sl = slice(half * 512, (half + 1) * 512)
                    P.op("pe", lambda e, Tp=Tp, i=i, sl=sl: e.matmul(Tp[:], lhsT=M1[:], rhs=zt[:, i, sl], start=True, stop=True),
                         reads=["M1", zttok], writes=[Ttok])
                    if k % 2 == 0:
                        P.op("act", lambda e, Tp=Tp, i=i, sl=sl: e.activation(out=t1s[:, i, sl], in_=Tp[:], func=AF.Identity),
                             reads=[Ttok], writes=[(t1tok, i, half)])
                    else:
                        P.op("dve", lambda e, Tp=Tp, i=i, sl=sl: e.tensor_copy(out=t1s[:, i, sl], in_=Tp[:]),
                             reads=[Ttok], writes=[(t1tok, i, half)])
                    k += 1
            rd = [(t1tok, i, h) for i in range(NB2) for h in range(2)]
            for r in range(2):
                P.dma("pool", C.T1_d[r, blk * NB2:(blk + 1) * NB2, :, :].rearrange("n2 k1 c -> k1 n2 c"),
                      t1s[r * 64:(r + 1) * 64, :, :], reads=rd, writes=[("T1_d", blk, r)])
        P.barrier()


def phase6(C):
    nc, P, sb, ps = C.nc, C.P, C.sb, C.ps
    NB1 = 8
    with ExitStack() as st:
        M2 = sb("M2", [128, 64, 64], BF16, st=st)
        P.dma("pool", M2[:], C.M2_d, writes=["M2"])
        ttr = Ring(st, nc, "tt", 2, [128, NB1, D], BF16)
        fsr = Ring(st, nc, "fs", 2, [64, NB1, D], BF16)
        Fr = Ring(st, nc, "f_ps", 4, [64, 512], F32, psum=True)
        tv = C.T1_d.rearrange("r n2 k1 c -> (r n2) k1 c")
        fv = C.f_d.rearrange("(k2 k1) c -> k2 k1 c", k1=64)
        k = 0
        for blk in range(64 // NB1):
            tt, tttok = ttr.next()
            P.dma("sp", tt[:], tv[:, blk * NB1:(blk + 1) * NB1, :], writes=[tttok])
            fs, fstok = fsr.next()
            for i in range(NB1):
                k1 = blk * NB1 + i
                for half in range(2):
                    Fp, Ftok = Fr.next()
                    sl = slice(half * 512, (half + 1) * 512)
                    P.op("pe", lambda e, Fp=Fp, i=i, sl=sl, k1=k1: e.matmul(Fp[:], lhsT=M2[:, k1, :], rhs=tt[:, i, sl],
                                                                        start=True, stop=True),
                         reads=["M2", tttok], writes=[Ftok])
                    if k % 2 == 0:
                        P.op("act", lambda e, Fp=Fp, i=i, sl=sl: e.activation(out=fs[:, i, sl], in_=Fp[:], func=AF.Identity),
                             reads=[Ftok], writes=[(fstok, i, half)])
                    else:
                        P.op("dve", lambda e, Fp=Fp, i=i, sl=sl: e.tensor_copy(out=fs[:, i, sl], in_=Fp[:]),
                             reads=[Ftok], writes=[(fstok, i, half)])
                    k += 1
            P.dma("pool", fv[:, blk * NB1:(blk + 1) * NB1, :], fs[:],
                  reads=[(fstok, i, h) for i in range(NB1) for h in range(2)], writes=[("f_d", blk)])
        P.barrier()
    with ExitStack() as st:
        wf = sb("wf", [128, 8, D], BF16, st=st)
        wfv = C.wf_d.rearrange("(kc p) f -> p kc f", p=128)
        for kc in range(8):
            P.dma("pool", wf[:, kc, :], wfv[:, kc, :], writes=[("wf", kc)])
        gate = sb("gate_p6", [128, D], st=st)
        bfr = sb("bf_row", [128, D], st=st)
        P.dma("sp", gate[:], C.gates_d[2].partition_broadcast(128), writes=["gate"])
        P.dma("sp", bfr[:], C.bf_d.partition_broadcast(128), writes=["bfr"])
        ftr = Ring(st, nc, "ft", 3, [128, D], BF16)
        tpr = Ring(st, nc, "p6tp", 2, [128, D], BF16, psum=True)
        fTr = Ring(st, nc, "fT", 2, [128, 8, 128], BF16)
        xtr = Ring(st, nc, "p6xt", 3, [128, D], F32)
        xor_ = Ring(st, nc, "p6xo", 2, [128, D], F32)
        Wr = Ring(st, nc, "wf_ps", 4, [128, 512], F32, psum=True)
        for T in range(NT):
            ft, fttok = ftr.next()
            P.dma("sp", ft[:], C.f_d[T * 128:(T + 1) * 128, :], writes=[fttok])
            xt, xtok = xtr.next()
            P.dma("sp", xt[:], C.x2_d[T * 128:(T + 1) * 128, :], writes=[xtok])
            tp, tptok = tpr.next()
            for kc in range(8):
                P.op("pe", lambda e, kc=kc: e.transpose(out=tp[:, kc * 128:(kc + 1) * 128], in_=ft[:, kc * 128:(kc + 1) * 128],
                                                         identity=C.ident[:]),
                     reads=[fttok, "ident"], writes=[tptok])
            fT, fTtok = fTr.next()
            P.op("act", lambda e: e.activation(out=fT[:].rearrange("p k t -> p (k t)"), in_=tp[:], func=AF.Identity),
                 reads=[tptok], writes=[fTtok])
            xo, xotok = xor_.next()
            for half in range(2):
                Wp, Wtok = Wr.next()
                sl = slice(half * 512, (half + 1) * 512)
                for kc in range(8):
                    P.op("pe", lambda e, Wp=Wp, kc=kc, sl=sl: e.matmul(Wp[:], lhsT=fT[:, kc, :], rhs=wf[:, kc, sl],
                                                                       start=(kc == 0), stop=(kc == 7)),
                         reads=[fTtok, ("wf", kc)], writes=[Wtok])
                P.op("dve", lambda e, Wp=Wp, sl=sl: e.tensor_tensor(out=xo[:, sl], in0=Wp[:], in1=bfr[:, sl], op=ALU.add),
                     reads=[Wtok, "bfr"], writes=[(xotok, half)])
                P.op("pool", lambda e, sl=sl: e.tensor_tensor(out=xo[:, sl], in0=xo[:, sl], in1=gate[:, sl], op=ALU.mult),
                     reads=[(xotok, half), "gate"], writes=[(xotok, half)])
                P.op("pool", lambda e, sl=sl: e.tensor_tensor(out=xo[:, sl], in0=xo[:, sl], in1=xt[:, sl], op=ALU.add),
                     reads=[(xotok, half), xtok], writes=[(xotok, half)])
            P.dma("pool", C.x3_d[T * 128:(T + 1) * 128, :], xo[:], reads=[(xotok, 0), (xotok, 1)], writes=[("x3_d", T)])
        P.barrier()


def _dft_tables():
    c = np.arange(256, dtype=np.float64)
    ang = 2 * np.pi * np.outer(c, c) / 256.0
    Cc, Sc = np.cos(ang) / 16.0, np.sin(ang) / 16.0
    fc = np.concatenate([Cc, -Sc], axis=1)
    fc = fc.reshape(2, 128, 512).transpose(1, 0, 2)
    n = np.arange(64, dtype=np.float64)
    a1 = 2 * np.pi * np.outer(n, n) / 64.0
    C1, S1 = np.cos(a1) / 8.0, np.sin(a1) / 8.0
    m1 = np.block([[C1, -S1], [S1, C1]])
    n2 = n[:, None, None]; k1 = n[None, :, None]; k2 = n[None, None, :]
    th = 2 * np.pi * (n2 * k2 / 64.0 + n2 * k1 / 4096.0)
    m2 = np.concatenate([np.cos(th), np.sin(th)], axis=0) / 8.0
    f32 = lambda a: np.ascontiguousarray(a.astype(np.float32))
    return f32(fc), f32(m1), f32(m2)


def _rope_tables():
    rows = SEQ // 64
    row = np.repeat(np.arange(rows), 64).astype(np.float32)
    col = np.tile(np.arange(64), rows).astype(np.float32)
    inv_freq = (np.float32(10000.0) ** (-np.arange(32, dtype=np.float32) / np.float32(32))).astype(np.float32)
    ang = np.concatenate([row[:, None] * inv_freq, col[:, None] * inv_freq], axis=-1).astype(np.float32)
    return np.cos(ang).astype(np.float32), np.sin(ang).astype(np.float32)


def _host_inputs(inputs):
    f = lambda a: np.ascontiguousarray(np.asarray(a, dtype=np.float32))
    x = f(inputs["x"]); c = f(inputs["c"]); ctx = f(inputs["ctx"]); c_ctx = f(inputs["c_ctx"])
    w_mod = f(inputs["w_mod"]); b_mod = f(inputs["b_mod"])
    pp = lambda v: np.ascontiguousarray(v.reshape(-1, 128).T)
    b_mod_pp = np.stack([pp(b_mod[l]) for l in range(2)])
    g_mix_pp = np.stack([pp(f(inputs["g_mix"])[l]) for l in range(2)])
    g_ffn_pp = np.stack([pp(f(inputs["g_ffn"])[l]) for l in range(2)])
    cos, sin = _rope_tables()
    shared = {
        "w_mod": w_mod, "b_mod_pp": b_mod_pp, "b_mod": b_mod, "g_mix_pp": g_mix_pp, "g_ffn_pp": g_ffn_pp,
        "w_qkv": f(inputs["w_qkv"])[0], "g_q": f(inputs["g_q"])[0], "g_k": f(inputs["g_k"])[0],
        "w_o": f(inputs["w_attn_out"])[0], "rope_cos": cos, "rope_sin": sin,
        "ident": np.eye(128, dtype=np.float32),
        "w_gate_up": f(inputs["w_gate_up"]), "w_down": f(inputs["w_down"]),
        "w_fourier": f(inputs["w_fourier"])[0], "b_fourier": f(inputs["b_fourier"])[0],
        "g_final": f(inputs["g_final"]),
    }
    shared["dft_fc"], shared["dft_m1"], shared["dft_m2"] = _dft_tables()
    maps = []
    for b in range(NCORES):
        c_pp = np.ascontiguousarray(np.stack([pp(c[b]), pp(c_ctx)], axis=-1))
        m = {"x": x[b], "ctx": ctx[b], "c_pp": c_pp}
        m.update(shared)
        maps.append(m)
    return maps


def kernel(**inputs):
    nc = build_program()
    maps = _host_inputs(inputs)
    res = run_bass_kernel_spmd(nc, maps, core_ids=list(range(NCORES)))
    return np.stack([np.asarray(r["out"], dtype=np.float32) for r in res.results], axis=0)
```

```python
import math
from contextlib import ExitStack

import numpy as np
import concourse.bass as bass
import concourse.mybir as mybir
from concourse.bass_utils import run_bass_kernel_spmd

F32 = mybir.dt.float32
BF16 = mybir.dt.bfloat16
AF = mybir.ActivationFunctionType
ALU = mybir.AluOpType
AX = mybir.AxisListType

D = 1024
SEQ = 4096
CTX = 256
NH = 8
NKV = 2
HD = 128
DFF = 2816
NFC = DFF // 128
NT = SEQ // 128
NTC = CTX // 128
NKT = NT + NTC
EPS = 1e-6
NCORES = 8


class Prog:
    def __init__(self, nc, stack):
        self.nc = nc
        self.eng = {"pe": nc.tensor, "act": nc.scalar, "dve": nc.vector, "pool": nc.gpsimd, "sp": nc.sync}
        self.semh = {}
        self.cnt = {}
        for e in self.eng:
            self.semh[e] = stack.enter_context(nc.semaphore("sem_" + e))
            self.cnt[e] = 0
        self.dq = {}
        for q, n in (("sp", 12), ("pool", 8), ("act", 4)):
            keys = []
            for i in range(n):
                k = "dma_%s_%d" % (q, i)
                self.semh[k] = stack.enter_context(nc.semaphore(k))
                self.cnt[k] = 0
                keys.append(k)
            self.dq[q] = {"keys": keys, "i": 0}
        self.known = {e: {} for e in self.eng}
        self.tok = {}
        self.ninst = 0

    def _need(self, e, reads, writes):
        need = {}

        def add(ev, same_ok):
            if ev is None:
                return
            sk, v = ev
            if sk == e and same_ok:
                return
            if need.get(sk, 0) < v:
                need[sk] = v

        for t in reads:
            st = self.tok.get(t)
            if st is not None:
                add(st["w"], False)
        for t in writes:
            st = self.tok.get(t)
            if st is not None:
                add(st["w"], True)
                for sk, v in st["r"].items():
                    add((sk, v), True)
        return need

    def _wait(self, e, need):
        eng = self.eng[e]
        kn = self.known[e]
        for sk, v in need.items():
            if kn.get(sk, 0) < v:
                eng.wait_ge(self.semh[sk], v)
                kn[sk] = v
                self.ninst += 1

    def _record(self, ev, reads, writes):
        for t in reads:
            st = self.tok.setdefault(t, {"w": None, "r": {}})
            if st["r"].get(ev[0], 0) < ev[1]:
                st["r"][ev[0]] = ev[1]
        for t in writes:
            self.tok[t] = {"w": ev, "r": {}}

    def op(self, e, fn, reads=(), writes=()):
        self._wait(e, self._need(e, reads, writes))
        inst = fn(self.eng[e])
        self.cnt[e] += 1
        inst.then_inc(self.semh[e], 1)
        self.ninst += 1
        self._record((e, self.cnt[e]), reads, writes)

    def dma(self, q, out, in_, reads=(), writes=(), **kw):
        dq = self.dq[q]
        k = dq["keys"][dq["i"] % len(dq["keys"])]
        dq["i"] += 1
        need = self._need(q, reads, writes)
        if self.cnt[k] > 0 and need.get(k, 0) < self.cnt[k]:
            need[k] = self.cnt[k]
        self._wait(q, need)
        inst = self.eng[q].dma_start(out=out, in_=in_, **kw)
        self.cnt[k] += 16
        inst.then_inc(self.semh[k], 16)
        self.ninst += 1
        self._record((k, self.cnt[k]), reads, writes)

    def barrier(self):
        for e in self.eng:
            need = {sk: v for sk, v in self.cnt.items() if v > 0}
            self._wait(e, need)
        self.tok = {}


class Ring:
    uid = 0

    def __init__(self, stack, nc, name, n, shape, dtype, psum=False):
        self.name = name
        self.n = n
        self.i = -1
        alloc = nc.psum_tensor if psum else nc.sbuf_tensor
        Ring.uid += 1
        self.tiles = [stack.enter_context(alloc("r%d_%s%d" % (Ring.uid, name, i), shape, dtype)) for i in range(n)]

    def next(self):
        self.i += 1
        s = self.i % self.n
        return self.tiles[s], (self.name, s)


class NS:
    pass


SCALE = float(HD) ** -0.5


def build_program(stop_after=None):
    nc = bass.Bass("TRN2", target_bir_lowering=False)
    C = NS()
    C.nc = nc
    C.stop_after = stop_after
    din = lambda name, shape, dt=F32: nc.dram_tensor(name, shape, dt, kind="ExternalInput").ap()
    dscr = lambda name, shape, dt=F32: nc.dram_tensor(name, shape, dt, kind="Internal").ap()
    C.x_d = din("x", [SEQ, D])
    C.ctx_d = din("ctx", [CTX, D])
    C.cpp_d = din("c_pp", [128, 8, 2])
    C.wmod_d = din("w_mod", [2, D, 6 * D])
    C.bmodpp_d = din("b_mod_pp", [2, 128, 48])
    C.bmod_d = din("b_mod", [2, 6 * D])
    C.gmixpp_d = din("g_mix_pp", [2, 128, 8])
    C.gffnpp_d = din("g_ffn_pp", [2, 128, 8])
    C.wqkv_d = din("w_qkv", [D, 1536])
    C.gq_d = din("g_q", [128])
    C.gk_d = din("g_k", [128])
    C.wo_d = din("w_o", [D, D])
    C.cos_d = din("rope_cos", [SEQ, 64])
    C.sin_d = din("rope_sin", [SEQ, 64])
    C.ident_d = din("ident", [128, 128])
    C.out_d = nc.dram_tensor("out", [SEQ, D], F32, kind="ExternalOutput").ap()
    C.QT_d = dscr("QT_scr", [NT, 128, 1024], BF16)
    C.gates_d = dscr("gates_scr", [4, 1024])
    C.wgu_d = din("w_gate_up", [2, D, 2 * DFF])
    C.wd_d = din("w_down", [2, DFF, D])
    C.wf_d = din("w_fourier", [D, D])
    C.bf_d = din("b_fourier", [D])
    C.gfin_d = din("g_final", [D])
    C.Fc_d = din("dft_fc", [128, 2, 512])
    C.M1_d = din("dft_m1", [128, 128])
    C.M2_d = din("dft_m2", [128, 64, 64])
    C.Z_d = dscr("Z_scr", [2, SEQ, D], BF16)
    C.T1_d = dscr("T1_scr", [2, 64, 64, D], BF16)
    C.f_d = dscr("f_scr", [SEQ, D], BF16)
    names = ["x1", "x2", "x3"]
    for i, nm in enumerate(names):
        setattr(C, nm + "_d", C.out_d if stop_after == "p%d" % (i + 2) and False else dscr(nm + "_scr", [SEQ, D]))
    if stop_after == "p2":
        C.x1_d = C.out_d
    if stop_after == "p3":
        C.x2_d = C.out_d
    if stop_after == "p6":
        C.x3_d = C.out_d

    with ExitStack() as gs:
        P = Prog(nc, gs)
        C.P = P
        C.gs = gs
        uid = [0]

        def _alloc(fn, pre, name, shape, dt, st):
            uid[0] += 1
            return st.enter_context(fn("%s%d_%s" % (pre, uid[0], name), shape, dt))

        C.sb = lambda name, shape, dt=F32, st=gs: _alloc(nc.sbuf_tensor, "sb", name, shape, dt, st)
        C.ps = lambda name, shape, dt=F32, st=gs: _alloc(nc.psum_tensor, "ps", name, shape, dt, st)
        sb = C.sb
        C.modpp = sb("modpp", [128, 2, 4, 8, 2])
        C.gmix = sb("gmix", [128, 2, 8])
        C.gffn = sb("gffn", [128, 2, 8])
        C.Amod = sb("Amod", [128, 5, 8])
        C.Bmod = sb("Bmod", [128, 5, 8])
        C.ident_f = sb("ident_f", [128, 128])
        C.ident = sb("ident", [128, 128], BF16)
        C.ones = sb("ones", [128, 128], BF16)
        P.dma("sp", C.ident_f[:], C.ident_d, writes=["ident_f"])
        P.op("dve", lambda e: e.tensor_copy(out=C.ident[:], in_=C.ident_f[:]), reads=["ident_f"], writes=["ident"])
        P.op("dve", lambda e: e.memset(C.ones[:], 1.0), writes=["ones"])
        C.eps = sb("eps", [128, 1])
        P.op("dve", lambda e: e.memset(C.eps[:], EPS), writes=["eps"])

        phase0(C)
        if stop_after == "p0":
            return nc
        with ExitStack() as st12:
            C.KT = sb("KT", [128, NKV, NKT * 128], BF16, st=st12)
            C.Vs = sb("Vs", [128, NKT, NKV * HD], BF16, st=st12)
            phase1(C)
            phase2(C)
        if stop_after == "p2":
            return nc
        ffn_phase(C, 0, C.x1_d, C.x2_d)
        if stop_after == "p3":
            return nc
        phase4(C)
        phase5(C)
        phase6(C)
        if stop_after == "p6":
            return nc
        ffn_phase(C, 1, C.x3_d, C.out_d, final=True)
        print("instructions:", P.ninst)
    return nc


def phase0(C):
    nc, P, sb, ps = C.nc, C.P, C.sb, C.ps
    modpp, gmix, gffn, Amod, Bmod = C.modpp, C.gmix, C.gffn, C.Amod, C.Bmod
    with ExitStack() as st:
        gates = sb("gates", [128, 4, 1024], st=st)
        cpp = sb("cpp", [128, 8, 2], st=st)
        sc = sb("sc", [128, 8, 2], st=st)
        sig = sb("sig", [128, 8, 2], st=st)
        scb = sb("scb", [128, 8, 128], st=st)
        bpp = sb("bpp", [128, 2, 48], st=st)
        brow = sb("brow", [128, 4, 1024], st=st)
        wring = Ring(st, nc, "wm", 2, [128, 8, 1024], F32)
        pp_ps = ps("pp_ps", [128, 512], st=st)
        row_ps = Ring(st, nc, "row_ps", 2, [128, 512], F32, psum=True)

        P.dma("sp", cpp[:], C.cpp_d, writes=["cpp"])
        P.dma("sp", bpp[:], C.bmodpp_d.rearrange("l p f -> p l f"), writes=["bpp"])
        P.dma("sp", gmix[:], C.gmixpp_d.rearrange("l p f -> p l f"), writes=["gmix"])
        P.dma("sp", gffn[:], C.gffnpp_d.rearrange("l p f -> p l f"), writes=["gffn"])
        for l in range(2):
            for gi, m in enumerate((2, 5)):
                P.dma("sp", brow[:, l * 2 + gi, :],
                      C.bmod_d[l, m * 1024:(m + 1) * 1024].partition_broadcast(128),
                      writes=[("brow", l * 2 + gi)])
        P.op("act", lambda e: e.activation(out=sig[:], in_=cpp[:], func=AF.Sigmoid), reads=["cpp"], writes=["sig"])
        P.op("dve", lambda e: e.tensor_tensor(out=sc[:], in0=cpp[:], in1=sig[:], op=ALU.mult),
             reads=["cpp", "sig"], writes=["sc"])
        P.op("dve", lambda e: e.tensor_copy(out=scb[:], in_=sc[:, :, 0:1].to_broadcast([128, 8, 128])),
             reads=["sc"], writes=["scb"])
        wv = C.wmod_d.rearrange("l (kc p) f -> l p kc f", p=128)
        for l in range(2):
            for m in range(6):
                wt, wtok = wring.next()
                P.dma("sp", wt[:], wv[l, :, :, m * 1024:(m + 1) * 1024], writes=[wtok])
                if m in (2, 5):
                    gi = l * 2 + (0 if m == 2 else 1)
                    for h in range(2):
                        rp, rtok = row_ps.next()
                        for kc in range(8):
                            P.op("pe", lambda e, kc=kc, rp=rp, wt=wt, h=h: e.matmul(
                                rp[:], lhsT=scb[:, kc, :], rhs=wt[:, kc, h * 512:(h + 1) * 512],
                                start=(kc == 0), stop=(kc == 7)),
                                reads=["scb", wtok], writes=[rtok])
                        P.op("dve", lambda e, rp=rp, gi=gi, h=h: e.tensor_tensor(
                            out=gates[:, gi, h * 512:(h + 1) * 512], in0=rp[:],
                            in1=brow[:, gi, h * 512:(h + 1) * 512], op=ALU.add),
                            reads=[rtok, ("brow", gi)], writes=[("gates", gi, h)])
                        if h == 1:
                            P.dma("sp", C.gates_d[gi:gi + 1, :], gates[0:1, gi, :],
                                  reads=[("gates", gi, 0), ("gates", gi, 1)], writes=[("gates_d", gi)])
                else:
                    mi = {0: 0, 1: 1, 3: 2, 4: 3}[m]
                    for fc in range(8):
                        o = ((l * 4 + mi) * 8 + fc) * 2
                        for kc in range(8):
                            P.op("pe", lambda e, kc=kc, fc=fc, wt=wt, o=o: e.matmul(
                                pp_ps[:, o:o + 2], lhsT=wt[:, kc, fc * 128:(fc + 1) * 128], rhs=sc[:, kc, :],
                                start=(kc == 0), stop=(kc == 7)),
                                reads=["sc", wtok], writes=["pp_ps"])
                    P.op("dve", lambda e, l=l, mi=mi, m=m: e.tensor_tensor(
                        out=modpp[:, l, mi, :, :],
                        in0=pp_ps[:, (l * 4 + mi) * 16:(l * 4 + mi + 1) * 16].rearrange("p (f t) -> p f t", t=2),
                        in1=bpp[:, l, m * 8:(m + 1) * 8].unsqueeze(2).to_broadcast([128, 8, 2]), op=ALU.add),
                        reads=["pp_ps", "bpp"], writes=[("modpp", l, mi)])
        combos = [(0, 0, 1, 0, gmix, 0), (1, 0, 1, 0, gmix, 1), (2, 0, 3, 2, gffn, 0),
                  (3, 1, 1, 0, gmix, 0), (4, 1, 3, 2, gffn, 0)]
        for idx, l, m_sc, m_sh, g, col in combos:
            P.op("dve", lambda e, idx=idx, l=l, m_sc=m_sc, g=g, col=col: e.scalar_tensor_tensor(
                out=Amod[:, idx, :], in0=modpp[:, l, m_sc, :, col], scalar=1.0, in1=g[:, l, :],
                op0=ALU.add, op1=ALU.mult),
                reads=[("modpp", l, m_sc), "gmix", "gffn"], writes=[("Amod", idx)])
            P.op("dve", lambda e, idx=idx, l=l, m_sh=m_sh, col=col: e.tensor_copy(
                out=Bmod[:, idx, :], in_=modpp[:, l, m_sh, :, col]),
                reads=[("modpp", l, m_sh)], writes=[("Bmod", idx)])
        P.barrier()


def rms_stage(C, R, src_ap, a_idx, hT, hTtok, col0=0):
    P = C.P
    xt, xtok = R["xt"].next()
    P.dma("sp", xt[:], src_ap, writes=[xtok])
    rms_from_tile(C, R, xt, xtok, a_idx, hT, hTtok, col0)
    return xt, xtok


def rms_from_tile(C, R, xt, xtok, a_idx, hT, hTtok, col0=0):
    P = C.P
    ss, sstok = R["ss"].next()
    xn, xntok = R["xn"].next()
    tp, tptok = R["tp"].next()
    junk = R["junk"]
    P.op("act", lambda e: e.activation(out=junk[:], in_=xt[:], func=AF.Square, accum_out=ss[:, 0:1]),
         reads=[xtok], writes=["junk", (sstok, 0)])
    P.op("act", lambda e: e.activation(out=ss[:, 1:2], in_=ss[:, 0:1], func=AF.Sqrt, scale=1.0 / D, bias=C.eps[:, 0:1]),
         reads=[(sstok, 0), "eps"], writes=[(sstok, 1)])
    P.op("dve", lambda e: e.reciprocal(out=ss[:, 2:3], in_=ss[:, 1:2]), reads=[(sstok, 1)], writes=[(sstok, 2)])
    P.op("act", lambda e: e.activation(out=xn[:], in_=xt[:], func=AF.Identity, scale=ss[:, 2:3]),
         reads=[xtok, (sstok, 2)], writes=[xntok])
    for kc in range(8):
        P.op("pe", lambda e, kc=kc: e.transpose(out=tp[:, kc * 128:(kc + 1) * 128], in_=xn[:, kc * 128:(kc + 1) * 128],
                                                 identity=C.ident[:]),
             reads=[xntok, "ident"], writes=[tptok])
    for kc in range(8):
        P.op("dve", lambda e, kc=kc: e.tensor_scalar(
            out=hT[:, kc, col0:col0 + 128], in0=tp[:, kc * 128:(kc + 1) * 128],
            scalar1=C.Amod[:, a_idx, kc:kc + 1], scalar2=C.Bmod[:, a_idx, kc:kc + 1], op0=ALU.mult, op1=ALU.add),
            reads=[tptok, ("Amod", a_idx), ("Bmod", a_idx)], writes=[hTtok])


def load_weight_bf16(C, st, name, dst, dst_tok_fn, src_view, nchunks, chunk_shape, engines=("dve", "pool")):
    P, nc = C.P, C.nc
    ring = Ring(st, nc, name + "_stg", 2, chunk_shape, F32)
    for i in range(nchunks):
        t, tok = ring.next()
        P.dma("sp", t[:], src_view(i), writes=[tok])
        eng = engines[i % len(engines)]
        P.op(eng, lambda e, t=t, i=i: e.tensor_copy(out=dst(i), in_=t[:]), reads=[tok], writes=[dst_tok_fn(i)])


def phase1(C):
    nc, P, sb, ps = C.nc, C.P, C.sb, C.ps
    with ExitStack() as st:
        wqkv = sb("wqkv", [128, 8, 1536], BF16, st=st)
        cosb = sb("cosb", [128, NT, 64], BF16, st=st)
        sinb = sb("sinb", [128, NT, 64], BF16, st=st)
        gqk = sb("gqk", [128, 10, 128], st=st)
        P.dma("pool", cosb[:], C.cos_d.rearrange("(t p) f -> p t f", p=128), writes=["cos"])
        P.dma("pool", sinb[:], C.sin_d.rearrange("(t p) f -> p t f", p=128), writes=["sin"])
        for h in range(10):
            P.dma("sp", gqk[:, h, :], (C.gq_d if h < 8 else C.gk_d).partition_broadcast(128), writes=["gqk"])
        wv = C.wqkv_d.rearrange("(kc p) f -> p kc f", p=128)
        for kc in range(8):
            P.dma("pool", wqkv[:, kc, :], wv[:, kc, :], writes=[("wqkv", kc)])
        R = {
            "xt": Ring(st, nc, "xt", 3, [128, D], F32),
            "ss": Ring(st, nc, "ss", 3, [128, 4], F32),
            "xn": Ring(st, nc, "xn", 2, [128, D], BF16),
            "tp": Ring(st, nc, "tp", 2, [128, D], BF16, psum=True),
            "junk": sb("junk", [128, D], BF16, st=st),
        }
        hTr = Ring(st, nc, "hT", 2, [128, 8, 128], BF16)
        qkv_ps = [ps("qkv_ps%d" % i, [128, 512], st=st) for i in range(3)]
        qkvr = Ring(st, nc, "qkv_sb", 3, [128, 1536], F32)
        sqb = sb("sqb", [128, 1280], BF16, st=st)
        ssq = Ring(st, nc, "ssq", 2, [128, 3, 10], F32)
        qgr = Ring(st, nc, "qg", 2, [128, 10, 128], BF16)
        t1 = sb("t1", [128, 10, 64], BF16, st=st)
        t2 = sb("t2", [128, 10, 64], BF16, st=st)
        t3 = sb("t3", [128, 10, 64], BF16, st=st)
        t4 = sb("t4", [128, 10, 64], BF16, st=st)
        qrr = Ring(st, nc, "qr", 2, [128, 10, 128], BF16)
        qT_ps = ps("qT_ps", [128, 1024], BF16, st=st)
        kT_ps = ps("kT_ps", [128, 1024], BF16, st=st)
        qTr = Ring(st, nc, "qT_sb", 2, [128, 1024], BF16)
        info = {}

        def stage1a(T):
            lat = T < NT
            src = C.x_d[T * 128:(T + 1) * 128, :] if lat else C.ctx_d[(T - NT) * 128:(T - NT + 1) * 128, :]
            hT, hTtok = hTr.next()
            rms_stage(C, R, src, 0 if lat else 1, hT, hTtok)
            info[T] = dict(hT=hT, hTtok=hTtok)

        def stage1b(T):
            lat = T < NT
            hT, hTtok = info[T]["hT"], info[T]["hTtok"]
            banks = (0, 1, 2) if lat else (2,)
            for nb in banks:
                for kc in range(8):
                    P.op("pe", lambda e, nb=nb, kc=kc: e.matmul(
                        qkv_ps[nb][:], lhsT=hT[:, kc, :], rhs=wqkv[:, kc, nb * 512:(nb + 1) * 512],
                        start=(kc == 0), stop=(kc == 7)),
                        reads=[hTtok, ("wqkv", kc)], writes=[("qkv_ps", nb)])
            qs, qstok = qkvr.next()
            for nb in banks:
                P.op("act", lambda e, nb=nb: e.activation(out=qs[:, nb * 512:(nb + 1) * 512], in_=qkv_ps[nb][:],
                                                          func=AF.Identity),
                     reads=[("qkv_ps", nb)], writes=[(qstok, nb)])
            info[T].update(qs=qs, qstok=qstok)

        def stage2(T):
            lat = T < NT
            qs, qstok = info[T]["qs"], info[T]["qstok"]
            h0 = 0 if lat else 8
            nh = 10 - h0
            lo = h0 * 128
            rd = [(qstok, nb) for nb in ((0, 1, 2) if lat else (2,))]
            sq3, sqtok = ssq.next()
            P.op("act", lambda e: e.activation(out=sqb[:, lo:1280], in_=qs[:, lo:1280], func=AF.Square),
                 reads=rd, writes=["sqb"])
            P.op("dve", lambda e: e.tensor_reduce(out=sq3[:, 0, h0:10],
                                                  in_=sqb[:, lo:1280].rearrange("p (h d) -> p h d", d=128),
                                                  axis=AX.X, op=ALU.add),
                 reads=["sqb"], writes=[(sqtok, 0)])
            P.op("act", lambda e: e.activation(out=sq3[:, 1, h0:10], in_=sq3[:, 0, h0:10], func=AF.Sqrt,
                                               scale=1.0 / HD, bias=C.eps[:, 0:1]),
                 reads=[(sqtok, 0), "eps"], writes=[(sqtok, 1)])
            P.op("dve", lambda e: e.reciprocal(out=sq3[:, 2, h0:10], in_=sq3[:, 1, h0:10]),
                 reads=[(sqtok, 1)], writes=[(sqtok, 2)])
            qg, qgtok = qgr.next()
            for h in range(h0, 10):
                P.op("dve", lambda e, h=h: e.scalar_tensor_tensor(
                    out=qg[:, h, :], in0=qs[:, h * 128:(h + 1) * 128], scalar=sq3[:, 2, h:h + 1], in1=gqk[:, h, :],
                    op0=ALU.mult, op1=ALU.mult),
                    reads=rd + [(sqtok, 2), "gqk"], writes=[(qgtok, h)])
            if lat:
                qr, qrtok = qrr.next()
                cb = cosb[:, T, :].unsqueeze(1).to_broadcast([128, 10, 64])
                sbb = sinb[:, T, :].unsqueeze(1).to_broadcast([128, 10, 64])
                x1 = qg[:, :, 0:64]
                x2 = qg[:, :, 64:128]
                qall = [(qgtok, h) for h in range(10)]
                P.op("dve", lambda e: e.tensor_tensor(out=t1[:], in0=x1, in1=cb, op=ALU.mult), reads=qall + ["cos"], writes=["t1"])
                P.op("dve", lambda e: e.tensor_tensor(out=t2[:], in0=x2, in1=sbb, op=ALU.mult), reads=qall + ["sin"], writes=["t2"])
                P.op("dve", lambda e: e.tensor_tensor(out=t3[:], in0=x1, in1=sbb, op=ALU.mult), reads=qall + ["sin"], writes=["t3"])
                P.op("dve", lambda e: e.tensor_tensor(out=t4[:], in0=x2, in1=cb, op=ALU.mult), reads=qall + ["cos"], writes=["t4"])
                P.op("dve", lambda e: e.tensor_tensor(out=qr[:, :, 0:64], in0=t1[:], in1=t2[:], op=ALU.subtract),
                     reads=["t1", "t2"], writes=[(qrtok, 0)])
                P.op("dve", lambda e: e.tensor_tensor(out=qr[:, :, 64:128], in0=t3[:], in1=t4[:], op=ALU.add),
                     reads=["t3", "t4"], writes=[(qrtok, 1)])
                src_t, src_rd = qr, [(qrtok, 0), (qrtok, 1)]
            else:
                src_t, src_rd = qg, [(qgtok, 8), (qgtok, 9)]
            if lat:
                for h in range(8):
                    P.op("pe", lambda e, h=h: e.transpose(out=qT_ps[:, h * 128:(h + 1) * 128], in_=src_t[:, h, :],
                                                           identity=C.ident[:]),
                         reads=src_rd + ["ident"], writes=["qT_ps"])
            for j in range(2):
                P.op("pe", lambda e, j=j: e.transpose(out=kT_ps[:, j * 128:(j + 1) * 128], in_=src_t[:, 8 + j, :],
                                                       identity=C.ident[:]),
                     reads=src_rd + ["ident"], writes=["kT_ps"])
            if lat:
                qT, qTtok = qTr.next()
                P.op("act", lambda e: e.activation(out=qT[:], in_=qT_ps[:], func=AF.Identity), reads=["qT_ps"], writes=[qTtok])
                P.dma("pool", C.QT_d[T], qT[:], reads=[qTtok], writes=[("QT_d", T)])
            P.op("dve", lambda e: e.tensor_copy(
                out=C.KT[:, :, T * 128:(T + 1) * 128], in_=kT_ps[:, 0:256].rearrange("p (j t) -> p j t", t=128)),
                reads=["kT_ps"], writes=[("KT", T)])
            P.op("pool", lambda e: e.tensor_copy(out=C.Vs[:, T, :], in_=qs[:, 1280:1536]), reads=[(qstok, 2)],
                 writes=[("V", T)])
            del info[T]

        for T in range(NKT + 2):
            if T < NKT:
                stage1a(T)
            if 0 <= T - 1 < NKT:
                stage1b(T - 1)
            if 0 <= T - 2 < NKT:
                stage2(T - 2)
        P.barrier()


def phase2(C):
    nc, P, sb, ps = C.nc, C.P, C.sb, C.ps
    NKP = NKT // 2
    with ExitStack() as st:
        wo = sb("wo", [128, 8, 1024], BF16, st=st)
        wv = C.wo_d.rearrange("(h p) f -> p h f", p=128)
        for h in range(8):
            P.dma("pool", wo[:, h, :], wv[:, h, :], writes=[("wo", h)])
        qTr = Ring(st, nc, "qT2", 2, [128, 1024], BF16)
        xtr = Ring(st, nc, "xt2", 3, [128, D], F32)
        PTr = Ring(st, nc, "PT", 4, [128, 1024], BF16)
        saccr = Ring(st, nc, "sacc", 2, [128, 1024], BF16)
        Sr = Ring(st, nc, "S_ps", 2, [128, 1024], F32, psum=True)
        accr = Ring(st, nc, "acc_ps", 2, [128, 512], F32, psum=True)
        den_ps = ps("den_ps", [128, 512], st=st)
        wo_ps = ps("wo_ps", [128, 512], st=st)
        recr = Ring(st, nc, "rec", 2, [128, 512], F32)
        OTr = Ring(st, nc, "OT", 2, [128, 8, 128], BF16)
        x1r = Ring(st, nc, "x1t", 2, [128, D], F32)
        gate = sb("gate_p2", [128, D], st=st)
        P.dma("sp", gate[:], C.gates_d[0].partition_broadcast(128), writes=["gate"])

        steps = [(T, kvh, p) for T in range(NT) for kvh in range(NKV) for p in range(NKP)]
        state = {}
        pending = []

        def load_tile(T):
            qT, qTtok = qTr.next()
            xt, xtok = xtr.next()
            P.dma("sp", qT[:], C.QT_d[T], writes=[qTtok])
            P.dma("sp", xt[:], C.x_d[T * 128:(T + 1) * 128, :], writes=[xtok])
            OT, OTtok = OTr.next()
            state[T] = dict(qT=qT, qTtok=qTtok, xt=xt, xtok=xtok, OT=OT, OTtok=OTtok)

        def emit_S(i):
            T, kvh, p = steps[i]
            if kvh == 0 and p == 0:
                if T not in state:
                    load_tile(T)
                if T + 1 < NT and (T + 1) not in state:
                    load_tile(T + 1)
            s = state[T]
            S, Stok = Sr.next()
            for u in range(2):
                kt = 2 * p + u
                P.op("pe", lambda e, kt=kt, u=u: e.matmul(
                    S[:, u * 512:(u + 1) * 512], lhsT=C.KT[:, kvh, kt * 128:(kt + 1) * 128],
                    rhs=s["qT"][:, kvh * 512:(kvh + 1) * 512], start=True, stop=True),
                    reads=[("KT", kt), s["qTtok"]], writes=[(Stok, u)])
            PT, PTtok = PTr.next()
            P.op("act", lambda e: e.activation(out=PT[:], in_=S[:], func=AF.Exp, scale=SCALE),
                 reads=[(Stok, 0), (Stok, 1)], writes=[PTtok])
            return PT, PTtok

        def emit_PV(i, PT, PTtok):
            T, kvh, p = steps[i]
            s = state[T]
            if p == 0:
                s["acc"], s["acctok"] = accr.next()
                s["sacc"], s["sacctok"] = saccr.next()
            acc, acctok, sacc, sacctok = s["acc"], s["acctok"], s["sacc"], s["sacctok"]
            for u in range(2):
                kt = 2 * p + u
                P.op("pe", lambda e, kt=kt, u=u: e.matmul(
                    acc[:], lhsT=C.Vs[:, kt, kvh * 128:(kvh + 1) * 128], rhs=PT[:, u * 512:(u + 1) * 512],
                    start=(kt == 0), stop=(kt == NKT - 1)),
                    reads=[("V", kt), PTtok], writes=[acctok])
            if p == 0:
                P.op("dve", lambda e: e.tensor_copy(out=sacc[:], in_=PT[:]), reads=[PTtok], writes=[sacctok])
            else:
                P.op("dve", lambda e: e.tensor_tensor(out=sacc[:], in0=sacc[:], in1=PT[:], op=ALU.add),
                     reads=[PTtok, sacctok], writes=[sacctok])
            if p == NKP - 1:
                OT, OTtok = s["OT"], s["OTtok"]

                def fin(T=T, kvh=kvh, acc=acc, acctok=acctok, sacc=sacc, sacctok=sacctok, OT=OT, OTtok=OTtok):
                    for u in range(2):
                        P.op("pe", lambda e, u=u: e.matmul(den_ps[:], lhsT=C.ones[:], rhs=sacc[:, u * 512:(u + 1) * 512],
                                                           start=(u == 0), stop=(u == 1)),
                             reads=["ones", sacctok], writes=["den_ps"])
                    rec, rectok = recr.next()
                    P.op("dve", lambda e: e.reciprocal(out=rec[:], in_=den_ps[:]), reads=["den_ps"], writes=[rectok])
                    P.op("dve", lambda e: e.tensor_tensor(
                        out=OT[:, kvh * 4:(kvh + 1) * 4, :].rearrange("p h q -> p (h q)"), in0=acc[:], in1=rec[:], op=ALU.mult),
                        reads=[acctok, rectok], writes=[(OTtok, kvh)])
                    if kvh == NKV - 1:
                        pending.append([2, lambda T=T: emit_wo(T)])

                pending.append([1, fin])

        def emit_wo(T):
            s = state[T]
            OT, OTtok = s["OT"], s["OTtok"]
            x1t, x1tok = x1r.next()
            for half in range(2):
                sl = slice(half * 512, (half + 1) * 512)
                for h in range(8):
                    P.op("pe", lambda e, half=half, h=h: e.matmul(
                        wo_ps[:], lhsT=OT[:, h, :], rhs=wo[:, h, half * 512:(half + 1) * 512],
                        start=(h == 0), stop=(h == 7)),
                        reads=[(OTtok, h // 4), ("wo", h)], writes=["wo_ps"])
                P.op("dve", lambda e, half=half, sl=sl: e.tensor_tensor(
                    out=x1t[:, sl], in0=wo_ps[:], in1=gate[:, sl], op=ALU.mult),
                    reads=["wo_ps", "gate"], writes=[(x1tok, half)])
                P.op("pool", lambda e, sl=sl: e.tensor_tensor(out=x1t[:, sl], in0=x1t[:, sl], in1=s["xt"][:, sl], op=ALU.add),
                     reads=[(x1tok, half), s["xtok"]], writes=[(x1tok, half)])
            P.dma("pool", C.x1_d[T * 128:(T + 1) * 128, :], x1t[:], reads=[(x1tok, 0), (x1tok, 1)], writes=[("x1_d", T)])
            del state[T]

        def tick():
            for p in list(pending):
                p[0] -= 1
                if p[0] <= 0:
                    pending.remove(p)
                    p[1]()

        LA = 1
        q = [emit_S(i) for i in range(LA)]
        for i in range(len(steps)):
            if i + LA < len(steps):
                q.append(emit_S(i + LA))
            emit_PV(i, *q.pop(0))
            tick()
        while pending:
            tick()
        P.barrier()


def ffn_phase(C, layer, src_d, dst_d, final=False):
    nc, P, sb, ps = C.nc, C.P, C.sb, C.ps
    a_idx = 2 if layer == 0 else 4
    gi = layer * 2 + 1
    with ExitStack() as st:
        wgu = sb("wgu", [128, 8, 2 * DFF], BF16, st=st)
        wd = sb("wd", [128, NFC, D], BF16, st=st)
        wguv = C.wgu_d[layer].rearrange("(kc p) f -> p kc f", p=128)
        wdv = C.wd_d[layer].rearrange("(fc p) d -> p fc d", p=128)
        for kc in range(8):
            for hh in range(2):
                P.dma("pool", wgu[:, kc, hh * DFF:(hh + 1) * DFF], wguv[:, kc, hh * DFF:(hh + 1) * DFF],
                      writes=[("wgu", kc, hh)])
        for fc in range(NFC):
            P.dma("pool", wd[:, fc, :], wdv[:, fc, :], writes=[("wd", fc)])
        gate = sb("gate_ffn", [128, D], st=st)
        P.dma("sp", gate[:], C.gates_d[gi].partition_broadcast(128), reads=[("gates_d", gi)], writes=["gate"])
        if final:
            gfin = sb("gfin", [128, D], st=st)
            P.dma("sp", gfin[:], C.gfin_d.partition_broadcast(128), writes=["gfin"])
        R = {
            "xt": Ring(st, nc, "fxt", 2, [128, D], F32),
            "ss": Ring(st, nc, "fss", 3, [128, 4], F32),
            "xn": Ring(st, nc, "fxn", 2, [128, D], BF16),
            "tp": Ring(st, nc, "ftp", 2, [128, D], BF16, psum=True),
            "junk": sb("fjunk", [128, D], BF16, st=st),
        }
        hT2 = sb("hT2", [128, 8, 512], BF16, st=st)
        aT = sb("aT", [128, NFC, 512], BF16, st=st)
        Gr = Ring(st, nc, "G_ps", 2, [128, 512], F32, psum=True)
        Ur = Ring(st, nc, "U_ps", 2, [128, 512], F32, psum=True)
        Dr = Ring(st, nc, "D_ps", 2, [128, 512], F32, psum=True)
        sgr = Ring(st, nc, "sg", 2, [128, 512], F32)
        xrr = Ring(st, nc, "xres", 2, [128, D], F32)
        xor_ = Ring(st, nc, "xo", 2, [128, D], F32)
        fss = Ring(st, nc, "finss", 2, [128, 4], F32)
        print("ffn sbuf remaining", nc.sbuf_bytes_remaining)
        NB = NT // 4

        def emit_rms(bk):
            for j in range(4):
                T = bk * 4 + j
                rms_stage(C, R, src_d[T * 128:(T + 1) * 128, :], a_idx, hT2, ("hT2", j), col0=j * 128)

        def emit_gu(bk):
            for fc in range(NFC):
                G, Gtok = Gr.next()
                U, Utok = Ur.next()
                for (ps_t, ps_tok, c0) in ((G, Gtok, fc * 128), (U, Utok, DFF + fc * 128)):
                    hh = 0 if c0 < DFF else 1
                    for kc in range(8):
                        P.op("pe", lambda e, ps_t=ps_t, c0=c0, kc=kc: e.matmul(
                            ps_t[:], lhsT=wgu[:, kc, c0:c0 + 128], rhs=hT2[:, kc, :], start=(kc == 0), stop=(kc == 7)),
                            reads=[("wgu", kc, hh)] + [("hT2", j) for j in range(4)], writes=[ps_tok])
                sg, sgtok = sgr.next()
                P.op("act", lambda e, sg=sg, G=G: e.activation(out=sg[:], in_=G[:], func=AF.Silu), reads=[Gtok], writes=[sgtok])
                P.op("dve", lambda e, sg=sg, U=U, fc=fc: e.tensor_tensor(out=aT[:, fc, :], in0=U[:], in1=sg[:], op=ALU.mult),
                     reads=[Utok, sgtok], writes=[("aT", fc)])

        def emit_down(bk):
            for j in range(4):
                T = bk * 4 + j
                xr, xrtok = xrr.next()
                P.dma("sp", xr[:], src_d[T * 128:(T + 1) * 128, :], writes=[xrtok])
                xo, xotok = xor_.next()
                for half in range(2):
                    Dp, Dtok = Dr.next()
                    sl = slice(half * 512, (half + 1) * 512)
                    for fc in range(NFC):
                        P.op("pe", lambda e, Dp=Dp, fc=fc, sl=sl, j=j: e.matmul(
                            Dp[:], lhsT=aT[:, fc, j * 128:(j + 1) * 128], rhs=wd[:, fc, sl],
                            start=(fc == 0), stop=(fc == NFC - 1)),
                            reads=[("aT", fc), ("wd", fc)], writes=[Dtok])
                    P.op("dve", lambda e, Dp=Dp, sl=sl, xo=xo: e.tensor_tensor(out=xo[:, sl], in0=Dp[:], in1=gate[:, sl], op=ALU.mult),
                         reads=[Dtok, "gate"], writes=[(xotok, half)])
                    P.op("pool", lambda e, sl=sl, xo=xo, xr=xr: e.tensor_tensor(out=xo[:, sl], in0=xo[:, sl], in1=xr[:, sl], op=ALU.add),
                         reads=[(xotok, half), xrtok], writes=[(xotok, half)])
                if final:
                    fs, fstok = fss.next()
                    P.op("act", lambda e, xo=xo, fs=fs: e.activation(out=R["junk"][:], in_=xo[:], func=AF.Square, accum_out=fs[:, 0:1]),
                         reads=[(xotok, 0), (xotok, 1)], writes=["junk", (fstok, 0)])
                    P.op("act", lambda e, fs=fs: e.activation(out=fs[:, 1:2], in_=fs[:, 0:1], func=AF.Sqrt, scale=1.0 / D, bias=C.eps[:, 0:1]),
                         reads=[(fstok, 0), "eps"], writes=[(fstok, 1)])
                    P.op("dve", lambda e, fs=fs: e.reciprocal(out=fs[:, 2:3], in_=fs[:, 1:2]), reads=[(fstok, 1)], writes=[(fstok, 2)])
                    P.op("dve", lambda e, xo=xo, fs=fs: e.scalar_tensor_tensor(
                        out=xo[:], in0=xo[:], scalar=fs[:, 2:3], in1=gfin[:], op0=ALU.mult, op1=ALU.mult),
                        reads=[(xotok, 0), (xotok, 1), (fstok, 2), "gfin"], writes=[(xotok, 0), (xotok, 1)])
                P.dma("pool", dst_d[T * 128:(T + 1) * 128, :], xo[:], reads=[(xotok, 0), (xotok, 1)], writes=[("dst", T)])

        emit_rms(0)
        for bk in range(NB):
            emit_gu(bk)
            if bk + 1 < NB:
                emit_rms(bk + 1)
            emit_down(bk)
        P.barrier()


def phase4(C):
    nc, P, sb, ps = C.nc, C.P, C.sb, C.ps
    with ExitStack() as st:
        Fc = sb("Fc", [128, 2, 512], BF16, st=st)
        P.dma("pool", Fc[:], C.Fc_d, writes=["Fc"])
        R = {
            "xt": Ring(st, nc, "p4xt", 3, [128, D], F32),
            "ss": Ring(st, nc, "p4ss", 3, [128, 4], F32),
            "xn": Ring(st, nc, "p4xn", 2, [128, D], BF16),
            "tp": Ring(st, nc, "p4tp", 2, [128, D], BF16, psum=True),
            "junk": sb("p4junk", [128, D], BF16, st=st),
        }
        hTr = Ring(st, nc, "p4hT", 2, [128, 8, 128], BF16)
        Zr = Ring(st, nc, "Z_ps", 4, [128, 512], F32, psum=True)
        zsr = Ring(st, nc, "zs", 3, [128, 2, D], BF16)
        for T in range(NT):
            hT, hTtok = hTr.next()
            rms_stage(C, R, C.x2_d[T * 128:(T + 1) * 128, :], 3, hT, hTtok)
            zs, zstok = zsr.next()
            for g in range(4):
                Z, Ztok = Zr.next()
                for cc in range(2):
                    P.op("pe", lambda e, Z=Z, g=g, cc=cc: e.matmul(Z[:], lhsT=hT[:, 2 * g + cc, :], rhs=Fc[:, cc, :],
                                                                    start=(cc == 0), stop=(cc == 1)),
                         reads=[hTtok, "Fc"], writes=[Ztok])
                eng = "act" if g % 2 == 0 else "dve"
                if eng == "act":
                    P.op("act", lambda e, Z=Z, g=g, zs=zs: e.activation(
                        out=zs[:, :, g * 256:(g + 1) * 256], in_=Z[:].rearrange("p (r c) -> p r c", r=2), func=AF.Identity),
                        reads=[Ztok], writes=[(zstok, g)])
                else:
                    P.op("dve", lambda e, Z=Z, g=g, zs=zs: e.tensor_copy(
                        out=zs[:, :, g * 256:(g + 1) * 256], in_=Z[:].rearrange("p (r c) -> p r c", r=2)),
                        reads=[Ztok], writes=[(zstok, g)])
            P.dma("pool", C.Z_d[:, T * 128:(T + 1) * 128, :].rearrange("r t c -> t r c"), zs[:],
                  reads=[(zstok, g) for g in range(4)], writes=[("Z_d", T)])
        P.barrier()


def phase5(C):
    nc, P, sb, ps = C.nc, C.P, C.sb, C.ps
    NB2 = 8
    with ExitStack() as st:
        M1 = sb("M1", [128, 128], BF16, st=st)
        P.dma("pool", M1[:], C.M1_d, writes=["M1"])
        ztr = Ring(st, nc, "zt", 2, [128, NB2, D], BF16)
        t1r = Ring(st, nc, "t1s", 2, [128, NB2, D], BF16)
        Tr = Ring(st, nc, "T1_ps", 4, [128, 512], F32, psum=True)
        zv = C.Z_d.rearrange("r (n1 n2) c -> (r n1) n2 c", n2=64)
        k = 0
        for blk in range(64 // NB2):
            zt, zttok = ztr.next()
            P.dma("sp", zt[:], zv[:, blk * NB2:(blk + 1) * NB2, :], writes=[zttok])
            t1s, t1tok = t1r.next()
            for i in range(NB2):
                for half in range(2):
                    Tp, Ttok = Tr.next()
                    sl = slice(half * 512, (half + 1) * 512)
                    P.op("pe", lambda e, Tp=Tp, i=i, sl=sl: e.matmul(Tp[:], lhsT=M1[:], rhs=zt[:, i, sl], start=True, stop=True),
                         reads=["M1", zttok], writes=[Ttok])
                    if k % 2 == 0:
                        P.op("act", lambda e, Tp=Tp, i=i, sl=sl: e.activation(out=t1s[:, i, sl], in_=Tp[:], func=AF.Identity),
                             reads=[Ttok], writes=[(t1tok, i, half)])
                    else:
                        P.op("dve", lambda e, Tp=Tp, i=i, sl=sl: e.tensor_copy(out=t1s[:, i, sl], in_=Tp[:]),
                             reads=[Ttok], writes=[(t1tok, i, half)])
                    k += 1
            rd = [(t1tok, i, h) for i in range(NB2) for h in range(2)]
            for r in range(2):
                P.dma("pool", C.T1_d[r, blk * NB2:(blk + 1) * NB2, :, :].rearrange("n2 k1 c -> k1 n2 c"),
                      t1s[r * 64:(r + 1) * 64, :, :], reads=rd, writes=[("T1_d", blk, r)])
        P.barrier()


def phase6(C):
    nc, P, sb, ps = C.nc, C.P, C.sb, C.ps
    NB1 = 8
    with ExitStack() as st:
        M2 = sb("M2", [128, 64, 64], BF16, st=st)
        P.dma("pool", M2[:], C.M2_d, writes=["M2"])
        ttr = Ring(st, nc, "tt", 2, [128, NB1, D], BF16)
        fsr = Ring(st, nc, "fs", 2, [64, NB1, D], BF16)
        Fr = Ring(st, nc, "f_ps", 4, [64, 512], F32, psum=True)
        tv = C.T1_d.rearrange("r n2 k1 c -> (r n2) k1 c")
        fv = C.f_d.rearrange("(k2 k1) c -> k2 k1 c", k1=64)
        k = 0
        for blk in range(64 // NB1):
            tt, tttok = ttr.next()
            P.dma("sp", tt[:], tv[:, blk * NB1:(blk + 1) * NB1, :], writes=[tttok])
            fs, fstok = fsr.next()
            for i in range(NB1):
                k1 = blk * NB1 + i
                for half in range(2):
                    Fp, Ftok = Fr.next()
                    sl = slice(half * 512, (half + 1) * 512)
                    P.op("pe", lambda e, Fp=Fp, i=i, sl=sl, k1=k1: e.matmul(Fp[:], lhsT=M2[:, k1, :], rhs=tt[:, i, sl],
                                                                        start=True, stop=True),
                         reads=["M2", tttok], writes=[Ftok])
                    if k % 2 == 0:
                        P.op("act", lambda e, Fp=Fp, i=i, sl=sl: e.activation(out=fs[:, i, sl], in_=Fp[:], func=AF.Identity),
                             reads=[Ftok], writes=[(fstok, i, half)])
                    else:
                        P.op("dve", lambda e, Fp=Fp, i=i, sl=sl: e.tensor_copy(out=fs[:, i, sl], in_=Fp[:]),
                             reads=[Ftok], writes=[(fstok, i, half)])
                    k += 1
            P.dma("pool", fv[:, blk * NB1:(blk + 1) * NB1, :], fs[:],
                  reads=[(fstok, i, h) for i in range(NB1) for h in range(2)], writes=[("f_d", blk)])
        P.barrier()
    with ExitStack() as st:
        wf = sb("wf", [128, 8, D], BF16, st=st)
        wfv = C.wf_d.rearrange("(kc p) f -> p kc f", p=128)
        for kc in range(8):
            P.dma("pool", wf[:, kc, :], wfv[:, kc, :], writes=[("wf", kc)])
        gate = sb("gate_p6", [128, D], st=st)
        bfr = sb("bf_row", [128, D], st=st)
        P.dma("sp", gate[:], C.gates_d[2].partition_broadcast(128), writes=["gate"])
        P.dma("sp", bfr[:], C.bf_d.partition_broadcast(128), writes=["bfr"])
        ftr = Ring(st, nc, "ft", 3, [128, D], BF16)
        tpr = Ring(st, nc, "p6tp", 2, [128, D], BF16, psum=True)
        fTr = Ring(st, nc, "fT", 2, [128, 8, 128], BF16)
        xtr = Ring(st, nc, "p6xt", 3, [128, D], F32)
        xor_ = Ring(st, nc, "p6xo", 2, [128, D], F32)
        Wr = Ring(st, nc, "wf_ps", 4, [128, 512], F32, psum=True)
        for T in range(NT):
            ft, fttok = ftr.next()
            P.dma("sp", ft[:], C.f_d[T * 128:(T + 1) * 128, :], writes=[fttok])
            xt, xtok = xtr.next()
            P.dma("sp", xt[:], C.x2_d[T * 128:(T + 1) * 128, :], writes=[xtok])
            tp, tptok = tpr.next()
            for kc in range(8):
                P.op("pe", lambda e, kc=kc: e.transpose(out=tp[:, kc * 128:(kc + 1) * 128], in_=ft[:, kc * 128:(kc + 1) * 128],
                                                         identity=C.ident[:]),
                     reads=[fttok, "ident"], writes=[tptok])
            fT, fTtok = fTr.next()
            P.op("act", lambda e: e.activation(out=fT[:].rearrange("p k t -> p (k t)"), in_=tp[:], func=AF.Identity),
                 reads=[tptok], writes=[fTtok])
            xo, xotok = xor_.next()
            for half in range(2):
                Wp, Wtok = Wr.next()
                sl = slice(half * 512, (half + 1) * 512)
                for kc in range(8):
                    P.op("pe", lambda e, Wp=Wp, kc=kc, sl=sl: e.matmul(Wp[:], lhsT=fT[:, kc, :], rhs=wf[:, kc, sl],
                                                                       start=(kc == 0), stop=(kc == 7)),
                         reads=[fTtok, ("wf", kc)], writes=[Wtok])
                P.op("dve", lambda e, Wp=Wp, sl=sl: e.tensor_tensor(out=xo[:, sl], in0=Wp[:], in1=bfr[:, sl], op=ALU.add),
                     reads=[Wtok, "bfr"], writes=[(xotok, half)])
                P.op("pool", lambda e, sl=sl: e.tensor_tensor(out=xo[:, sl], in0=xo[:, sl], in1=gate[:, sl], op=ALU.mult),
                     reads=[(xotok, half), "gate"], writes=[(xotok, half)])
                P.op("pool", lambda e, sl=sl: e.tensor_tensor(out=xo[:, sl], in0=xo[:, sl], in1=xt[:, sl], op=ALU.add),
                     reads=[(xotok, half), xtok], writes=[(xotok, half)])
            P.dma("pool", C.x3_d[T * 128:(T + 1) * 128, :], xo[:], reads=[(xotok, 0), (xotok, 1)], writes=[("x3_d", T)])
        P.barrier()


def _dft_tables():
    c = np.arange(256, dtype=np.float64)
    ang = 2 * np.pi * np.outer(c, c) / 256.0
    Cc, Sc = np.cos(ang) / 16.0, np.sin(ang) / 16.0
    fc = np.concatenate([Cc, -Sc], axis=1)
    fc = fc.reshape(2, 128, 512).transpose(1, 0, 2)
    n = np.arange(64, dtype=np.float64)
    a1 = 2 * np.pi * np.outer(n, n) / 64.0
    C1, S1 = np.cos(a1) / 8.0, np.sin(a1) / 8.0
    m1 = np.block([[C1, -S1], [S1, C1]])
    n2 = n[:, None, None]; k1 = n[None, :, None]; k2 = n[None, None, :]
    th = 2 * np.pi * (n2 * k2 / 64.0 + n2 * k1 / 4096.0)
    m2 = np.concatenate([np.cos(th), np.sin(th)], axis=0) / 8.0
    f32 = lambda a: np.ascontiguousarray(a.astype(np.float32))
    return f32(fc), f32(m1), f32(m2)


def _rope_tables():
    rows = SEQ // 64
    row = np.repeat(np.arange(rows), 64).astype(np.float32)
    col = np.tile(np.arange(64), rows).astype(np.float32)
    inv_freq = (np.float32(10000.0) ** (-np.arange(32, dtype=np.float32) / np.float32(32))).astype(np.float32)
    ang = np.concatenate([row[:, None] * inv_freq, col[:, None] * inv_freq], axis=-1).astype(np.float32)
    return np.cos(ang).astype(np.float32), np.sin(ang).astype(np.float32)


def _host_inputs(inputs):
    f = lambda a: np.ascontiguousarray(np.asarray(a, dtype=np.float32))
    x = f(inputs["x"]); c = f(inputs["c"]); ctx = f(inputs["ctx"]); c_ctx = f(inputs["c_ctx"])
    w_mod = f(inputs["w_mod"]); b_mod = f(inputs["b_mod"])
    pp = lambda v: np.ascontiguousarray(v.reshape(-1, 128).T)
    b_mod_pp = np.stack([pp(b_mod[l]) for l in range(2)])
    g_mix_pp = np.stack([pp(f(inputs["g_mix"])[l]) for l in range(2)])
    g_ffn_pp = np.stack([pp(f(inputs["g_ffn"])[l]) for l in range(2)])
    cos, sin = _rope_tables()
    shared = {
        "w_mod": w_mod, "b_mod_pp": b_mod_pp, "b_mod": b_mod, "g_mix_pp": g_mix_pp, "g_ffn_pp": g_ffn_pp,
        "w_qkv": f(inputs["w_qkv"])[0], "g_q": f(inputs["g_q"])[0], "g_k": f(inputs["g_k"])[0],
        "w_o": f(inputs["w_attn_out"])[0], "rope_cos": cos, "rope_sin": sin,
        "ident": np.eye(128, dtype=np.float32),
        "w_gate_up": f(inputs["w_gate_up"]), "w_down": f(inputs["w_down"]),
        "w_fourier": f(inputs["w_fourier"])[0], "b_fourier": f(inputs["b_fourier"])[0],
        "g_final": f(inputs["g_final"]),
    }
    shared["dft_fc"], shared["dft_m1"], shared["dft_m2"] = _dft_tables()
    maps = []
    for b in range(NCORES):
        c_pp = np.ascontiguousarray(np.stack([pp(c[b]), pp(c_ctx)], axis=-1))
        m = {"x": x[b], "ctx": ctx[b], "c_pp": c_pp}
        m.update(shared)
        maps.append(m)
    return maps


def kernel(**inputs):
    nc = build_program()
    maps = _host_inputs(inputs)
    res = run_bass_kernel_spmd(nc, maps, core_ids=list(range(NCORES)))
    return np.stack([np.asarray(r["out"], dtype=np.float32) for r in res.results], axis=0)
```

```python
import math
from contextlib import ExitStack

import numpy as np
import concourse.bass as bass
import concourse.mybir as mybir
from concourse.bass_utils import run_bass_kernel_spmd

F32 = mybir.dt.float32
BF16 = mybir.dt.bfloat16
AF = mybir.ActivationFunctionType
ALU = mybir.AluOpType
AX = mybir.AxisListType

D = 1024
SEQ = 4096
CTX = 256
NH = 8
NKV = 2
HD = 128
DFF = 2816
NFC = DFF // 128
NT = SEQ // 128
NTC = CTX // 128
NKT = NT + NTC
EPS = 1e-6
NCORES = 8


class Prog:
    def __init__(self, nc, stack):
        self.nc = nc
        self.eng = {"pe": nc.tensor, "act": nc.scalar, "dve": nc.vector, "pool": nc.gpsimd, "sp": nc.sync}
        self.semh = {}
        self.cnt = {}
        for e in self.eng:
            self.semh[e] = stack.enter_context(nc.semaphore("sem_" + e))
            self.cnt[e] = 0
        self.dq = {}
        for q, n in (("sp", 12), ("pool", 8), ("act", 4)):
            keys = []
            for i in range(n):
                k = "dma_%s_%d" % (q, i)
                self.semh[k] = stack.enter_context(nc.semaphore(k))
                self.cnt[k] = 0
                keys.append(k)
            self.dq[q] = {"keys": keys, "i": 0}
        self.known = {e: {} for e in self.eng}
        self.tok = {}
        self.ninst = 0

    def _need(self, e, reads, writes):
        need = {}

        def add(ev, same_ok):
            if ev is None:
                return
            sk, v = ev
            if sk == e and same_ok:
                return
            if need.get(sk, 0) < v:
                need[sk] = v

        for t in reads:
            st = self.tok.get(t)
            if st is not None:
                add(st["w"], False)
        for t in writes:
            st = self.tok.get(t)
            if st is not None:
                add(st["w"], True)
                for sk, v in st["r"].items():
                    add((sk, v), True)
        return need

    def _wait(self, e, need):
        eng = self.eng[e]
        kn = self.known[e]
        for sk, v in need.items():
            if kn.get(sk, 0) < v:
                eng.wait_ge(self.semh[sk], v)
                kn[sk] = v
                self.ninst += 1

    def _record(self, ev, reads, writes):
        for t in reads:
            st = self.tok.setdefault(t, {"w": None, "r": {}})
            if st["r"].get(ev[0], 0) < ev[1]:
                st["r"][ev[0]] = ev[1]
        for t in writes:
            self.tok[t] = {"w": ev, "r": {}}

    def op(self, e, fn, reads=(), writes=()):
        self._wait(e, self._need(e, reads, writes))
        inst = fn(self.eng[e])
        self.cnt[e] += 1
        inst.then_inc(self.semh[e], 1)
        self.ninst += 1
        self._record((e, self.cnt[e]), reads, writes)

    def dma(self, q, out, in_, reads=(), writes=(), **kw):
        dq = self.dq[q]
        k = dq["keys"][dq["i"] % len(dq["keys"])]
        dq["i"] += 1
        need = self._need(q, reads, writes)
        if self.cnt[k] > 0 and need.get(k, 0) < self.cnt[k]:
            need[k] = self.cnt[k]
        self._wait(q, need)
        inst = self.eng[q].dma_start(out=out, in_=in_, **kw)
        self.cnt[k] += 16
        inst.then_inc(self.semh[k], 16)
        self.ninst += 1
        self._record((k, self.cnt[k]), reads, writes)

    def barrier(self):
        for e in self.eng:
            need = {sk: v for sk, v in self.cnt.items() if v > 0}
            self._wait(e, need)
        self.tok = {}


class Ring:
    uid = 0

    def __init__(self, stack, nc, name, n, shape, dtype, psum=False):
        self.name = name
        self.n = n
        self.i = -1
        alloc = nc.psum_tensor if psum else nc.sbuf_tensor
        Ring.uid += 1
        self.tiles = [stack.enter_context(alloc("r%d_%s%d" % (Ring.uid, name, i), shape, dtype)) for i in range(n)]

    def next(self):
        self.i += 1
        s = self.i % self.n
        return self.tiles[s], (self.name, s)


class NS:
    pass


SCALE = float(HD) ** -0.5


def build_program(stop_after=None):
    nc = bass.Bass("TRN2", target_bir_lowering=False)
    C = NS()
    C.nc = nc
    C.stop_after = stop_after
    din = lambda name, shape, dt=F32: nc.dram_tensor(name, shape, dt, kind="ExternalInput").ap()
    dscr = lambda name, shape, dt=F32: nc.dram_tensor(name, shape, dt, kind="Internal").ap()
    C.x_d = din("x", [SEQ, D])
    C.ctx_d = din("ctx", [CTX, D])
    C.cpp_d = din("c_pp", [128, 8, 2])
    C.wmod_d = din("w_mod", [2, D, 6 * D])
    C.bmodpp_d = din("b_mod_pp", [2, 128, 48])
    C.bmod_d = din("b_mod", [2, 6 * D])
    C.gmixpp_d = din("g_mix_pp", [2, 128, 8])
    C.gffnpp_d = din("g_ffn_pp", [2, 128, 8])
    C.wqkv_d = din("w_qkv", [D, 1536])
    C.gq_d = din("g_q", [128])
    C.gk_d = din("g_k", [128])
    C.wo_d = din("w_o", [D, D])
    C.cos_d = din("rope_cos", [SEQ, 64])
    C.sin_d = din("rope_sin", [SEQ, 64])
    C.ident_d = din("ident", [128, 128])
    C.out_d = nc.dram_tensor("out", [SEQ, D], F32, kind="ExternalOutput").ap()
    C.QT_d = dscr("QT_scr", [NT, 128, 1024], BF16)
    C.gates_d = dscr("gates_scr", [4, 1024])
    C.wgu_d = din("w_gate_up", [2, D, 2 * DFF])
    C.wd_d = din("w_down", [2, DFF, D])
    C.wf_d = din("w_fourier", [D, D])
    C.bf_d = din("b_fourier", [D])
    C.gfin_d = din("g_final", [D])
    C.Fc_d = din("dft_fc", [128, 2, 512])
    C.M1_d = din("dft_m1", [128, 128])
    C.M2_d = din("dft_m2", [128, 64, 64])
    C.Z_d = dscr("Z_scr", [2, SEQ, D], BF16)
    C.T1_d = dscr("T1_scr", [2, 64, 64, D], BF16)
    C.f_d = dscr("f_scr", [SEQ, D], BF16)
    names = ["x1", "x2", "x3"]
    for i, nm in enumerate(names):
        setattr(C, nm + "_d", C.out_d if stop_after == "p%d" % (i + 2) and False else dscr(nm + "_scr", [SEQ, D]))
    if stop_after == "p2":
        C.x1_d = C.out_d
    if stop_after == "p3":
        C.x2_d = C.out_d
    if stop_after == "p6":
        C.x3_d = C.out_d

    with ExitStack() as gs:
        P = Prog(nc, gs)
        C.P = P
        C.gs = gs
        uid = [0]

        def _alloc(fn, pre, name, shape, dt, st):
            uid[0] += 1
            return st.enter_context(fn("%s%d_%s" % (pre, uid[0], name), shape, dt))

        C.sb = lambda name, shape, dt=F32, st=gs: _alloc(nc.sbuf_tensor, "sb", name, shape, dt, st)
        C.ps = lambda name, shape, dt=F32, st=gs: _alloc(nc.psum_tensor, "ps", name, shape, dt, st)
        sb = C.sb
        C.modpp = sb("modpp", [128, 2, 4, 8, 2])
        C.gmix = sb("gmix", [128, 2, 8])
        C.gffn = sb("gffn", [128, 2, 8])
        C.Amod = sb("Amod", [128, 5, 8])
        C.Bmod = sb("Bmod", [128, 5, 8])
        C.ident_f = sb("ident_f", [128, 128])
        C.ident = sb("ident", [128, 128], BF16)
        C.ones = sb("ones", [128, 128], BF16)
        P.dma("sp", C.ident_f[:], C.ident_d, writes=["ident_f"])
        P.op("dve", lambda e: e.tensor_copy(out=C.ident[:], in_=C.ident_f[:]), reads=["ident_f"], writes=["ident"])
        P.op("dve", lambda e: e.memset(C.ones[:], 1.0), writes=["ones"])
        C.eps = sb("eps", [128, 1])
        P.op("dve", lambda e: e.memset(C.eps[:], EPS), writes=["eps"])

        phase0(C)
        if stop_after == "p0":
            return nc
        with ExitStack() as st12:
            C.KT = sb("KT", [128, NKV, NKT * 128], BF16, st=st12)
            C.Vs = sb("Vs", [128, NKT, NKV * HD], BF16, st=st12)
            phase1(C)
            phase2(C)
        if stop_after == "p2":
            return nc
        ffn_phase(C, 0, C.x1_d, C.x2_d)
        if stop_after == "p3":
            return nc
        phase4(C)
        phase5(C)
        phase6(C)
        if stop_after == "p6":
            return nc
        ffn_phase(C, 1, C.x3_d, C.out_d, final=True)
        print("instructions:", P.ninst)
    return nc


def phase0(C):
    nc, P, sb, ps = C.nc, C.P, C.sb, C.ps
    modpp, gmix, gffn, Amod, Bmod = C.modpp, C.gmix, C.gffn, C.Amod, C.Bmod
    with ExitStack() as st:
        gates = sb("gates", [128, 4, 1024], st=st)
        cpp = sb("cpp", [128, 8, 2], st=st)
        sc = sb("sc", [128, 8, 2], st=st)
        sig = sb("sig", [128, 8, 2], st=st)
        scb = sb("scb", [128, 8, 128], st=st)
        bpp = sb("bpp", [128, 2, 48], st=st)
        brow = sb("brow", [128, 4, 1024], st=st)
        wring = Ring(st, nc, "wm", 2, [128, 8, 1024], F32)
        pp_ps = ps("pp_ps", [128, 512], st=st)
        row_ps = Ring(st, nc, "row_ps", 2, [128, 512], F32, psum=True)

        P.dma("sp", cpp[:], C.cpp_d, writes=["cpp"])
        P.dma("sp", bpp[:], C.bmodpp_d.rearrange("l p f -> p l f"), writes=["bpp"])
        P.dma("sp", gmix[:], C.gmixpp_d.rearrange("l p f -> p l f"), writes=["gmix"])
        P.dma("sp", gffn[:], C.gffnpp_d.rearrange("l p f -> p l f"), writes=["gffn"])
        for l in range(2):
            for gi, m in enumerate((2, 5)):
                P.dma("sp", brow[:, l * 2 + gi, :],
                      C.bmod_d[l, m * 1024:(m + 1) * 1024].partition_broadcast(128),
                      writes=[("brow", l * 2 + gi)])
        P.op("act", lambda e: e.activation(out=sig[:], in_=cpp[:], func=AF.Sigmoid), reads=["cpp"], writes=["sig"])
        P.op("dve", lambda e: e.tensor_tensor(out=sc[:], in0=cpp[:], in1=sig[:], op=ALU.mult),
             reads=["cpp", "sig"], writes=["sc"])
        P.op("dve", lambda e: e.tensor_copy(out=scb[:], in_=sc[:, :, 0:1].to_broadcast([128, 8, 128])),
             reads=["sc"], writes=["scb"])
        wv = C.wmod_d.rearrange("l (kc p) f -> l p kc f", p=128)
        for l in range(2):
            for m in range(6):
                wt, wtok = wring.next()
                P.dma("sp", wt[:], wv[l, :, :, m * 1024:(m + 1) * 1024], writes=[wtok])
                if m in (2, 5):
                    gi = l * 2 + (0 if m == 2 else 1)
                    for h in range(2):
                        rp, rtok = row_ps.next()
                        for kc in range(8):
                            P.op("pe", lambda e, kc=kc, rp=rp, wt=wt, h=h: e.matmul(
                                rp[:], lhsT=scb[:, kc, :], rhs=wt[:, kc, h * 512:(h + 1) * 512],
                                start=(kc == 0), stop=(kc == 7)),
                                reads=["scb", wtok], writes=[rtok])
                        P.op("dve", lambda e, rp=rp, gi=gi, h=h: e.tensor_tensor(
                            out=gates[:, gi, h * 512:(h + 1) * 512], in0=rp[:],
                            in1=brow[:, gi, h * 512:(h + 1) * 512], op=ALU.add),
                            reads=[rtok, ("brow", gi)], writes=[("gates", gi, h)])
                        if h == 1:
                            P.dma("sp", C.gates_d[gi:gi + 1, :], gates[0:1, gi, :],
                                  reads=[("gates", gi, 0), ("gates", gi, 1)], writes=[("gates_d", gi)])
                else:
                    mi = {0: 0, 1: 1, 3: 2, 4: 3}[m]
                    for fc in range(8):
                        o = ((l * 4 + mi) * 8 + fc) * 2
                        for kc in range(8):
                            P.op("pe", lambda e, kc=kc, fc=fc, wt=wt, o=o: e.matmul(
                                pp_ps[:, o:o + 2], lhsT=wt[:, kc, fc * 128:(fc + 1) * 128], rhs=sc[:, kc, :],
                                start=(kc == 0), stop=(kc == 7)),
                                reads=["sc", wtok], writes=["pp_ps"])
                    P.op("dve", lambda e, l=l, mi=mi, m=m: e.tensor_tensor(
                        out=modpp[:, l, mi, :, :],
                        in0=pp_ps[:, (l * 4 + mi) * 16:(l * 4 + mi + 1) * 16].rearrange("p (f t) -> p f t", t=2),
                        in1=bpp[:, l, m * 8:(m + 1) * 8].unsqueeze(2).to_broadcast([128, 8, 2]), op=ALU.add),
                        reads=["pp_ps", "bpp"], writes=[("modpp", l, mi)])
        combos = [(0, 0, 1, 0, gmix, 0), (1, 0, 1, 0, gmix, 1), (2, 0, 3, 2, gffn, 0),
                  (3, 1, 1, 0, gmix, 0), (4, 1, 3, 2, gffn, 0)]
        for idx, l, m_sc, m_sh, g, col in combos:
            P.op("dve", lambda e, idx=idx, l=l, m_sc=m_sc, g=g, col=col: e.scalar_tensor_tensor(
                out=Amod[:, idx, :], in0=modpp[:, l, m_sc, :, col], scalar=1.0, in1=g[:, l, :],
                op0=ALU.add, op1=ALU.mult),
                reads=[("modpp", l, m_sc), "gmix", "gffn"], writes=[("Amod", idx)])
            P.op("dve", lambda e, idx=idx, l=l, m_sh=m_sh, col=col: e.tensor_copy(
                out=Bmod[:, idx, :], in_=modpp[:, l, m_sh, :, col]),
                reads=[("modpp", l, m_sh)], writes=[("Bmod", idx)])
        P.barrier()


def rms_stage(C, R, src_ap, a_idx, hT, hTtok, col0=0):
    P = C.P
    xt, xtok = R["xt"].next()
    P.dma("sp", xt[:], src_ap, writes=[xtok])
    rms_from_tile(C, R, xt, xtok, a_idx, hT, hTtok, col0)
    return xt, xtok


def rms_from_tile(C, R, xt, xtok, a_idx, hT, hTtok, col0=0):
    P = C.P
    ss, sstok = R["ss"].next()
    xn, xntok = R["xn"].next()
    tp, tptok = R["tp"].next()
    junk = R["junk"]
    P.op("act", lambda e: e.activation(out=junk[:], in_=xt[:], func=AF.Square, accum_out=ss[:, 0:1]),
         reads=[xtok], writes=["junk", (sstok, 0)])
    P.op("act", lambda e: e.activation(out=ss[:, 1:2], in_=ss[:, 0:1], func=AF.Sqrt, scale=1.0 / D, bias=C.eps[:, 0:1]),
         reads=[(sstok, 0), "eps"], writes=[(sstok, 1)])
    P.op("dve", lambda e: e.reciprocal(out=ss[:, 2:3], in_=ss[:, 1:2]), reads=[(sstok, 1)], writes=[(sstok, 2)])
    P.op("act", lambda e: e.activation(out=xn[:], in_=xt[:], func=AF.Identity, scale=ss[:, 2:3]),
         reads=[xtok, (sstok, 2)], writes=[xntok])
    for kc in range(8):
        P.op("pe", lambda e, kc=kc: e.transpose(out=tp[:, kc * 128:(kc + 1) * 128], in_=xn[:, kc * 128:(kc + 1) * 128],
                                                 identity=C.ident[:]),
             reads=[xntok, "ident"], writes=[tptok])
    for kc in range(8):
        P.op("dve", lambda e, kc=kc: e.tensor_scalar(
            out=hT[:, kc, col0:col0 + 128], in0=tp[:, kc * 128:(kc + 1) * 128],
            scalar1=C.Amod[:, a_idx, kc:kc + 1], scalar2=C.Bmod[:, a_idx, kc:kc + 1], op0=ALU.mult, op1=ALU.add),
            reads=[tptok, ("Amod", a_idx), ("Bmod", a_idx)], writes=[hTtok])


def load_weight_bf16(C, st, name, dst, dst_tok_fn, src_view, nchunks, chunk_shape, engines=("dve", "pool")):
    P, nc = C.P, C.nc
    ring = Ring(st, nc, name + "_stg", 2, chunk_shape, F32)
    for i in range(nchunks):
        t, tok = ring.next()
        P.dma("sp", t[:], src_view(i), writes=[tok])
        eng = engines[i % len(engines)]
        P.op(eng, lambda e, t=t, i=i: e.tensor_copy(out=dst(i), in_=t[:]), reads=[tok], writes=[dst_tok_fn(i)])


def phase1(C):
    nc, P, sb, ps = C.nc, C.P, C.sb, C.ps
    with ExitStack() as st:
        wqkv = sb("wqkv", [128, 8, 1536], BF16, st=st)
        cosb = sb("cosb", [128, NT, 64], BF16, st=st)
        sinb = sb("sinb", [128, NT, 64], BF16, st=st)
        gqk = sb("gqk", [128, 10, 128], st=st)
        P.dma("pool", cosb[:], C.cos_d.rearrange("(t p) f -> p t f", p=128), writes=["cos"])
        P.dma("pool", sinb[:], C.sin_d.rearrange("(t p) f -> p t f", p=128), writes=["sin"])
        for h in range(10):
            P.dma("sp", gqk[:, h, :], (C.gq_d if h < 8 else C.gk_d).partition_broadcast(128), writes=["gqk"])
        wv = C.wqkv_d.rearrange("(kc p) f -> p kc f", p=128)
        for kc in range(8):
            P.dma("pool", wqkv[:, kc, :], wv[:, kc, :], writes=[("wqkv", kc)])
        R = {
            "xt": Ring(st, nc, "xt", 3, [128, D], F32),
            "ss": Ring(st, nc, "ss", 3, [128, 4], F32),
            "xn": Ring(st, nc, "xn", 2, [128, D], BF16),
            "tp": Ring(st, nc, "tp", 2, [128, D], BF16, psum=True),
            "junk": sb("junk", [128, D], BF16, st=st),
        }
        hTr = Ring(st, nc, "hT", 2, [128, 8, 128], BF16)
        qkv_ps = [ps("qkv_ps%d" % i, [128, 512], st=st) for i in range(3)]
        qkvr = Ring(st, nc, "qkv_sb", 3, [128, 1536], F32)
        sqb = sb("sqb", [128, 1280], BF16, st=st)
        ssq = Ring(st, nc, "ssq", 2, [128, 3, 10], F32)
        qgr = Ring(st, nc, "qg", 2, [128, 10, 128], BF16)
        t1 = sb("t1", [128, 10, 64], BF16, st=st)
        t2 = sb("t2", [128, 10, 64], BF16, st=st)
        t3 = sb("t3", [128, 10, 64], BF16, st=st)
        t4 = sb("t4", [128, 10, 64], BF16, st=st)
        qrr = Ring(st, nc, "qr", 2, [128, 10, 128], BF16)
        qT_ps = ps("qT_ps", [128, 1024], BF16, st=st)
        kT_ps = ps("kT_ps", [128, 1024], BF16, st=st)
        qTr = Ring(st, nc, "qT_sb", 2, [128, 1024], BF16)
        info = {}

        def stage1a(T):
            lat = T < NT
            src = C.x_d[T * 128:(T + 1) * 128, :] if lat else C.ctx_d[(T - NT) * 128:(T - NT + 1) * 128, :]
            hT, hTtok = hTr.next()
            rms_stage(C, R, src, 0 if lat else 1, hT, hTtok)
            info[T] = dict(hT=hT, hTtok=hTtok)

        def stage1b(T):
            lat = T < NT
            hT, hTtok = info[T]["hT"], info[T]["hTtok"]
            banks = (0, 1, 2) if lat else (2,)
            for nb in banks:
                for kc in range(8):
                    P.op("pe", lambda e, nb=nb, kc=kc: e.matmul(
                        qkv_ps[nb][:], lhsT=hT[:, kc, :], rhs=wqkv[:, kc, nb * 512:(nb + 1) * 512],
                        start=(kc == 0), stop=(kc == 7)),
                        reads=[hTtok, ("wqkv", kc)], writes=[("qkv_ps", nb)])
            qs, qstok = qkvr.next()
            for nb in banks:
                P.op("act", lambda e, nb=nb: e.activation(out=qs[:, nb * 512:(nb + 1) * 512], in_=qkv_ps[nb][:],
                                                          func=AF.Identity),
                     reads=[("qkv_ps", nb)], writes=[(qstok, nb)])
            info[T].update(qs=qs, qstok=qstok)

        def stage2(T):
            lat = T < NT
            qs, qstok = info[T]["qs"], info[T]["qstok"]
            h0 = 0 if lat else 8
            nh = 10 - h0
            lo = h0 * 128
            rd = [(qstok, nb) for nb in ((0, 1, 2) if lat else (2,))]
            sq3, sqtok = ssq.next()
            P.op("act", lambda e: e.activation(out=sqb[:, lo:1280], in_=qs[:, lo:1280], func=AF.Square),
                 reads=rd, writes=["sqb"])
            P.op("dve", lambda e: e.tensor_reduce(out=sq3[:, 0, h0:10],
                                                  in_=sqb[:, lo:1280].rearrange("p (h d) -> p h d", d=128),
                                                  axis=AX.X, op=ALU.add),
                 reads=["sqb"], writes=[(sqtok, 0)])
            P.op("act", lambda e: e.activation(out=sq3[:, 1, h0:10], in_=sq3[:, 0, h0:10], func=AF.Sqrt,
                                               scale=1.0 / HD, bias=C.eps[:, 0:1]),
                 reads=[(sqtok, 0), "eps"], writes=[(sqtok, 1)])
            P.op("dve", lambda e: e.reciprocal(out=sq3[:, 2, h0:10], in_=sq3[:, 1, h0:10]),
                 reads=[(sqtok, 1)], writes=[(sqtok, 2)])
            qg, qgtok = qgr.next()
            for h in range(h0, 10):
                P.op("dve", lambda e, h=h: e.scalar_tensor_tensor(
                    out=qg[:, h, :], in0=qs[:, h * 128:(h + 1) * 128], scalar=sq3[:, 2, h:h + 1], in1=gqk[:, h, :],
                    op0=ALU.mult, op1=ALU.mult),
                    reads=rd + [(sqtok, 2), "gqk"], writes=[(qgtok, h)])
            if lat:
                qr, qrtok = qrr.next()
                cb = cosb[:, T, :].unsqueeze(1).to_broadcast([128, 10, 64])
                sbb = sinb[:, T, :].unsqueeze(1).to_broadcast([128, 10, 64])
                x1 = qg[:, :, 0:64]
                x2 = qg[:, :, 64:128]
                qall = [(qgtok, h) for h in range(10)]
                P.op("dve", lambda e: e.tensor_tensor(out=t1[:], in0=x1, in1=cb, op=ALU.mult), reads=qall + ["cos"], writes=["t1"])
                P.op("dve", lambda e: e.tensor_tensor(out=t2[:], in0=x2, in1=sbb, op=ALU.mult), reads=qall + ["sin"], writes=["t2"])
                P.op("dve", lambda e: e.tensor_tensor(out=t3[:], in0=x1, in1=sbb, op=ALU.mult), reads=qall + ["sin"], writes=["t3"])
                P.op("dve", lambda e: e.tensor_tensor(out=t4[:], in0=x2, in1=cb, op=ALU.mult), reads=qall + ["cos"], writes=["t4"])
                P.op("dve", lambda e: e.tensor_tensor(out=qr[:, :, 0:64], in0=t1[:], in1=t2[:], op=ALU.subtract),
                     reads=["t1", "t2"], writes=[(qrtok, 0)])
                P.op("dve", lambda e: e.tensor_tensor(out=qr[:, :, 64:128], in0=t3[:], in1=t4[:], op=ALU.add),
                     reads=["t3", "t4"], writes=[(qrtok, 1)])
                src_t, src_rd = qr, [(qrtok, 0), (qrtok, 1)]
            else:
                src_t, src_rd = qg, [(qgtok, 8), (qgtok, 9)]
            if lat:
                for h in range(8):
                    P.op("pe", lambda e, h=h: e.transpose(out=qT_ps[:, h * 128:(h + 1) * 128], in_=src_t[:, h, :],
                                                           identity=C.ident[:]),
                         reads=src_rd + ["ident"], writes=["qT_ps"])
            for j in range(2):
                P.op("pe", lambda e, j=j: e.transpose(out=kT_ps[:, j * 128:(j + 1) * 128], in_=src_t[:, 8 + j, :],
                                                       identity=C.ident[:]),
                     reads=src_rd + ["ident"], writes=["kT_ps"])
            if lat:
                qT, qTtok = qTr.next()
                P.op("act", lambda e: e.activation(out=qT[:], in_=qT_ps[:], func=AF.Identity), reads=["qT_ps"], writes=[qTtok])
                P.dma("pool", C.QT_d[T], qT[:], reads=[qTtok], writes=[("QT_d", T)])
            P.op("dve", lambda e: e.tensor_copy(
                out=C.KT[:, :, T * 128:(T + 1) * 128], in_=kT_ps[:, 0:256].rearrange("p (j t) -> p j t", t=128)),
                reads=["kT_ps"], writes=[("KT", T)])
            P.op("pool", lambda e: e.tensor_copy(out=C.Vs[:, T, :], in_=qs[:, 1280:1536]), reads=[(qstok, 2)],
                 writes=[("V", T)])
            del info[T]

        for T in range(NKT + 2):
            if T < NKT:
                stage1a(T)
            if 0 <= T - 1 < NKT:
                stage1b(T - 1)
            if 0 <= T - 2 < NKT:
                stage2(T - 2)
        P.barrier()


def phase2(C):
    nc, P, sb, ps = C.nc, C.P, C.sb, C.ps
    NKP = NKT // 2
    with ExitStack() as st:
        wo = sb("wo", [128, 8, 1024], BF16, st=st)
        wv = C.wo_d.rearrange("(h p) f -> p h f", p=128)
        for h in range(8):
            P.dma("pool", wo[:, h, :], wv[:, h, :], writes=[("wo", h)])
        qTr = Ring(st, nc, "qT2", 2, [128, 1024], BF16)
        xtr = Ring(st, nc, "xt2", 3, [128, D], F32)
        PTr = Ring(st, nc, "PT", 10, [128, 1024], BF16)
        saccr = Ring(st, nc, "sacc", 2, [128, 2, 1024], BF16)
        Sr = Ring(st, nc, "S_ps", 2, [128, 1024], F32, psum=True)
        accr = Ring(st, nc, "acc_ps", 2, [128, 512], F32, psum=True)
        den_ps = ps("den_ps", [128, 512], st=st)
        wo_ps = ps("wo_ps", [128, 512], st=st)
        recr = Ring(st, nc, "rec", 2, [128, 512], F32)
        OTr = Ring(st, nc, "OT", 2, [128, 8, 128], BF16)
        x1r = Ring(st, nc, "x1t", 2, [128, D], F32)
        gate = sb("gate_p2", [128, D], st=st)
        P.dma("sp", gate[:], C.gates_d[0].partition_broadcast(128), writes=["gate"])

        steps = [(T, kvh, p) for T in range(NT) for kvh in range(NKV) for p in range(NKP)]
        state = {}
        pending = []

        def load_tile(T):
            qT, qTtok = qTr.next()
            xt, xtok = xtr.next()
            P.dma("sp", qT[:], C.QT_d[T], writes=[qTtok])
            P.dma("sp", xt[:], C.x_d[T * 128:(T + 1) * 128, :], writes=[xtok])
            OT, OTtok = OTr.next()
            state[T] = dict(qT=qT, qTtok=qTtok, xt=xt, xtok=xtok, OT=OT, OTtok=OTtok)

        def emit_S(i):
            T, kvh, p = steps[i]
            if kvh == 0 and p == 0:
                if T not in state:
                    load_tile(T)
                if T + 1 < NT and (T + 1) not in state:
                    load_tile(T + 1)
            s = state[T]
            S, Stok = Sr.next()
            for u in range(2):
                kt = 2 * p + u
                P.op("pe", lambda e, kt=kt, u=u: e.matmul(
                    S[:, u * 512:(u + 1) * 512], lhsT=C.KT[:, kvh, kt * 128:(kt + 1) * 128],
                    rhs=s["qT"][:, kvh * 512:(kvh + 1) * 512], start=True, stop=True),
                    reads=[("KT", kt), s["qTtok"]], writes=[(Stok, u)])
            PT, PTtok = PTr.next()
            P.op("act", lambda e: e.activation(out=PT[:], in_=S[:], func=AF.Exp, scale=SCALE),
                 reads=[(Stok, 0), (Stok, 1)], writes=[PTtok])
            return PT, PTtok

        def emit_PV(i, PT, PTtok):
            T, kvh, p = steps[i]
            s = state[T]
            if p == 0:
                s["acc"], s["acctok"] = accr.next()
                s["sacc"], s["sacctok"] = saccr.next()
            acc, acctok, sacc, sacctok = s["acc"], s["acctok"], s["sacc"], s["sacctok"]
            for u in range(2):
                kt = 2 * p + u
                P.op("pe", lambda e, kt=kt, u=u: e.matmul(
                    acc[:], lhsT=C.Vs[:, kt, kvh * 128:(kvh + 1) * 128], rhs=PT[:, u * 512:(u + 1) * 512],
                    start=(kt == 0), stop=(kt == NKT - 1)),
                    reads=[("V", kt), PTtok], writes=[acctok])
            a = p % 2
            if p < 2:
                P.op("dve", lambda e: e.tensor_copy(out=sacc[:, a, :], in_=PT[:]), reads=[PTtok], writes=[(sacctok, a)])
            else:
                P.op("dve", lambda e: e.tensor_tensor(out=sacc[:, a, :], in0=sacc[:, a, :], in1=PT[:], op=ALU.add),
                     reads=[PTtok, (sacctok, a)], writes=[(sacctok, a)])
            if p == NKP - 1:
                OT, OTtok = s["OT"], s["OTtok"]

                def fin(T=T, kvh=kvh, acc=acc, acctok=acctok, sacc=sacc, sacctok=sacctok, OT=OT, OTtok=OTtok):
                    for u in range(4):
                        P.op("pe", lambda e, u=u: e.matmul(den_ps[:], lhsT=C.ones[:],
                                                           rhs=sacc[:, u // 2, (u % 2) * 512:(u % 2 + 1) * 512],
                                                           start=(u == 0), stop=(u == 3)),
                             reads=["ones", (sacctok, 0), (sacctok, 1)], writes=["den_ps"])
                    rec, rectok = recr.next()
                    P.op("dve", lambda e: e.reciprocal(out=rec[:], in_=den_ps[:]), reads=["den_ps"], writes=[rectok])
                    P.op("dve", lambda e: e.tensor_tensor(
                        out=OT[:, kvh * 4:(kvh + 1) * 4, :].rearrange("p h q -> p (h q)"), in0=acc[:], in1=rec[:], op=ALU.mult),
                        reads=[acctok, rectok], writes=[(OTtok, kvh)])
                    if kvh == NKV - 1:
                        pending.append([5, lambda T=T: emit_wo(T)])

                pending.append([2, fin])

        def emit_wo(T):
            s = state[T]
            OT, OTtok = s["OT"], s["OTtok"]
            x1t, x1tok = x1r.next()

            def chunk(half, c):
                sl = slice(half * 512, (half + 1) * 512)
                for h in (2 * c, 2 * c + 1):
                    P.op("pe", lambda e, h=h: e.matmul(
                        wo_ps[:], lhsT=OT[:, h, :], rhs=wo[:, h, half * 512:(half + 1) * 512],
                        start=(h == 0), stop=(h == 7)),
                        reads=[(OTtok, h // 4), ("wo", h)], writes=["wo_ps"])
                if c == 3:
                    P.op("dve", lambda e: e.tensor_tensor(out=x1t[:, sl], in0=wo_ps[:], in1=gate[:, sl], op=ALU.mult),
                         reads=["wo_ps", "gate"], writes=[(x1tok, half)])
                    P.op("pool", lambda e: e.tensor_tensor(out=x1t[:, sl], in0=x1t[:, sl], in1=s["xt"][:, sl], op=ALU.add),
                         reads=[(x1tok, half), s["xtok"]], writes=[(x1tok, half)])
                    if half == 1:
                        P.dma("pool", C.x1_d[T * 128:(T + 1) * 128, :], x1t[:], reads=[(x1tok, 0), (x1tok, 1)],
                              writes=[("x1_d", T)])
                        del state[T]
                        return
                nh, ncn = (half, c + 1) if c < 3 else (half + 1, 0)
                pending.append([1, lambda: chunk(nh, ncn)])

            chunk(0, 0)

        def tick():
            for p in list(pending):
                p[0] -= 1
                if p[0] <= 0:
                    pending.remove(p)
                    p[1]()

        LA = 2
        q = [emit_S(i) for i in range(LA)]
        for i in range(len(steps)):
            if i + LA < len(steps):
                q.append(emit_S(i + LA))
            emit_PV(i, *q.pop(0))
            tick()
        while pending:
            tick()
        P.barrier()


def ffn_phase(C, layer, src_d, dst_d, final=False):
    nc, P, sb, ps = C.nc, C.P, C.sb, C.ps
    a_idx = 2 if layer == 0 else 4
    gi = layer * 2 + 1
    with ExitStack() as st:
        wgu = sb("wgu", [128, 8, 2 * DFF], BF16, st=st)
        wd = sb("wd", [128, NFC, D], BF16, st=st)
        wguv = C.wgu_d[layer].rearrange("(kc p) f -> p kc f", p=128)
        wdv = C.wd_d[layer].rearrange("(fc p) d -> p fc d", p=128)
        for kc in range(8):
            for hh in range(2):
                P.dma("pool", wgu[:, kc, hh * DFF:(hh + 1) * DFF], wguv[:, kc, hh * DFF:(hh + 1) * DFF],
                      writes=[("wgu", kc, hh)])
        for fc in range(NFC):
            P.dma("pool", wd[:, fc, :], wdv[:, fc, :], writes=[("wd", fc)])
        gate = sb("gate_ffn", [128, D], st=st)
        P.dma("sp", gate[:], C.gates_d[gi].partition_broadcast(128), reads=[("gates_d", gi)], writes=["gate"])
        if final:
            gfin = sb("gfin", [128, D], st=st)
            P.dma("sp", gfin[:], C.gfin_d.partition_broadcast(128), writes=["gfin"])
        R = {
            "xt": Ring(st, nc, "fxt", 2, [128, D], F32),
            "ss": Ring(st, nc, "fss", 3, [128, 4], F32),
            "xn": Ring(st, nc, "fxn", 2, [128, D], BF16),
            "tp": Ring(st, nc, "ftp", 2, [128, D], BF16, psum=True),
            "junk": sb("fjunk", [128, D], BF16, st=st),
        }
        hT2 = sb("hT2", [128, 8, 512], BF16, st=st)
        aT = sb("aT", [128, NFC, 512], BF16, st=st)
        Gr = Ring(st, nc, "G_ps", 2, [128, 512], F32, psum=True)
        Ur = Ring(st, nc, "U_ps", 2, [128, 512], F32, psum=True)
        Dr = Ring(st, nc, "D_ps", 2, [128, 512], F32, psum=True)
        sgr = Ring(st, nc, "sg", 2, [128, 512], F32)
        xrr = Ring(st, nc, "xres", 2, [128, D], F32)
        xor_ = Ring(st, nc, "xo", 2, [128, D], F32)
        fss = Ring(st, nc, "finss", 2, [128, 4], F32)
        print("ffn sbuf remaining", nc.sbuf_bytes_remaining)
        NB = NT // 4

        def emit_rms(bk):
            for j in range(4):
                T = bk * 4 + j
                rms_stage(C, R, src_d[T * 128:(T + 1) * 128, :], a_idx, hT2, ("hT2", j), col0=j * 128)

        def emit_gu(bk):
            for fc in range(NFC):
                G, Gtok = Gr.next()
                U, Utok = Ur.next()
                for (ps_t, ps_tok, c0) in ((G, Gtok, fc * 128), (U, Utok, DFF + fc * 128)):
                    hh = 0 if c0 < DFF else 1
                    for kc in range(8):
                        P.op("pe", lambda e, ps_t=ps_t, c0=c0, kc=kc: e.matmul(
                            ps_t[:], lhsT=wgu[:, kc, c0:c0 + 128], rhs=hT2[:, kc, :], start=(kc == 0), stop=(kc == 7)),
                            reads=[("wgu", kc, hh)] + [("hT2", j) for j in range(4)], writes=[ps_tok])
                sg, sgtok = sgr.next()
                P.op("act", lambda e, sg=sg, G=G: e.activation(out=sg[:], in_=G[:], func=AF.Silu), reads=[Gtok], writes=[sgtok])
                P.op("dve", lambda e, sg=sg, U=U, fc=fc: e.tensor_tensor(out=aT[:, fc, :], in0=U[:], in1=sg[:], op=ALU.mult),
                     reads=[Utok, sgtok], writes=[("aT", fc)])

        def emit_down(bk):
            for j in range(4):
                T = bk * 4 + j
                xr, xrtok = xrr.next()
                P.dma("sp", xr[:], src_d[T * 128:(T + 1) * 128, :], writes=[xrtok])
                xo, xotok = xor_.next()
                for half in range(2):
                    Dp, Dtok = Dr.next()
                    sl = slice(half * 512, (half + 1) * 512)
                    for fc in range(NFC):
                        P.op("pe", lambda e, Dp=Dp, fc=fc, sl=sl, j=j: e.matmul(
                            Dp[:], lhsT=aT[:, fc, j * 128:(j + 1) * 128], rhs=wd[:, fc, sl],
                            start=(fc == 0), stop=(fc == NFC - 1)),
                            reads=[("aT", fc), ("wd", fc)], writes=[Dtok])
                    P.op("dve", lambda e, Dp=Dp, sl=sl, xo=xo: e.tensor_tensor(out=xo[:, sl], in0=Dp[:], in1=gate[:, sl], op=ALU.mult),
                         reads=[Dtok, "gate"], writes=[(xotok, half)])
                    P.op("pool", lambda e, sl=sl, xo=xo, xr=xr: e.tensor_tensor(out=xo[:, sl], in0=xo[:, sl], in1=xr[:, sl], op=ALU.add),
                         reads=[(xotok, half), xrtok], writes=[(xotok, half)])
                if final:
                    fs, fstok = fss.next()
                    P.op("act", lambda e, xo=xo, fs=fs: e.activation(out=R["junk"][:], in_=xo[:], func=AF.Square, accum_out=fs[:, 0:1]),
                         reads=[(xotok, 0), (xotok, 1)], writes=["junk", (fstok, 0)])
                    P.op("act", lambda e, fs=fs: e.activation(out=fs[:, 1:2], in_=fs[:, 0:1], func=AF.Sqrt, scale=1.0 / D, bias=C.eps[:, 0:1]),
                         reads=[(fstok, 0), "eps"], writes=[(fstok, 1)])
                    P.op("dve", lambda e, fs=fs: e.reciprocal(out=fs[:, 2:3], in_=fs[:, 1:2]), reads=[(fstok, 1)], writes=[(fstok, 2)])
                    P.op("dve", lambda e, xo=xo, fs=fs: e.scalar_tensor_tensor(
                        out=xo[:], in0=xo[:], scalar=fs[:, 2:3], in1=gfin[:], op0=ALU.mult, op1=ALU.mult),
                        reads=[(xotok, 0), (xotok, 1), (fstok, 2), "gfin"], writes=[(xotok, 0), (xotok, 1)])
                P.dma("pool", dst_d[T * 128:(T + 1) * 128, :], xo[:], reads=[(xotok, 0), (xotok, 1)], writes=[("dst", T)])

        emit_rms(0)
        for bk in range(NB):
            emit_gu(bk)
            if bk + 1 < NB:
                emit_rms(bk + 1)
            emit_down(bk)
        P.barrier()


def phase4(C):
    nc, P, sb, ps = C.nc, C.P, C.sb, C.ps
    with ExitStack() as st:
        Fc = sb("Fc", [128, 2, 512], BF16, st=st)
        P.dma("pool", Fc[:], C.Fc_d, writes=["Fc"])
        R = {
            "xt": Ring(st, nc, "p4xt", 3, [128, D], F32),
            "ss": Ring(st, nc, "p4ss", 3, [128, 4], F32),
            "xn": Ring(st, nc, "p4xn", 2, [128, D], BF16),
            "tp": Ring(st, nc, "p4tp", 2, [128, D], BF16, psum=True),
            "junk": sb("p4junk", [128, D], BF16, st=st),
        }
        hTr = Ring(st, nc, "p4hT", 2, [128, 8, 128], BF16)
        Zr = Ring(st, nc, "Z_ps", 4, [128, 512], F32, psum=True)
        zsr = Ring(st, nc, "zs", 3, [128, 2, D], BF16)
        for T in range(NT):
            hT, hTtok = hTr.next()
            rms_stage(C, R, C.x2_d[T * 128:(T + 1) * 128, :], 3, hT, hTtok)
            zs, zstok = zsr.next()
            for g in range(4):
                Z, Ztok = Zr.next()
                for cc in range(2):
                    P.op("pe", lambda e, Z=Z, g=g, cc=cc: e.matmul(Z[:], lhsT=hT[:, 2 * g + cc, :], rhs=Fc[:, cc, :],
                                                                    start=(cc == 0), stop=(cc == 1)),
                         reads=[hTtok, "Fc"], writes=[Ztok])
                eng = "act" if g % 2 == 0 else "dve"
                if eng == "act":
                    P.op("act", lambda e, Z=Z, g=g, zs=zs: e.activation(
                        out=zs[:, :, g * 256:(g + 1) * 256], in_=Z[:].rearrange("p (r c) -> p r c", r=2), func=AF.Identity),
                        reads=[Ztok], writes=[(zstok, g)])
                else:
                    P.op("dve", lambda e, Z=Z, g=g, zs=zs: e.tensor_copy(
                        out=zs[:, :, g * 256:(g + 1) * 256], in_=Z[:].rearrange("p (r c) -> p r c", r=2)),
                        reads=[Ztok], writes=[(zstok, g)])
            P.dma("pool", C.Z_d[:, T * 128:(T + 1) * 128, :].rearrange("r t c -> t r c"), zs[:],
                  reads=[(zstok, g) for g in range(4)], writes=[("Z_d", T)])
        P.barrier()


def phase5(C):
    nc, P, sb, ps = C.nc, C.P, C.sb, C.ps
    NB2 = 8
    with ExitStack() as st:
        M1 = sb("M1", [128, 128], BF16, st=st)
        P.dma("pool", M1[:], C.M1_d, writes=["M1"])
        ztr = Ring(st, nc, "zt", 2, [128, NB2, D], BF16)
        t1r = Ring(st, nc, "t1s", 2, [128, NB2, D], BF16)
        Tr = Ring(st, nc, "T1_ps", 4, [128, 512], F32, psum=True)
        zv = C.Z_d.rearrange("r (n1 n2) c -> (r n1) n2 c", n2=64)
        k = 0
        for blk in range(64 // NB2):
            zt, zttok = ztr.next()
            P.dma("sp", zt[:], zv[:, blk * NB2:(blk + 1) * NB2, :], writes=[zttok])
            t1s, t1tok = t1r.next()
            for i in range(NB2):
                for half in range(2):
                    Tp, Ttok = Tr.next()
                    sl = slice(half * 512, (half + 1) * 512)
                    P.op("pe", lambda e, Tp=Tp, i=i, sl=sl: e.matmul(Tp[:], lhsT=M1[:], rhs=zt[:, i, sl], start=True, stop=True),
                         reads=["M1", zttok], writes=[Ttok])
                    if k % 2 == 0:
                        P.op("act", lambda e, Tp=Tp, i=i, sl=sl: e.activation(out=t1s[:, i, sl], in_=Tp[:], func=AF.Identity),
                             reads=[Ttok], writes=[(t1tok, i, half)])
                    else:
                        P.op("dve", lambda e, Tp=Tp, i=i, sl=sl: e.tensor_copy(out=t1s[:, i, sl], in_=Tp[:]),
                             reads=[Ttok], writes=[(t1tok, i, half)])
                    k += 1
            rd = [(t1tok, i, h) for i in range(NB2) for h in range(2)]
            for r in range(2):
                P.dma("pool", C.T1_d[r, blk * NB2:(blk + 1) * NB2, :, :].rearrange("n2 k1 c -> k1 n2 c"),
                      t1s[r * 64:(r + 1) * 64, :, :], reads=rd, writes=[("T1_d", blk, r)])
        P.barrier()


def phase6(C):
    nc, P, sb, ps = C.nc, C.P, C.sb, C.ps
    NB1 = 8
    with ExitStack() as st:
        M2 = sb("M2", [128, 64, 64], BF16, st=st)
        P.dma("pool", M2[:], C.M2_d, writes=["M2"])
        ttr = Ring(st, nc, "tt", 2, [128, NB1, D], BF16)
        fsr = Ring(st, nc, "fs", 2, [64, NB1, D], BF16)
        Fr = Ring(st, nc, "f_ps", 4, [64, 512], F32, psum=True)
        tv = C.T1_d.rearrange("r n2 k1 c -> (r n2) k1 c")
        fv = C.f_d.rearrange("(k2 k1) c -> k2 k1 c", k1=64)
        k = 0
        for blk in range(64 // NB1):
            tt, tttok = ttr.next()
            P.dma("sp", tt[:], tv[:, blk * NB1:(blk + 1) * NB1, :], writes=[tttok])
            fs, fstok = fsr.next()
            for i in range(NB1):
                k1 = blk * NB1 + i
                for half in range(2):
                    Fp, Ftok = Fr.next()
                    sl = slice(half * 512, (half + 1) * 512)
                    P.op("pe", lambda e, Fp=Fp, i=i, sl=sl, k1=k1: e.matmul(Fp[:], lhsT=M2[:, k1, :], rhs=tt[:, i, sl],
                                                                        start=True, stop=True),
                         reads=["M2", tttok], writes=[Ftok])
                    if k % 2 == 0:
                        P.op("act", lambda e, Fp=Fp, i=i, sl=sl: e.activation(out=fs[:, i, sl], in_=Fp[:], func=AF.Identity),
                             reads=[Ftok], writes=[(fstok, i, half)])
                    else:
                        P.op("dve", lambda e, Fp=Fp, i=i, sl=sl: e.tensor_copy(out=fs[:, i, sl], in_=Fp[:]),
                             reads=[Ftok], writes=[(fstok, i, half)])
                    k += 1
            P.dma("pool", fv[:, blk * NB1:(blk + 1) * NB1, :], fs[:],
                  reads=[(fstok, i, h) for i in range(NB1) for h in range(2)], writes=[("f_d", blk)])
        P.barrier()
    with ExitStack() as st:
        wf = sb("wf", [128, 8, D], BF16, st=st)
        wfv = C.wf_d.rearrange("(kc p) f -> p kc f", p=128)
        for kc in range(8):
            P.dma("pool", wf[:, kc, :], wfv[:, kc, :], writes=[("wf", kc)])
        gate = sb("gate_p6", [128, D], st=st)
        bfr = sb("bf_row", [128, D], st=st)
        P.dma("sp", gate[:], C.gates_d[2].partition_broadcast(128), writes=["gate"])
        P.dma("sp", bfr[:], C.bf_d.partition_broadcast(128), writes=["bfr"])
        ftr = Ring(st, nc, "ft", 3, [128, D], BF16)
        tpr = Ring(st, nc, "p6tp", 2, [128, D], BF16, psum=True)
        fTr = Ring(st, nc, "fT", 2, [128, 8, 128], BF16)
        xtr = Ring(st, nc, "p6xt", 3, [128, D], F32)
        xor_ = Ring(st, nc, "p6xo", 2, [128, D], F32)
        Wr = Ring(st, nc, "wf_ps", 4, [128, 512], F32, psum=True)
        for T in range(NT):
            ft, fttok = ftr.next()
            P.dma("sp", ft[:], C.f_d[T * 128:(T + 1) * 128, :], writes=[fttok])
            xt, xtok = xtr.next()
            P.dma("sp", xt[:], C.x2_d[T * 128:(T + 1) * 128, :], writes=[xtok])
            tp, tptok = tpr.next()
            for kc in range(8):
                P.op("pe", lambda e, kc=kc: e.transpose(out=tp[:, kc * 128:(kc + 1) * 128], in_=ft[:, kc * 128:(kc + 1) * 128],
                                                         identity=C.ident[:]),
                     reads=[fttok, "ident"], writes=[tptok])
            fT, fTtok = fTr.next()
            P.op("act", lambda e: e.activation(out=fT[:].rearrange("p k t -> p (k t)"), in_=tp[:], func=AF.Identity),
                 reads=[tptok], writes=[fTtok])
            xo, xotok = xor_.next()
            for half in range(2):
                Wp, Wtok = Wr.next()
                sl = slice(half * 512, (half + 1) * 512)
                for kc in range(8):
                    P.op("pe", lambda e, Wp=Wp, kc=kc, sl=sl: e.matmul(Wp[:], lhsT=fT[:, kc, :], rhs=wf[:, kc, sl],
                                                                       start=(kc == 0), stop=(kc == 7)),
                         reads=[fTtok, ("wf", kc)], writes=[Wtok])
                P.op("dve", lambda e, Wp=Wp, sl=sl: e.tensor_tensor(out=xo[:, sl], in0=Wp[:], in1=bfr[:, sl], op=ALU.add),
                     reads=[Wtok, "bfr"], writes=[(xotok, half)])
                P.op("pool", lambda e, sl=sl: e.tensor_tensor(out=xo[:, sl], in0=xo[:, sl], in1=gate[:, sl], op=ALU.mult),
                     reads=[(xotok, half), "gate"], writes=[(xotok, half)])
                P.op("pool", lambda e, sl=sl: e.tensor_tensor(out=xo[:, sl], in0=xo[:, sl], in1=xt[:, sl], op=ALU.add),
                     reads=[(xotok, half), xtok], writes=[(xotok, half)])
            P.dma("pool", C.x3_d[T * 128:(T + 1) * 128, :], xo[:], reads=[(xotok, 0), (xotok, 1)], writes=[("x3_d", T)])
        P.barrier()


def _dft_tables():
    c = np.arange(256, dtype=np.float64)
    ang = 2 * np.pi * np.outer(c, c) / 256.0
    Cc, Sc = np.cos(ang) / 16.0, np.sin(ang) / 16.0
    fc = np.concatenate([Cc, -Sc], axis=1)
    fc = fc.reshape(2, 128, 512).transpose(1, 0, 2)
    n = np.arange(64, dtype=np.float64)
    a1 = 2 * np.pi * np.outer(n, n) / 64.0
    C1, S1 = np.cos(a1) / 8.0, np.sin(a1) / 8.0
    m1 = np.block([[C1, -S1], [S1, C1]])
    n2 = n[:, None, None]; k1 = n[None, :, None]; k2 = n[None, None, :]
    th = 2 * np.pi * (n2 * k2 / 64.0 + n2 * k1 / 4096.0)
    m2 = np.concatenate([np.cos(th), np.sin(th)], axis=0) / 8.0
    f32 = lambda a: np.ascontiguousarray(a.astype(np.float32))
    return f32(fc), f32(m1), f32(m2)


def _rope_tables():
    rows = SEQ // 64
    row = np.repeat(np.arange(rows), 64).astype(np.float32)
    col = np.tile(np.arange(64), rows).astype(np.float32)
    inv_freq = (np.float32(10000.0) ** (-np.arange(32, dtype=np.float32) / np.float32(32))).astype(np.float32)
    ang = np.concatenate([row[:, None] * inv_freq, col[:, None] * inv_freq], axis=-1).astype(np.float32)
    return np.cos(ang).astype(np.float32), np.sin(ang).astype(np.float32)


def _host_inputs(inputs):
    f = lambda a: np.ascontiguousarray(np.asarray(a, dtype=np.float32))
    x = f(inputs["x"]); c = f(inputs["c"]); ctx = f(inputs["ctx"]); c_ctx = f(inputs["c_ctx"])
    w_mod = f(inputs["w_mod"]); b_mod = f(inputs["b_mod"])
    pp = lambda v: np.ascontiguousarray(v.reshape(-1, 128).T)
    b_mod_pp = np.stack([pp(b_mod[l]) for l in range(2)])
    g_mix_pp = np.stack([pp(f(inputs["g_mix"])[l]) for l in range(2)])
    g_ffn_pp = np.stack([pp(f(inputs["g_ffn"])[l]) for l in range(2)])
    cos, sin = _rope_tables()
    shared = {
        "w_mod": w_mod, "b_mod_pp": b_mod_pp, "b_mod": b_mod, "g_mix_pp": g_mix_pp, "g_ffn_pp": g_ffn_pp,
        "w_qkv": f(inputs["w_qkv"])[0], "g_q": f(inputs["g_q"])[0], "g_k": f(inputs["g_k"])[0],
        "w_o": f(inputs["w_attn_out"])[0], "rope_cos": cos, "rope_sin": sin,
        "ident": np.eye(128, dtype=np.float32),
        "w_gate_up": f(inputs["w_gate_up"]), "w_down": f(inputs["w_down"]),
        "w_fourier": f(inputs["w_fourier"])[0], "b_fourier": f(inputs["b_fourier"])[0],
        "g_final": f(inputs["g_final"]),
    }
    shared["dft_fc"], shared["dft_m1"], shared["dft_m2"] = _dft_tables()
    maps = []
    for b in range(NCORES):
        c_pp = np.ascontiguousarray(np.stack([pp(c[b]), pp(c_ctx)], axis=-1))
        m = {"x": x[b], "ctx": ctx[b], "c_pp": c_pp}
        m.update(shared)
        maps.append(m)
    return maps


def kernel(**inputs):
    nc = build_program()
    maps = _host_inputs(inputs)
    res = run_bass_kernel_spmd(nc, maps, core_ids=list(range(NCORES)))
    return np.stack([np.asarray(r["out"], dtype=np.float32) for r in res.results], axis=0)
```

```python
import math
from contextlib import ExitStack

import numpy as np
import concourse.bass as bass
import concourse.mybir as mybir
from concourse.bass_utils import run_bass_kernel_spmd

F32 = mybir.dt.float32
BF16 = mybir.dt.bfloat16
AF = mybir.ActivationFunctionType
ALU = mybir.AluOpType
AX = mybir.AxisListType

D = 1024
SEQ = 4096
CTX = 256
NH = 8
NKV = 2
HD = 128
DFF = 2816
NFC = DFF // 128
NT = SEQ // 128
NTC = CTX // 128
NKT = NT + NTC
EPS = 1e-6
NCORES = 8


class Prog:
    def __init__(self, nc, stack):
        self.nc = nc
        self.eng = {"pe": nc.tensor, "act": nc.scalar, "dve": nc.vector, "pool": nc.gpsimd, "sp": nc.sync}
        self.semh = {}
        self.cnt = {}
        for e in self.eng:
            self.semh[e] = stack.enter_context(nc.semaphore("sem_" + e))
            self.cnt[e] = 0
        self.dq = {}
        for q, n in (("sp", 12), ("pool", 8), ("act", 4)):
            keys = []
            for i in range(n):
                k = "dma_%s_%d" % (q, i)
                self.semh[k] = stack.enter_context(nc.semaphore(k))
                self.cnt[k] = 0
                keys.append(k)
            self.dq[q] = {"keys": keys, "i": 0}
        self.known = {e: {} for e in self.eng}
        self.tok = {}
        self.ninst = 0

    def _need(self, e, reads, writes):
        need = {}

        def add(ev, same_ok):
            if ev is None:
                return
            sk, v = ev
            if sk == e and same_ok:
                return
            if need.get(sk, 0) < v:
                need[sk] = v

        for t in reads:
            st = self.tok.get(t)
            if st is not None:
                add(st["w"], False)
        for t in writes:
            st = self.tok.get(t)
            if st is not None:
                add(st["w"], True)
                for sk, v in st["r"].items():
                    add((sk, v), True)
        return need

    def _wait(self, e, need):
        eng = self.eng[e]
        kn = self.known[e]
        for sk, v in need.items():
            if kn.get(sk, 0) < v:
                eng.wait_ge(self.semh[sk], v)
                kn[sk] = v
                self.ninst += 1

    def _record(self, ev, reads, writes):
        for t in reads:
            st = self.tok.setdefault(t, {"w": None, "r": {}})
            if st["r"].get(ev[0], 0) < ev[1]:
                st["r"][ev[0]] = ev[1]
        for t in writes:
            self.tok[t] = {"w": ev, "r": {}}

    def op(self, e, fn, reads=(), writes=()):
        self._wait(e, self._need(e, reads, writes))
        inst = fn(self.eng[e])
        self.cnt[e] += 1
        inst.then_inc(self.semh[e], 1)
        self.ninst += 1
        self._record((e, self.cnt[e]), reads, writes)

    def dma(self, q, out, in_, reads=(), writes=(), **kw):
        dq = self.dq[q]
        k = dq["keys"][dq["i"] % len(dq["keys"])]
        dq["i"] += 1
        need = self._need(q, reads, writes)
        if self.cnt[k] > 0 and need.get(k, 0) < self.cnt[k]:
            need[k] = self.cnt[k]
        self._wait(q, need)
        inst = self.eng[q].dma_start(out=out, in_=in_, **kw)
        self.cnt[k] += 16
        inst.then_inc(self.semh[k], 16)
        self.ninst += 1
        self._record((k, self.cnt[k]), reads, writes)

    def barrier(self):
        for e in self.eng:
            need = {sk: v for sk, v in self.cnt.items() if v > 0}
            self._wait(e, need)
        self.tok = {}


class Ring:
    uid = 0

    def __init__(self, stack, nc, name, n, shape, dtype, psum=False):
        self.name = name
        self.n = n
        self.i = -1
        alloc = nc.psum_tensor if psum else nc.sbuf_tensor
        Ring.uid += 1
        self.tiles = [stack.enter_context(alloc("r%d_%s%d" % (Ring.uid, name, i), shape, dtype)) for i in range(n)]

    def next(self):
        self.i += 1
        s = self.i % self.n
        return self.tiles[s], (self.name, s)


class NS:
    pass


SCALE = float(HD) ** -0.5


def build_program(stop_after=None):
    nc = bass.Bass("TRN2", target_bir_lowering=False)
    C = NS()
    C.nc = nc
    C.stop_after = stop_after
    din = lambda name, shape, dt=F32: nc.dram_tensor(name, shape, dt, kind="ExternalInput").ap()
    dscr = lambda name, shape, dt=F32: nc.dram_tensor(name, shape, dt, kind="Internal").ap()
    C.x_d = din("x", [SEQ, D])
    C.ctx_d = din("ctx", [CTX, D])
    C.cpp_d = din("c_pp", [128, 8, 2])
    C.wmod_d = din("w_mod", [2, D, 6 * D])
    C.bmodpp_d = din("b_mod_pp", [2, 128, 48])
    C.bmod_d = din("b_mod", [2, 6 * D])
    C.gmixpp_d = din("g_mix_pp", [2, 128, 8])
    C.gffnpp_d = din("g_ffn_pp", [2, 128, 8])
    C.wqkv_d = din("w_qkv", [D, 1536])
    C.gq_d = din("g_q", [128])
    C.gk_d = din("g_k", [128])
    C.wo_d = din("w_o", [D, D])
    C.cos_d = din("rope_cos", [SEQ, 64])
    C.sin_d = din("rope_sin", [SEQ, 64])
    C.ident_d = din("ident", [128, 128])
    C.out_d = nc.dram_tensor("out", [SEQ, D], F32, kind="ExternalOutput").ap()
    C.QT_d = dscr("QT_scr", [NT, 128, 1024], BF16)
    C.gates_d = dscr("gates_scr", [4, 1024])
    C.wgu_d = din("w_gate_up", [2, D, 2 * DFF])
    C.wd_d = din("w_down", [2, DFF, D])
    C.wf_d = din("w_fourier", [D, D])
    C.bf_d = din("b_fourier", [D])
    C.gfin_d = din("g_final", [D])
    C.Fc_d = din("dft_fc", [128, 2, 512])
    C.M1_d = din("dft_m1", [128, 128])
    C.M2_d = din("dft_m2", [128, 64, 64])
    C.Z_d = dscr("Z_scr", [2, SEQ, D], BF16)
    C.T1_d = dscr("T1_scr", [2, 64, 64, D], BF16)
    C.f_d = dscr("f_scr", [SEQ, D], BF16)
    names = ["x1", "x2", "x3"]
    for i, nm in enumerate(names):
        setattr(C, nm + "_d", C.out_d if stop_after == "p%d" % (i + 2) and False else dscr(nm + "_scr", [SEQ, D]))
    if stop_after == "p2":
        C.x1_d = C.out_d
    if stop_after == "p3":
        C.x2_d = C.out_d
    if stop_after == "p6":
        C.x3_d = C.out_d

    with ExitStack() as gs:
        P = Prog(nc, gs)
        C.P = P
        C.gs = gs
        uid = [0]

        def _alloc(fn, pre, name, shape, dt, st):
            uid[0] += 1
            return st.enter_context(fn("%s%d_%s" % (pre, uid[0], name), shape, dt))

        C.sb = lambda name, shape, dt=F32, st=gs: _alloc(nc.sbuf_tensor, "sb", name, shape, dt, st)
        C.ps = lambda name, shape, dt=F32, st=gs: _alloc(nc.psum_tensor, "ps", name, shape, dt, st)
        sb = C.sb
        C.modpp = sb("modpp", [128, 2, 4, 8, 2])
        C.gmix = sb("gmix", [128, 2, 8])
        C.gffn = sb("gffn", [128, 2, 8])
        C.Amod = sb("Amod", [128, 5, 8])
        C.Bmod = sb("Bmod", [128, 5, 8])
        C.ident_f = sb("ident_f", [128, 128])
        C.ident = sb("ident", [128, 128], BF16)
        C.ones = sb("ones", [128, 128], BF16)
        P.dma("sp", C.ident_f[:], C.ident_d, writes=["ident_f"])
        P.op("dve", lambda e: e.tensor_copy(out=C.ident[:], in_=C.ident_f[:]), reads=["ident_f"], writes=["ident"])
        P.op("dve", lambda e: e.memset(C.ones[:], 1.0), writes=["ones"])
        C.eps = sb("eps", [128, 1])
        P.op("dve", lambda e: e.memset(C.eps[:], EPS), writes=["eps"])

        phase0_setup(C)
        with ExitStack() as st12:
            C.KT = sb("KT", [128, NKV, NKT * 128], BF16, st=st12)
            C.Vs = sb("Vs", [128, NKT, NKV * HD], BF16, st=st12)
            phase1(C)
            phase2(C)
        if stop_after == "p2":
            return nc
        ffn_phase(C, 0, C.x1_d, C.x2_d)
        if stop_after == "p3":
            return nc
        phase4(C)
        phase5(C)
        phase6(C)
        if stop_after == "p6":
            return nc
        ffn_phase(C, 1, C.x3_d, C.out_d, final=True)
        print("instructions:", P.ninst)
    return nc


def rms_stage(C, R, src_ap, a_idx, hT, hTtok, col0=0):
    P = C.P
    xt, xtok = R["xt"].next()
    P.dma("sp", xt[:], src_ap, writes=[xtok])
    rms_from_tile(C, R, xt, xtok, a_idx, hT, hTtok, col0)
    return xt, xtok


def rms_from_tile(C, R, xt, xtok, a_idx, hT, hTtok, col0=0):
    P = C.P
    ss, sstok = R["ss"].next()
    xn, xntok = R["xn"].next()
    tp, tptok = R["tp"].next()
    junk = R["junk"]
    P.op("act", lambda e: e.activation(out=junk[:], in_=xt[:], func=AF.Square, accum_out=ss[:, 0:1]),
         reads=[xtok], writes=["junk", (sstok, 0)])
    P.op("act", lambda e: e.activation(out=ss[:, 1:2], in_=ss[:, 0:1], func=AF.Sqrt, scale=1.0 / D, bias=C.eps[:, 0:1]),
         reads=[(sstok, 0), "eps"], writes=[(sstok, 1)])
    P.op("dve", lambda e: e.reciprocal(out=ss[:, 2:3], in_=ss[:, 1:2]), reads=[(sstok, 1)], writes=[(sstok, 2)])
    P.op("act", lambda e: e.activation(out=xn[:], in_=xt[:], func=AF.Identity, scale=ss[:, 2:3]),
         reads=[xtok, (sstok, 2)], writes=[xntok])
    for kc in range(8):
        P.op("pe", lambda e, kc=kc: e.transpose(out=tp[:, kc * 128:(kc + 1) * 128], in_=xn[:, kc * 128:(kc + 1) * 128],
                                                 identity=C.ident[:]),
             reads=[xntok, "ident"], writes=[tptok])
    for kc in range(8):
        P.op("dve", lambda e, kc=kc: e.tensor_scalar(
            out=hT[:, kc, col0:col0 + 128], in0=tp[:, kc * 128:(kc + 1) * 128],
            scalar1=C.Amod[:, a_idx, kc:kc + 1], scalar2=C.Bmod[:, a_idx, kc:kc + 1], op0=ALU.mult, op1=ALU.add),
            reads=[tptok, ("Amod", a_idx), ("Bmod", a_idx)], writes=[hTtok])


def load_weight_bf16(C, st, name, dst, dst_tok_fn, src_view, nchunks, chunk_shape, engines=("dve", "pool")):
    P, nc = C.P, C.nc
    ring = Ring(st, nc, name + "_stg", 2, chunk_shape, F32)
    for i in range(nchunks):
        t, tok = ring.next()
        P.dma("sp", t[:], src_view(i), writes=[tok])
        eng = engines[i % len(engines)]
        P.op(eng, lambda e, t=t, i=i: e.tensor_copy(out=dst(i), in_=t[:]), reads=[tok], writes=[dst_tok_fn(i)])


def phase0_setup(C):
    nc, P, sb, ps = C.nc, C.P, C.sb, C.ps
    cpp = sb("cpp", [128, 8, 2])
    sig = sb("sig", [128, 8, 2])
    sc = sb("sc", [128, 8, 2])
    C.scb = sb("scb", [128, 8, 2], BF16)
    C.bpp = sb("bpp", [128, 2, 48])
    P.dma("sp", cpp[:], C.cpp_d, writes=["cpp"])
    P.dma("sp", C.bpp[:], C.bmodpp_d.rearrange("l p f -> p l f"), writes=["bpp"])
    P.dma("sp", C.gmix[:], C.gmixpp_d.rearrange("l p f -> p l f"), writes=["gmix"])
    P.dma("sp", C.gffn[:], C.gffnpp_d.rearrange("l p f -> p l f"), writes=["gffn"])
    P.op("act", lambda e: e.activation(out=sig[:], in_=cpp[:], func=AF.Sigmoid), reads=["cpp"], writes=["sig"])
    P.op("dve", lambda e: e.tensor_tensor(out=sc[:], in0=cpp[:], in1=sig[:], op=ALU.mult),
         reads=["cpp", "sig"], writes=["sc"])
    P.op("dve", lambda e: e.tensor_copy(out=C.scb[:], in_=sc[:]), reads=["sc"], writes=["scb"])


def mod_items(C, st, specs):
    nc, P, sb, ps = C.nc, C.P, C.sb, C.ps
    wring = Ring(st, nc, "wmod", 2, [128, 8, 512], BF16)
    bg_ps = ps("bg_ps", [128, 512], st=st)
    grow = Ring(st, nc, "grow", 2, [1, 512], F32)
    brow = Ring(st, nc, "browr", 2, [1, 512], F32)
    wv = C.wmod_d.rearrange("l (kc p) f -> l p kc f", p=128)
    items = []
    combos = {(0, 1): [(0, 0, 1, 0, C.gmix, 0), (1, 0, 1, 0, C.gmix, 1)], (0, 4): [(2, 0, 3, 2, C.gffn, 0)],
              (1, 1): [(3, 1, 1, 0, C.gmix, 0)], (1, 4): [(4, 1, 3, 2, C.gffn, 0)]}

    def make(l, m, half):
        def item():
            wt, wtok = wring.next()
            c0 = m * 1024 + half * 512
            P.dma("pool", wt[:], wv[l, :, :, c0:c0 + 512], writes=[wtok])
            if m in (2, 5):
                gi = l * 2 + (0 if m == 2 else 1)
                br, brtok = brow.next()
                P.dma("sp", br[:], C.bmod_d[l:l + 1, c0:c0 + 512], writes=[brtok])
                for kc in range(8):
                    P.op("pe", lambda e, kc=kc: e.matmul(bg_ps[0:1, :], lhsT=C.scb[:, kc, 0:1], rhs=wt[:, kc, :],
                                                         start=(kc == 0), stop=(kc == 7)),
                         reads=["scb", wtok], writes=["bg_ps"])
                gr, grtok = grow.next()
                P.op("dve", lambda e: e.tensor_tensor(out=gr[:], in0=bg_ps[0:1, :], in1=br[:], op=ALU.add),
                     reads=["bg_ps", brtok], writes=[grtok])
                P.dma("sp", C.gates_d[gi:gi + 1, half * 512:(half + 1) * 512], gr[:], reads=[grtok],
                      writes=[("gates_d", gi, half)])
            else:
                mi = {0: 0, 1: 1, 3: 2, 4: 3}[m]
                for f4 in range(4):
                    for kc in range(8):
                        P.op("pe", lambda e, kc=kc, f4=f4: e.matmul(
                            bg_ps[:, 2 * f4:2 * f4 + 2], lhsT=wt[:, kc, f4 * 128:(f4 + 1) * 128], rhs=C.scb[:, kc, :],
                            start=(kc == 0), stop=(kc == 7)),
                            reads=["scb", wtok], writes=["bg_ps"])
                P.op("dve", lambda e: e.tensor_tensor(
                    out=C.modpp[:, l, mi, half * 4:(half + 1) * 4, :],
                    in0=bg_ps[:, 0:8].rearrange("p (f t) -> p f t", t=2),
                    in1=C.bpp[:, l, m * 8 + half * 4:m * 8 + half * 4 + 4].unsqueeze(2).to_broadcast([128, 4, 2]), op=ALU.add),
                    reads=["bg_ps", "bpp"], writes=[("modpp", l, mi, half)])
                if half == 1 and (l, m) in combos:
                    for idx, l_, m_sc, m_sh, g, col in combos[(l, m)]:
                        P.op("dve", lambda e, idx=idx, m_sc=m_sc, g=g, col=col: e.scalar_tensor_tensor(
                            out=C.Amod[:, idx, :], in0=C.modpp[:, l, m_sc, :, col], scalar=1.0, in1=g[:, l, :],
                            op0=ALU.add, op1=ALU.mult),
                            reads=[("modpp", l, m_sc, 0), ("modpp", l, m_sc, 1), "gmix", "gffn"], writes=[("Amod", idx)])
                        P.op("dve", lambda e, idx=idx, m_sh=m_sh, col=col: e.tensor_copy(
                            out=C.Bmod[:, idx, :], in_=C.modpp[:, l, m_sh, :, col]),
                            reads=[("modpp", l, m_sh, 0), ("modpp", l, m_sh, 1)], writes=[("Bmod", idx)])
        return item

    for (l, m) in specs:
        for half in range(2):
            items.append(make(l, m, half))
    return items


def phase1(C):
    nc, P, sb, ps = C.nc, C.P, C.sb, C.ps
    with ExitStack() as st:
        all_items = mod_items(C, st, [(0, 0), (0, 1), (0, 3), (0, 4), (0, 2), (0, 5), (1, 0), (1, 1), (1, 3), (1, 4), (1, 2), (1, 5)])
        fg = all_items[:4]
        items_bg = None
        wqkv = sb("wqkv", [128, 8, 1536], BF16, st=st)
        cosb = sb("cosb", [128, NT, 64], BF16, st=st)
        sinb = sb("sinb", [128, NT, 64], BF16, st=st)
        gqk = sb("gqk", [128, 10, 128], st=st)
        for it in fg:
            it()
        P.dma("pool", cosb[:], C.cos_d.rearrange("(t p) f -> p t f", p=128), writes=["cos"])
        P.dma("pool", sinb[:], C.sin_d.rearrange("(t p) f -> p t f", p=128), writes=["sin"])
        for h in range(10):
            P.dma("sp", gqk[:, h, :], (C.gq_d if h < 8 else C.gk_d).partition_broadcast(128), writes=["gqk"])
        wv = C.wqkv_d.rearrange("(kc p) f -> p kc f", p=128)
        for kc in range(8):
            P.dma("pool", wqkv[:, kc, :], wv[:, kc, :], writes=[("wqkv", kc)])
        xtr = Ring(st, nc, "xt", 3, [128, D], F32)
        ssr = Ring(st, nc, "ss", 3, [128, 4], F32)
        xnr = Ring(st, nc, "xn", 2, [128, D], BF16)
        tpr = Ring(st, nc, "tp", 2, [128, D], BF16, psum=True)
        junk = sb("junk", [128, D], BF16, st=st)
        hTr = Ring(st, nc, "hT", 2, [128, 8, 128], BF16)
        qkv_ps = [ps("qkv_ps%d" % i, [128, 512], st=st) for i in range(3)]
        qkvr = Ring(st, nc, "qkv_sb", 3, [128, 1536], F32)
        ssq = Ring(st, nc, "ssq", 3, [128, 3, 10], F32)
        qg = sb("qg", [128, 10, 128], BF16, st=st)
        t1 = sb("t1", [128, 10, 64], BF16, st=st)
        t2 = sb("t2", [128, 10, 64], BF16, st=st)
        t3 = sb("t3", [128, 10, 64], BF16, st=st)
        t4 = sb("t4", [128, 10, 64], BF16, st=st)
        qrr = Ring(st, nc, "qr", 2, [128, 10, 128], BF16)
        qT_ps = ps("qT_ps", [128, 1024], BF16, st=st)
        kT_ps = ps("kT_ps", [128, 1024], BF16, st=st)
        qTr = Ring(st, nc, "qT_sb", 2, [128, 1024], BF16)
        items_bg = all_items[4:]
        print("p1 sbuf remaining", nc.sbuf_bytes_remaining)
        info = {}
        lat = lambda T: T < NT

        def sA(T):
            src = C.x_d[T * 128:(T + 1) * 128, :] if lat(T) else C.ctx_d[(T - NT) * 128:(T - NT + 1) * 128, :]
            xt, xtok = xtr.next()
            P.dma("sp", xt[:], src, writes=[xtok])
            info[T] = dict(xt=xt, xtok=xtok)

        def sB(T):
            s = info[T]
            xt, xtok = s["xt"], s["xtok"]
            ss, sstok = ssr.next()
            xn, xntok = xnr.next()
            P.op("act", lambda e: e.activation(out=junk[:], in_=xt[:], func=AF.Square, accum_out=ss[:, 0:1]),
                 reads=[xtok], writes=["junk", (sstok, 0)])
            P.op("act", lambda e: e.activation(out=ss[:, 1:2], in_=ss[:, 0:1], func=AF.Ln, scale=1.0 / D, bias=C.eps[:, 0:1]),
                 reads=[(sstok, 0), "eps"], writes=[(sstok, 1)])
            P.op("act", lambda e: e.activation(out=ss[:, 2:3], in_=ss[:, 1:2], func=AF.Exp, scale=-0.5),
                 reads=[(sstok, 1)], writes=[(sstok, 2)])
            P.op("act", lambda e: e.activation(out=xn[:], in_=xt[:], func=AF.Identity, scale=ss[:, 2:3]),
                 reads=[xtok, (sstok, 2)], writes=[xntok])
            s.update(xn=xn, xntok=xntok)

        def sE(T):
            s = info[T]
            tp, tptok = tpr.next()
            for kc in range(8):
                P.op("pe", lambda e, kc=kc: e.transpose(out=tp[:, kc * 128:(kc + 1) * 128],
                                                         in_=s["xn"][:, kc * 128:(kc + 1) * 128], identity=C.ident[:]),
                     reads=[s["xntok"], "ident"], writes=[tptok])
            s.update(tp=tp, tptok=tptok)

        def sF(T):
            s = info[T]
            a_idx = 0 if lat(T) else 1
            hT, hTtok = hTr.next()
            for kc in range(8):
                P.op("dve", lambda e, kc=kc: e.tensor_scalar(
                    out=hT[:, kc, :], in0=s["tp"][:, kc * 128:(kc + 1) * 128],
                    scalar1=C.Amod[:, a_idx, kc:kc + 1], scalar2=C.Bmod[:, a_idx, kc:kc + 1], op0=ALU.mult, op1=ALU.add),
                    reads=[s["tptok"], ("Amod", a_idx), ("Bmod", a_idx)], writes=[hTtok])
            s.update(hT=hT, hTtok=hTtok)

        def sG(T):
            s = info[T]
            banks = (0, 1, 2) if lat(T) else (2,)
            for nb in banks:
                for kc in range(8):
                    P.op("pe", lambda e, nb=nb, kc=kc: e.matmul(
                        qkv_ps[nb][:], lhsT=s["hT"][:, kc, :], rhs=wqkv[:, kc, nb * 512:(nb + 1) * 512],
                        start=(kc == 0), stop=(kc == 7)),
                        reads=[s["hTtok"], ("wqkv", kc)], writes=[("qkv_ps", nb)])

        def sH(T):
            s = info[T]
            banks = (0, 1, 2) if lat(T) else (2,)
            h0 = 0 if lat(T) else 8
            qs, qstok = qkvr.next()
            for nb in banks:
                P.op("act", lambda e, nb=nb: e.activation(out=qs[:, nb * 512:(nb + 1) * 512], in_=qkv_ps[nb][:],
                                                          func=AF.Identity),
                     reads=[("qkv_ps", nb)], writes=[(qstok, nb)])
            sq3, sqtok = ssq.next()
            for h in range(h0, 10):
                P.op("act", lambda e, h=h: e.activation(out=junk[:, 0:128], in_=qs[:, h * 128:(h + 1) * 128], func=AF.Square,
                                                        accum_out=sq3[:, 0, h:h + 1]),
                     reads=[(qstok, h // 4)], writes=["junk", (sqtok, 0, h)])
            P.op("act", lambda e: e.activation(out=sq3[:, 1, h0:10], in_=sq3[:, 0, h0:10], func=AF.Ln,
                                               scale=1.0 / HD, bias=C.eps[:, 0:1]),
                 reads=[(sqtok, 0, h) for h in range(h0, 10)] + ["eps"], writes=[(sqtok, 1)])
            P.op("act", lambda e: e.activation(out=sq3[:, 2, h0:10], in_=sq3[:, 1, h0:10], func=AF.Exp, scale=-0.5),
                 reads=[(sqtok, 1)], writes=[(sqtok, 2)])
            s.update(qs=qs, qstok=qstok, sq3=sq3, sqtok=sqtok)

        def sK(T):
            s = info[T]
            qs, qstok, sq3, sqtok = s["qs"], s["qstok"], s["sq3"], s["sqtok"]
            h0 = 0 if lat(T) else 8
            rd = [(qstok, nb) for nb in ((0, 1, 2) if lat(T) else (2,))]
            P.op("pool", lambda e: e.tensor_copy(out=C.Vs[:, T, :], in_=qs[:, 1280:1536]), reads=[(qstok, 2)],
                 writes=[("V", T)])
            for h in range(h0, 10):
                P.op("dve", lambda e, h=h: e.scalar_tensor_tensor(
                    out=qg[:, h, :], in0=qs[:, h * 128:(h + 1) * 128], scalar=sq3[:, 2, h:h + 1], in1=gqk[:, h, :],
                    op0=ALU.mult, op1=ALU.mult),
                    reads=rd + [(sqtok, 2), "gqk"], writes=[("qg", h)])
            qr, qrtok = qrr.next()
            if lat(T):
                cb = cosb[:, T, :].unsqueeze(1).to_broadcast([128, 10, 64])
                sbb = sinb[:, T, :].unsqueeze(1).to_broadcast([128, 10, 64])
                x1 = qg[:, :, 0:64]
                x2 = qg[:, :, 64:128]
                qall = [("qg", h) for h in range(10)]
                P.op("dve", lambda e: e.tensor_tensor(out=t1[:], in0=x1, in1=cb, op=ALU.mult), reads=qall + ["cos"], writes=["t1"])
                P.op("dve", lambda e: e.tensor_tensor(out=t2[:], in0=x2, in1=sbb, op=ALU.mult), reads=qall + ["sin"], writes=["t2"])
                P.op("dve", lambda e: e.tensor_tensor(out=t3[:], in0=x1, in1=sbb, op=ALU.mult), reads=qall + ["sin"], writes=["t3"])
                P.op("dve", lambda e: e.tensor_tensor(out=t4[:], in0=x2, in1=cb, op=ALU.mult), reads=qall + ["cos"], writes=["t4"])
                P.op("dve", lambda e: e.tensor_tensor(out=qr[:, :, 0:64], in0=t1[:], in1=t2[:], op=ALU.subtract),
                     reads=["t1", "t2"], writes=[(qrtok, 0)])
                P.op("dve", lambda e: e.tensor_tensor(out=qr[:, :, 64:128], in0=t3[:], in1=t4[:], op=ALU.add),
                     reads=["t3", "t4"], writes=[(qrtok, 1)])
            else:
                P.op("dve", lambda e: e.tensor_copy(out=qr[:, 8:10, :], in_=qg[:, 8:10, :]),
                     reads=[("qg", 8), ("qg", 9)], writes=[(qrtok, 0), (qrtok, 1)])
            s.update(qr=qr, qrtok=qrtok)

        def sL(T):
            s = info[T]
            qr, rd = s["qr"], [(s["qrtok"], 0), (s["qrtok"], 1)]
            if lat(T):
                for h in range(8):
                    P.op("pe", lambda e, h=h: e.transpose(out=qT_ps[:, h * 128:(h + 1) * 128], in_=qr[:, h, :],
                                                           identity=C.ident[:]),
                         reads=rd + ["ident"], writes=["qT_ps"])
            for j in range(2):
                P.op("pe", lambda e, j=j: e.transpose(out=kT_ps[:, j * 128:(j + 1) * 128], in_=qr[:, 8 + j, :],
                                                       identity=C.ident[:]),
                     reads=rd + ["ident"], writes=["kT_ps"])

        def sM(T):
            if lat(T):
                qT, qTtok = qTr.next()
                P.op("act", lambda e: e.activation(out=qT[:], in_=qT_ps[:], func=AF.Identity), reads=["qT_ps"], writes=[qTtok])
                P.dma("pool", C.QT_d[T], qT[:], reads=[qTtok], writes=[("QT_d", T)])
            P.op("dve", lambda e: e.tensor_copy(
                out=C.KT[:, :, T * 128:(T + 1) * 128], in_=kT_ps[:, 0:256].rearrange("p (j t) -> p j t", t=128)),
                reads=["kT_ps"], writes=[("KT", T)])
            del info[T]

        order = [(sH, 5), (sM, 8), (sL, 7), (sK, 6), (sG, 4), (sF, 3), (sE, 2), (sB, 1), (sA, 0)]
        nit = NKT + 8
        for i in range(nit):
            for fn, skew in order:
                T = i - skew
                if 0 <= T < NKT:
                    fn(T)
            if items_bg and i >= 2:
                items_bg.pop(0)()
        while items_bg:
            items_bg.pop(0)()
        P.barrier()


def phase2(C):
    nc, P, sb, ps = C.nc, C.P, C.sb, C.ps
    NKP = NKT // 2
    with ExitStack() as st:
        wo = sb("wo", [128, 8, 1024], BF16, st=st)
        wv = C.wo_d.rearrange("(h p) f -> p h f", p=128)
        for h in range(8):
            P.dma("pool", wo[:, h, :], wv[:, h, :], writes=[("wo", h)])
        qTr = Ring(st, nc, "qT2", 2, [128, 1024], BF16)
        xtr = Ring(st, nc, "xt2", 3, [128, D], F32)
        PTr = Ring(st, nc, "PT", 10, [128, 1024], BF16)
        saccr = Ring(st, nc, "sacc", 2, [128, 2, 1024], BF16)
        Sr = Ring(st, nc, "S_ps", 2, [128, 1024], F32, psum=True)
        accr = Ring(st, nc, "acc_ps", 2, [128, 512], F32, psum=True)
        den_ps = ps("den_ps", [128, 512], st=st)
        wo_ps = ps("wo_ps", [128, 512], st=st)
        recr = Ring(st, nc, "rec", 2, [128, 512], F32)
        OTr = Ring(st, nc, "OT", 2, [128, 8, 128], BF16)
        x1r = Ring(st, nc, "x1t", 2, [128, D], F32)
        gate = sb("gate_p2", [128, D], st=st)
        P.dma("sp", gate[:], C.gates_d[0].partition_broadcast(128), writes=["gate"])

        steps = [(T, kvh, p) for T in range(NT) for kvh in range(NKV) for p in range(NKP)]
        state = {}
        pending = []

        def load_tile(T):
            qT, qTtok = qTr.next()
            xt, xtok = xtr.next()
            P.dma("sp", qT[:], C.QT_d[T], writes=[qTtok])
            P.dma("sp", xt[:], C.x_d[T * 128:(T + 1) * 128, :], writes=[xtok])
            OT, OTtok = OTr.next()
            state[T] = dict(qT=qT, qTtok=qTtok, xt=xt, xtok=xtok, OT=OT, OTtok=OTtok)

        def emit_S(i):
            T, kvh, p = steps[i]
            if kvh == 0 and p == 0:
                if T not in state:
                    load_tile(T)
                if T + 1 < NT and (T + 1) not in state:
                    load_tile(T + 1)
            s = state[T]
            S, Stok = Sr.next()
            for u in range(2):
                kt = 2 * p + u
                P.op("pe", lambda e, kt=kt, u=u: e.matmul(
                    S[:, u * 512:(u + 1) * 512], lhsT=C.KT[:, kvh, kt * 128:(kt + 1) * 128],
                    rhs=s["qT"][:, kvh * 512:(kvh + 1) * 512], start=True, stop=True),
                    reads=[("KT", kt), s["qTtok"]], writes=[(Stok, u)])
            PT, PTtok = PTr.next()
            P.op("act", lambda e: e.activation(out=PT[:], in_=S[:], func=AF.Exp, scale=SCALE),
                 reads=[(Stok, 0), (Stok, 1)], writes=[PTtok])
            return PT, PTtok

        def emit_PV(i, PT, PTtok):
            T, kvh, p = steps[i]
            s = state[T]
            if p == 0:
                s["acc"], s["acctok"] = accr.next()
                s["sacc"], s["sacctok"] = saccr.next()
            acc, acctok, sacc, sacctok = s["acc"], s["acctok"], s["sacc"], s["sacctok"]
            for u in range(2):
                kt = 2 * p + u
                P.op("pe", lambda e, kt=kt, u=u: e.matmul(
                    acc[:], lhsT=C.Vs[:, kt, kvh * 128:(kvh + 1) * 128], rhs=PT[:, u * 512:(u + 1) * 512],
                    start=(kt == 0), stop=(kt == NKT - 1)),
                    reads=[("V", kt), PTtok], writes=[acctok])
            a = p % 2
            if p < 2:
                P.op("dve", lambda e: e.tensor_copy(out=sacc[:, a, :], in_=PT[:]), reads=[PTtok], writes=[(sacctok, a)])
            else:
                P.op("dve", lambda e: e.tensor_tensor(out=sacc[:, a, :], in0=sacc[:, a, :], in1=PT[:], op=ALU.add),
                     reads=[PTtok, (sacctok, a)], writes=[(sacctok, a)])
            if p == NKP - 1:
                OT, OTtok = s["OT"], s["OTtok"]

                def fin(T=T, kvh=kvh, acc=acc, acctok=acctok, sacc=sacc, sacctok=sacctok, OT=OT, OTtok=OTtok):
                    for u in range(4):
                        P.op("pe", lambda e, u=u: e.matmul(den_ps[:], lhsT=C.ones[:],
                                                           rhs=sacc[:, u // 2, (u % 2) * 512:(u % 2 + 1) * 512],
                                                           start=(u == 0), stop=(u == 3)),
                             reads=["ones", (sacctok, 0), (sacctok, 1)], writes=["den_ps"])
                    rec, rectok = recr.next()
                    P.op("dve", lambda e: e.reciprocal(out=rec[:], in_=den_ps[:]), reads=["den_ps"], writes=[rectok])
                    P.op("dve", lambda e: e.tensor_tensor(
                        out=OT[:, kvh * 4:(kvh + 1) * 4, :].rearrange("p h q -> p (h q)"), in0=acc[:], in1=rec[:], op=ALU.mult),
                        reads=[acctok, rectok], writes=[(OTtok, kvh)])
                    if kvh == NKV - 1:
                        pending.append([5, lambda T=T: emit_wo(T)])

                pending.append([2, fin])

        def emit_wo(T):
            s = state[T]
            OT, OTtok = s["OT"], s["OTtok"]
            x1t, x1tok = x1r.next()

            def chunk(half, c):
                sl = slice(half * 512, (half + 1) * 512)
                for h in (2 * c, 2 * c + 1):
                    P.op("pe", lambda e, h=h: e.matmul(
                        wo_ps[:], lhsT=OT[:, h, :], rhs=wo[:, h, half * 512:(half + 1) * 512],
                        start=(h == 0), stop=(h == 7)),
                        reads=[(OTtok, h // 4), ("wo", h)], writes=["wo_ps"])
                if c == 3:
                    P.op("dve", lambda e: e.tensor_tensor(out=x1t[:, sl], in0=wo_ps[:], in1=gate[:, sl], op=ALU.mult),
                         reads=["wo_ps", "gate"], writes=[(x1tok, half)])
                    P.op("pool", lambda e: e.tensor_tensor(out=x1t[:, sl], in0=x1t[:, sl], in1=s["xt"][:, sl], op=ALU.add),
                         reads=[(x1tok, half), s["xtok"]], writes=[(x1tok, half)])
                    if half == 1:
                        P.dma("pool", C.x1_d[T * 128:(T + 1) * 128, :], x1t[:], reads=[(x1tok, 0), (x1tok, 1)],
                              writes=[("x1_d", T)])
                        del state[T]
                        return
                nh, ncn = (half, c + 1) if c < 3 else (half + 1, 0)
                pending.append([1, lambda: chunk(nh, ncn)])

            chunk(0, 0)

        def tick():
            for p in list(pending):
                p[0] -= 1
                if p[0] <= 0:
                    pending.remove(p)
                    p[1]()

        LA = 2
        q = [emit_S(i) for i in range(LA)]
        for i in range(len(steps)):
            if i + LA < len(steps):
                q.append(emit_S(i + LA))
            emit_PV(i, *q.pop(0))
            tick()
        while pending:
            tick()
        P.barrier()


def ffn_phase(C, layer, src_d, dst_d, final=False):
    nc, P, sb, ps = C.nc, C.P, C.sb, C.ps
    a_idx = 2 if layer == 0 else 4
    gi = layer * 2 + 1
    with ExitStack() as st:
        wgu = sb("wgu", [128, 8, 2 * DFF], BF16, st=st)
        wd = sb("wd", [128, NFC, D], BF16, st=st)
        wguv = C.wgu_d[layer].rearrange("(kc p) f -> p kc f", p=128)
        wdv = C.wd_d[layer].rearrange("(fc p) d -> p fc d", p=128)
        for kc in range(8):
            for hh in range(2):
                P.dma("pool", wgu[:, kc, hh * DFF:(hh + 1) * DFF], wguv[:, kc, hh * DFF:(hh + 1) * DFF],
                      writes=[("wgu", kc, hh)])
        for fc in range(NFC):
            P.dma("pool", wd[:, fc, :], wdv[:, fc, :], writes=[("wd", fc)])
        gate = sb("gate_ffn", [128, D], st=st)
        P.dma("sp", gate[:], C.gates_d[gi].partition_broadcast(128), reads=[("gates_d", gi)], writes=["gate"])
        if final:
            gfin = sb("gfin", [128, D], st=st)
            P.dma("sp", gfin[:], C.gfin_d.partition_broadcast(128), writes=["gfin"])
        R = {
            "xt": Ring(st, nc, "fxt", 2, [128, D], F32),
            "ss": Ring(st, nc, "fss", 3, [128, 4], F32),
            "xn": Ring(st, nc, "fxn", 2, [128, D], BF16),
            "tp": Ring(st, nc, "ftp", 2, [128, D], BF16, psum=True),
            "junk": sb("fjunk", [128, D], BF16, st=st),
        }
        hT2 = sb("hT2", [128, 8, 512], BF16, st=st)
        aT = sb("aT", [128, NFC, 512], BF16, st=st)
        Gr = Ring(st, nc, "G_ps", 2, [128, 512], F32, psum=True)
        Ur = Ring(st, nc, "U_ps", 2, [128, 512], F32, psum=True)
        Dr = Ring(st, nc, "D_ps", 2, [128, 512], F32, psum=True)
        sgr = Ring(st, nc, "sg", 2, [128, 512], F32)
        xrr = Ring(st, nc, "xres", 2, [128, D], F32)
        xor_ = Ring(st, nc, "xo", 2, [128, D], F32)
        fss = Ring(st, nc, "finss", 2, [128, 4], F32)
        print("ffn sbuf remaining", nc.sbuf_bytes_remaining)
        NB = NT // 4

        def emit_rms(bk):
            for j in range(4):
                T = bk * 4 + j
                rms_stage(C, R, src_d[T * 128:(T + 1) * 128, :], a_idx, hT2, ("hT2", j), col0=j * 128)

        def emit_gu(bk):
            for fc in range(NFC):
                G, Gtok = Gr.next()
                U, Utok = Ur.next()
                for (ps_t, ps_tok, c0) in ((G, Gtok, fc * 128), (U, Utok, DFF + fc * 128)):
                    hh = 0 if c0 < DFF else 1
                    for kc in range(8):
                        P.op("pe", lambda e, ps_t=ps_t, c0=c0, kc=kc: e.matmul(
                            ps_t[:], lhsT=wgu[:, kc, c0:c0 + 128], rhs=hT2[:, kc, :], start=(kc == 0), stop=(kc == 7)),
                            reads=[("wgu", kc, hh)] + [("hT2", j) for j in range(4)], writes=[ps_tok])
                sg, sgtok = sgr.next()
                P.op("act", lambda e, sg=sg, G=G: e.activation(out=sg[:], in_=G[:], func=AF.Silu), reads=[Gtok], writes=[sgtok])
                P.op("dve", lambda e, sg=sg, U=U, fc=fc: e.tensor_tensor(out=aT[:, fc, :], in0=U[:], in1=sg[:], op=ALU.mult),
                     reads=[Utok, sgtok], writes=[("aT", fc)])

        def emit_down(bk):
            for j in range(4):
                T = bk * 4 + j
                xr, xrtok = xrr.next()
                P.dma("sp", xr[:], src_d[T * 128:(T + 1) * 128, :], writes=[xrtok])
                xo, xotok = xor_.next()
                for half in range(2):
                    Dp, Dtok = Dr.next()
                    sl = slice(half * 512, (half + 1) * 512)
                    for fc in range(NFC):
                        P.op("pe", lambda e, Dp=Dp, fc=fc, sl=sl, j=j: e.matmul(
                            Dp[:], lhsT=aT[:, fc, j * 128:(j + 1) * 128], rhs=wd[:, fc, sl],
                            start=(fc == 0), stop=(fc == NFC - 1)),
                            reads=[("aT", fc), ("wd", fc)], writes=[Dtok])
                    P.op("dve", lambda e, Dp=Dp, sl=sl, xo=xo: e.tensor_tensor(out=xo[:, sl], in0=Dp[:], in1=gate[:, sl], op=ALU.mult),
                         reads=[Dtok, "gate"], writes=[(xotok, half)])
                    P.op("pool", lambda e, sl=sl, xo=xo, xr=xr: e.tensor_tensor(out=xo[:, sl], in0=xo[:, sl], in1=xr[:, sl], op=ALU.add),
                         reads=[(xotok, half), xrtok], writes=[(xotok, half)])
                if final:
                    fs, fstok = fss.next()
                    P.op("act", lambda e, xo=xo, fs=fs: e.activation(out=R["junk"][:], in_=xo[:], func=AF.Square, accum_out=fs[:, 0:1]),
                         reads=[(xotok, 0), (xotok, 1)], writes=["junk", (fstok, 0)])
                    P.op("act", lambda e, fs=fs: e.activation(out=fs[:, 1:2], in_=fs[:, 0:1], func=AF.Sqrt, scale=1.0 / D, bias=C.eps[:, 0:1]),
                         reads=[(fstok, 0), "eps"], writes=[(fstok, 1)])
                    P.op("dve", lambda e, fs=fs: e.reciprocal(out=fs[:, 2:3], in_=fs[:, 1:2]), reads=[(fstok, 1)], writes=[(fstok, 2)])
                    P.op("dve", lambda e, xo=xo, fs=fs: e.scalar_tensor_tensor(
                        out=xo[:], in0=xo[:], scalar=fs[:, 2:3], in1=gfin[:], op0=ALU.mult, op1=ALU.mult),
                        reads=[(xotok, 0), (xotok, 1), (fstok, 2), "gfin"], writes=[(xotok, 0), (xotok, 1)])
                P.dma("pool", dst_d[T * 128:(T + 1) * 128, :], xo[:], reads=[(xotok, 0), (xotok, 1)], writes=[("dst", T)])

        emit_rms(0)
        for bk in range(NB):
            emit_gu(bk)
            if bk + 1 < NB:
                emit_rms(bk + 1)
            emit_down(bk)
        P.barrier()


def phase4(C):
    nc, P, sb, ps = C.nc, C.P, C.sb, C.ps
    with ExitStack() as st:
        Fc = sb("Fc", [128, 2, 512], BF16, st=st)
        P.dma("pool", Fc[:], C.Fc_d, writes=["Fc"])
        R = {
            "xt": Ring(st, nc, "p4xt", 3, [128, D], F32),
            "ss": Ring(st, nc, "p4ss", 3, [128, 4], F32),
            "xn": Ring(st, nc, "p4xn", 2, [128, D], BF16),
            "tp": Ring(st, nc, "p4tp", 2, [128, D], BF16, psum=True),
            "junk": sb("p4junk", [128, D], BF16, st=st),
        }
        hTr = Ring(st, nc, "p4hT", 2, [128, 8, 128], BF16)
        Zr = Ring(st, nc, "Z_ps", 4, [128, 512], F32, psum=True)
        zsr = Ring(st, nc, "zs", 3, [128, 2, D], BF16)
        for T in range(NT):
            hT, hTtok = hTr.next()
            rms_stage(C, R, C.x2_d[T * 128:(T + 1) * 128, :], 3, hT, hTtok)
            zs, zstok = zsr.next()
            for g in range(4):
                Z, Ztok = Zr.next()
                for cc in range(2):
                    P.op("pe", lambda e, Z=Z, g=g, cc=cc: e.matmul(Z[:], lhsT=hT[:, 2 * g + cc, :], rhs=Fc[:, cc, :],
                                                                    start=(cc == 0), stop=(cc == 1)),
                         reads=[hTtok, "Fc"], writes=[Ztok])
                eng = "act" if g % 2 == 0 else "dve"
                if eng == "act":
                    P.op("act", lambda e, Z=Z, g=g, zs=zs: e.activation(
                        out=zs[:, :, g * 256:(g + 1) * 256], in_=Z[:].rearrange("p (r c) -> p r c", r=2), func=AF.Identity),
                        reads=[Ztok], writes=[(zstok, g)])
                else:
                    P.op("dve", lambda e, Z=Z, g=g, zs=zs: e.tensor_copy(
                        out=zs[:, :, g * 256:(g + 1) * 256], in_=Z[:].rearrange("p (r c) -> p r c", r=2)),
                        reads=[Ztok], writes=[(zstok, g)])
            P.dma("pool", C.Z_d[:, T * 128:(T + 1) * 128, :].rearrange("r t c -> t r c"), zs[:],
                  reads=[(zstok, g) for g in range(4)], writes=[("Z_d", T)])
        P.barrier()


def phase5(C):
    nc, P, sb, ps = C.nc, C.P, C.sb, C.ps
    NB2 = 8
    with ExitStack() as st:
        M1 = sb("M1", [128, 128], BF16, st=st)
        P.dma("pool", M1[:], C.M1_d, writes=["M1"])
        ztr = Ring(st, nc, "zt", 2, [128, NB2, D], BF16)
        t1r = Ring(st, nc, "t1s", 2, [128, NB2, D], BF16)
        Tr = Ring(st, nc, "T1_ps", 4, [128, 512], F32, psum=True)
        zv = C.Z_d.rearrange("r (n1 n2) c -> (r n1) n2 c", n2=64)
        k = 0
        for blk in range(64 // NB2):
            zt, zttok = ztr.next()
            P.dma("sp", zt[:], zv[:, blk * NB2:(blk + 1) * NB2, :], writes=[zttok])
            t1s, t1tok = t1r.next()
            for i in range(NB2):
                for half in range(2):
                    Tp, Ttok = Tr.next()
                    sl = slice(half * 512, (half + 1) * 512)
                    P.op("pe", lambda e, Tp=Tp, i=i, sl=sl: e.matmul(Tp[:], lhsT=M1[:], rhs=zt[:, i, sl], start=True, stop=True),
                         reads=["M1", zttok], writes=[Ttok])
                    if k % 2 == 0:
                        P.op("act", lambda e, Tp=Tp, i=i, sl=sl: e.activation(out=t1s[:, i, sl], in_=Tp[:], func=AF.Identity),
                             reads=[Ttok], writes=[(t1tok, i, half)])
                    else:
                        P.op("dve", lambda e, Tp=Tp, i=i, sl=sl: e.tensor_copy(out=t1s[:, i, sl], in_=Tp[:]),
                             reads=[Ttok], writes=[(t1tok, i, half)])
                    k += 1
            rd = [(t1tok, i, h) for i in range(NB2) for h in range(2)]
            for r in range(2):
                P.dma("pool", C.T1_d[r, blk * NB2:(blk + 1) * NB2, :, :].rearrange("n2 k1 c -> k1 n2 c"),
                      t1s[r * 64:(r + 1) * 64, :, :], reads=rd, writes=[("T1_d", blk, r)])
        P.barrier()


def phase6(C):
    nc, P, sb, ps = C.nc, C.P, C.sb, C.ps
    NB1 = 8
    with ExitStack() as st:
        M2 = sb("M2", [128, 64, 64], BF16, st=st)
        P.dma("pool", M2[:], C.M2_d, writes=["M2"])
        ttr = Ring(st, nc, "tt", 2, [128, NB1, D], BF16)
        fsr = Ring(st, nc, "fs", 2, [64, NB1, D], BF16)
        Fr = Ring(st, nc, "f_ps", 4, [64, 512], F32, psum=True)
        tv = C.T1_d.rearrange("r n2 k1 c -> (r n2) k1 c")
        fv = C.f_d.rearrange("(k2 k1) c -> k2 k1 c", k1=64)
        k = 0
        for blk in range(64 // NB1):
            tt, tttok = ttr.next()
            P.dma("sp", tt[:], tv[:, blk * NB1:(blk + 1) * NB1, :], writes=[tttok])
            fs, fstok = fsr.next()
            for i in range(NB1):
                k1 = blk * NB1 + i
                for half in range(2):
                    Fp, Ftok = Fr.next()
                    sl = slice(half * 512, (half + 1) * 512)
                    P.op("pe", lambda e, Fp=Fp, i=i, sl=sl, k1=k1: e.matmul(Fp[:], lhsT=M2[:, k1, :], rhs=tt[:, i, sl],
                                                                        start=True, stop=True),
                         reads=["M2", tttok], writes=[Ftok])
                    if k % 2 == 0:
                        P.op("act", lambda e, Fp=Fp, i=i, sl=sl: e.activation(out=fs[:, i, sl], in_=Fp[:], func=AF.Identity),
                             reads=[Ftok], writes=[(fstok, i, half)])
                    else:
                        P.op("dve", lambda e, Fp=Fp, i=i, sl=sl: e.tensor_copy(out=fs[:, i, sl], in_=Fp[:]),
                             reads=[Ftok], writes=[(fstok, i, half)])
                    k += 1
            P.dma("pool", fv[:, blk * NB1:(blk + 1) * NB1, :], fs[:],
                  reads=[(fstok, i, h) for i in range(NB1) for h in range(2)], writes=[("f_d", blk)])
        P.barrier()
    with ExitStack() as st:
        wf = sb("wf", [128, 8, D], BF16, st=st)
        wfv = C.wf_d.rearrange("(kc p) f -> p kc f", p=128)
        for kc in range(8):
            P.dma("pool", wf[:, kc, :], wfv[:, kc, :], writes=[("wf", kc)])
        gate = sb("gate_p6", [128, D], st=st)
        bfr = sb("bf_row", [128, D], st=st)
        P.dma("sp", gate[:], C.gates_d[2].partition_broadcast(128), writes=["gate"])
        P.dma("sp", bfr[:], C.bf_d.partition_broadcast(128), writes=["bfr"])
        ftr = Ring(st, nc, "ft", 3, [128, D], BF16)
        tpr = Ring(st, nc, "p6tp", 2, [128, D], BF16, psum=True)
        fTr = Ring(st, nc, "fT", 2, [128, 8, 128], BF16)
        xtr = Ring(st, nc, "p6xt", 3, [128, D], F32)
        xor_ = Ring(st, nc, "p6xo", 2, [128, D], F32)
        Wr = Ring(st, nc, "wf_ps", 4, [128, 512], F32, psum=True)
        for T in range(NT):
            ft, fttok = ftr.next()
            P.dma("sp", ft[:], C.f_d[T * 128:(T + 1) * 128, :], writes=[fttok])
            xt, xtok = xtr.next()
            P.dma("sp", xt[:], C.x2_d[T * 128:(T + 1) * 128, :], writes=[xtok])
            tp, tptok = tpr.next()
            for kc in range(8):
                P.op("pe", lambda e, kc=kc: e.transpose(out=tp[:, kc * 128:(kc + 1) * 128], in_=ft[:, kc * 128:(kc + 1) * 128],
                                                         identity=C.ident[:]),
                     reads=[fttok, "ident"], writes=[tptok])
            fT, fTtok = fTr.next()
            P.op("act", lambda e: e.activation(out=fT[:].rearrange("p k t -> p (k t)"), in_=tp[:], func=AF.Identity),
                 reads=[tptok], writes=[fTtok])
            xo, xotok = xor_.next()
            for half in range(2):
                Wp, Wtok = Wr.next()
                sl = slice(half * 512, (half + 1) * 512)
                for kc in range(8):
                    P.op("pe", lambda e, Wp=Wp, kc=kc, sl=sl: e.matmul(Wp[:], lhsT=fT[:, kc, :], rhs=wf[:, kc, sl],
                                                                       start=(kc == 0), stop=(kc == 7)),
                         reads=[fTtok, ("wf", kc)], writes=[Wtok])
                P.op("dve", lambda e, Wp=Wp, sl=sl: e.tensor_tensor(out=xo[:, sl], in0=Wp[:], in1=bfr[:, sl], op=ALU.add),
                     reads=[Wtok, "bfr"], writes=[(xotok, half)])
                P.op("pool", lambda e, sl=sl: e.tensor_tensor(out=xo[:, sl], in0=xo[:, sl], in1=gate[:, sl], op=ALU.mult),
                     reads=[(xotok, half), "gate"], writes=[(xotok, half)])
                P.op("pool", lambda e, sl=sl: e.tensor_tensor(out=xo[:, sl], in0=xo[:, sl], in1=xt[:, sl], op=ALU.add),
                     reads=[(xotok, half), xtok], writes=[(xotok, half)])
            P.dma("pool", C.x3_d[T * 128:(T + 1) * 128, :], xo[:], reads=[(xotok, 0), (xotok, 1)], writes=[("x3_d", T)])
        P.barrier()


def _dft_tables():
    c = np.arange(256, dtype=np.float64)
    ang = 2 * np.pi * np.outer(c, c) / 256.0
    Cc, Sc = np.cos(ang) / 16.0, np.sin(ang) / 16.0
    fc = np.concatenate([Cc, -Sc], axis=1)
    fc = fc.reshape(2, 128, 512).transpose(1, 0, 2)
    n = np.arange(64, dtype=np.float64)
    a1 = 2 * np.pi * np.outer(n, n) / 64.0
    C1, S1 = np.cos(a1) / 8.0, np.sin(a1) / 8.0
    m1 = np.block([[C1, -S1], [S1, C1]])
    n2 = n[:, None, None]; k1 = n[None, :, None]; k2 = n[None, None, :]
    th = 2 * np.pi * (n2 * k2 / 64.0 + n2 * k1 / 4096.0)
    m2 = np.concatenate([np.cos(th), np.sin(th)], axis=0) / 8.0
    f32 = lambda a: np.ascontiguousarray(a.astype(np.float32))
    return f32(fc), f32(m1), f32(m2)


def _rope_tables():
    rows = SEQ // 64
    row = np.repeat(np.arange(rows), 64).astype(np.float32)
    col = np.tile(np.arange(64), rows).astype(np.float32)
    inv_freq = (np.float32(10000.0) ** (-np.arange(32, dtype=np.float32) / np.float32(32))).astype(np.float32)
    ang = np.concatenate([row[:, None] * inv_freq, col[:, None] * inv_freq], axis=-1).astype(np.float32)
    return np.cos(ang).astype(np.float32), np.sin(ang).astype(np.float32)


def _host_inputs(inputs):
    f = lambda a: np.ascontiguousarray(np.asarray(a, dtype=np.float32))
    x = f(inputs["x"]); c = f(inputs["c"]); ctx = f(inputs["ctx"]); c_ctx = f(inputs["c_ctx"])
    w_mod = f(inputs["w_mod"]); b_mod = f(inputs["b_mod"])
    pp = lambda v: np.ascontiguousarray(v.reshape(-1, 128).T)
    b_mod_pp = np.stack([pp(b_mod[l]) for l in range(2)])
    g_mix_pp = np.stack([pp(f(inputs["g_mix"])[l]) for l in range(2)])
    g_ffn_pp = np.stack([pp(f(inputs["g_ffn"])[l]) for l in range(2)])
    cos, sin = _rope_tables()
    shared = {
        "w_mod": w_mod, "b_mod_pp": b_mod_pp, "b_mod": b_mod, "g_mix_pp": g_mix_pp, "g_ffn_pp": g_ffn_pp,
        "w_qkv": f(inputs["w_qkv"])[0], "g_q": f(inputs["g_q"])[0], "g_k": f(inputs["g_k"])[0],
        "w_o": f(inputs["w_attn_out"])[0], "rope_cos": cos, "rope_sin": sin,
        "ident": np.eye(128, dtype=np.float32),
        "w_gate_up": f(inputs["w_gate_up"]), "w_down": f(inputs["w_down"]),
        "w_fourier": f(inputs["w_fourier"])[0], "b_fourier": f(inputs["b_fourier"])[0],
        "g_final": f(inputs["g_final"]),
    }
    shared["dft_fc"], shared["dft_m1"], shared["dft_m2"] = _dft_tables()
    maps = []
    for b in range(NCORES):
        c_pp = np.ascontiguousarray(np.stack([pp(c[b]), pp(c_ctx)], axis=-1))
        m = {"x": x[b], "ctx": ctx[b], "c_pp": c_pp}
        m.update(shared)
        maps.append(m)
    return maps


def kernel(**inputs):
    nc = build_program()
    maps = _host_inputs(inputs)
    res = run_bass_kernel_spmd(nc, maps, core_ids=list(range(NCORES)))
    return np.stack([np.asarray(r["out"], dtype=np.float32) for r in res.results], axis=0)
```

```python
import math
from contextlib import ExitStack

import numpy as np
import concourse.bass as bass
import concourse.mybir as mybir
from concourse.bass_utils import run_bass_kernel_spmd

F32 = mybir.dt.float32
BF16 = mybir.dt.bfloat16
AF = mybir.ActivationFunctionType
ALU = mybir.AluOpType
AX = mybir.AxisListType

D = 1024
SEQ = 4096
CTX = 256
NH = 8
NKV = 2
HD = 128
DFF = 2816
NFC = DFF // 128
NT = SEQ // 128
NTC = CTX // 128
NKT = NT + NTC
EPS = 1e-6
NCORES = 8


class Prog:
    def __init__(self, nc, stack):
        self.nc = nc
        self.eng = {"pe": nc.tensor, "act": nc.scalar, "dve": nc.vector, "pool": nc.gpsimd, "sp": nc.sync}
        self.semh = {}
        self.cnt = {}
        for e in self.eng:
            self.semh[e] = stack.enter_context(nc.semaphore("sem_" + e))
            self.cnt[e] = 0
        self.dq = {}
        for q, n in (("sp", 12), ("pool", 8), ("act", 4)):
            keys = []
            for i in range(n):
                k = "dma_%s_%d" % (q, i)
                self.semh[k] = stack.enter_context(nc.semaphore(k))
                self.cnt[k] = 0
                keys.append(k)
            self.dq[q] = {"keys": keys, "i": 0}
        self.known = {e: {} for e in self.eng}
        self.tok = {}
        self.ninst = 0

    def _need(self, e, reads, writes):
        need = {}

        def add(ev, same_ok):
            if ev is None:
                return
            sk, v = ev
            if sk == e and same_ok:
                return
            if need.get(sk, 0) < v:
                need[sk] = v

        for t in reads:
            st = self.tok.get(t)
            if st is not None:
                add(st["w"], False)
        for t in writes:
            st = self.tok.get(t)
            if st is not None:
                add(st["w"], True)
                for sk, v in st["r"].items():
                    add((sk, v), True)
        return need

    def _wait(self, e, need):
        eng = self.eng[e]
        kn = self.known[e]
        for sk, v in need.items():
            if kn.get(sk, 0) < v:
                eng.wait_ge(self.semh[sk], v)
                kn[sk] = v
                self.ninst += 1

    def _record(self, ev, reads, writes):
        for t in reads:
            st = self.tok.setdefault(t, {"w": None, "r": {}})
            if st["r"].get(ev[0], 0) < ev[1]:
                st["r"][ev[0]] = ev[1]
        for t in writes:
            self.tok[t] = {"w": ev, "r": {}}

    def op(self, e, fn, reads=(), writes=()):
        self._wait(e, self._need(e, reads, writes))
        inst = fn(self.eng[e])
        self.cnt[e] += 1
        inst.then_inc(self.semh[e], 1)
        self.ninst += 1
        self._record((e, self.cnt[e]), reads, writes)

    def dma(self, q, out, in_, reads=(), writes=(), **kw):
        dq = self.dq[q]
        k = dq["keys"][dq["i"] % len(dq["keys"])]
        dq["i"] += 1
        need = self._need(q, reads, writes)
        if self.cnt[k] > 0 and need.get(k, 0) < self.cnt[k]:
            need[k] = self.cnt[k]
        self._wait(q, need)
        inst = self.eng[q].dma_start(out=out, in_=in_, **kw)
        self.cnt[k] += 16
        inst.then_inc(self.semh[k], 16)
        self.ninst += 1
        self._record((k, self.cnt[k]), reads, writes)

    def barrier(self):
        for e in self.eng:
            need = {sk: v for sk, v in self.cnt.items() if v > 0}
            self._wait(e, need)
        self.tok = {}


class Ring:
    uid = 0

    def __init__(self, stack, nc, name, n, shape, dtype, psum=False):
        self.name = name
        self.n = n
        self.i = -1
        alloc = nc.psum_tensor if psum else nc.sbuf_tensor
        Ring.uid += 1
        self.tiles = [stack.enter_context(alloc("r%d_%s%d" % (Ring.uid, name, i), shape, dtype)) for i in range(n)]

    def next(self):
        self.i += 1
        s = self.i % self.n
        return self.tiles[s], (self.name, s)


class NS:
    pass


SCALE = float(HD) ** -0.5


def build_program(stop_after=None):
    nc = bass.Bass("TRN2", target_bir_lowering=False)
    C = NS()
    C.nc = nc
    C.stop_after = stop_after
    din = lambda name, shape, dt=F32: nc.dram_tensor(name, shape, dt, kind="ExternalInput").ap()
    dscr = lambda name, shape, dt=F32: nc.dram_tensor(name, shape, dt, kind="Internal").ap()
    C.x_d = din("x", [SEQ, D])
    C.ctx_d = din("ctx", [CTX, D])
    C.cpp_d = din("c_pp", [128, 8, 2])
    C.wmod_d = din("w_mod", [2, D, 6 * D])
    C.bmodpp_d = din("b_mod_pp", [2, 128, 48])
    C.bmod_d = din("b_mod", [2, 6 * D])
    C.gmixpp_d = din("g_mix_pp", [2, 128, 8])
    C.gffnpp_d = din("g_ffn_pp", [2, 128, 8])
    C.wqkv_d = din("w_qkv", [D, 1536])
    C.gq_d = din("g_q", [128])
    C.gk_d = din("g_k", [128])
    C.wo_d = din("w_o", [D, D])
    C.cos_d = din("rope_cos", [SEQ, 64])
    C.sin_d = din("rope_sin", [SEQ, 64])
    C.ident_d = din("ident", [128, 128])
    C.out_d = nc.dram_tensor("out", [SEQ, D], F32, kind="ExternalOutput").ap()
    C.QT_d = dscr("QT_scr", [NT, 128, 1024], BF16)
    C.gates_d = dscr("gates_scr", [4, 1024])
    C.wgu_d = din("w_gate_up", [2, D, 2 * DFF])
    C.wd_d = din("w_down", [2, DFF, D])
    C.wf_d = din("w_fourier", [D, D])
    C.bf_d = din("b_fourier", [D])
    C.gfin_d = din("g_final", [D])
    C.Fc_d = din("dft_fc", [128, 2, 512])
    C.M1_d = din("dft_m1", [128, 128])
    C.M2_d = din("dft_m2", [128, 64, 64])
    C.Z_d = dscr("Z_scr", [2, SEQ, D], BF16)
    C.T1_d = dscr("T1_scr", [2, 64, 64, D], BF16)
    C.f_d = dscr("f_scr", [SEQ, D], BF16)
    names = ["x1", "x2", "x3"]
    for i, nm in enumerate(names):
        setattr(C, nm + "_d", C.out_d if stop_after == "p%d" % (i + 2) and False else dscr(nm + "_scr", [SEQ, D]))
    if stop_after == "p2":
        C.x1_d = C.out_d
    if stop_after == "p3":
        C.x2_d = C.out_d
    if stop_after == "p6":
        C.x3_d = C.out_d

    with ExitStack() as gs:
        P = Prog(nc, gs)
        C.P = P
        C.gs = gs
        uid = [0]

        def _alloc(fn, pre, name, shape, dt, st):
            uid[0] += 1
            return st.enter_context(fn("%s%d_%s" % (pre, uid[0], name), shape, dt))

        C.sb = lambda name, shape, dt=F32, st=gs: _alloc(nc.sbuf_tensor, "sb", name, shape, dt, st)
        C.ps = lambda name, shape, dt=F32, st=gs: _alloc(nc.psum_tensor, "ps", name, shape, dt, st)
        sb = C.sb
        C.modpp = sb("modpp", [128, 2, 4, 8, 2])
        C.gmix = sb("gmix", [128, 2, 8])
        C.gffn = sb("gffn", [128, 2, 8])
        C.Amod = sb("Amod", [128, 5, 8])
        C.Bmod = sb("Bmod", [128, 5, 8])
        C.ident_f = sb("ident_f", [128, 128])
        C.ident = sb("ident", [128, 128], BF16)
        C.ones = sb("ones", [128, 128], BF16)
        P.dma("sp", C.ident_f[:], C.ident_d, writes=["ident_f"])
        P.op("dve", lambda e: e.tensor_copy(out=C.ident[:], in_=C.ident_f[:]), reads=["ident_f"], writes=["ident"])
        P.op("dve", lambda e: e.memset(C.ones[:], 1.0), writes=["ones"])
        C.eps = sb("eps", [128, 1])
        P.op("dve", lambda e: e.memset(C.eps[:], EPS), writes=["eps"])

        phase0_setup(C)
        with ExitStack() as st12:
            C.KT = sb("KT", [128, NKV, NKT * 128], BF16, st=st12)
            C.Vs = sb("Vs", [128, NKT, NKV * HD], BF16, st=st12)
            phase1(C)
            phase2(C)
        if stop_after == "p2":
            return nc
        ffn_phase(C, 0, C.x1_d, C.x2_d)
        if stop_after == "p3":
            return nc
        phase4(C)
        phase5(C)
        phase6(C)
        if stop_after == "p6":
            return nc
        ffn_phase(C, 1, C.x3_d, C.out_d, final=True)
        print("instructions:", P.ninst)
    return nc


def rms_stage(C, R, src_ap, a_idx, hT, hTtok, col0=0):
    P = C.P
    xt, xtok = R["xt"].next()
    P.dma("sp", xt[:], src_ap, writes=[xtok])
    rms_from_tile(C, R, xt, xtok, a_idx, hT, hTtok, col0)
    return xt, xtok


def rms_from_tile(C, R, xt, xtok, a_idx, hT, hTtok, col0=0):
    P = C.P
    ss, sstok = R["ss"].next()
    xn, xntok = R["xn"].next()
    tp, tptok = R["tp"].next()
    junk = R["junk"]
    P.op("act", lambda e: e.activation(out=junk[:], in_=xt[:], func=AF.Square, accum_out=ss[:, 0:1]),
         reads=[xtok], writes=["junk", (sstok, 0)])
    P.op("act", lambda e: e.activation(out=ss[:, 1:2], in_=ss[:, 0:1], func=AF.Sqrt, scale=1.0 / D, bias=C.eps[:, 0:1]),
         reads=[(sstok, 0), "eps"], writes=[(sstok, 1)])
    P.op("dve", lambda e: e.reciprocal(out=ss[:, 2:3], in_=ss[:, 1:2]), reads=[(sstok, 1)], writes=[(sstok, 2)])
    P.op("act", lambda e: e.activation(out=xn[:], in_=xt[:], func=AF.Identity, scale=ss[:, 2:3]),
         reads=[xtok, (sstok, 2)], writes=[xntok])
    for kc in range(8):
        P.op("pe", lambda e, kc=kc: e.transpose(out=tp[:, kc * 128:(kc + 1) * 128], in_=xn[:, kc * 128:(kc + 1) * 128],
                                                 identity=C.ident[:]),
             reads=[xntok, "ident"], writes=[tptok])
    for kc in range(8):
        P.op("dve", lambda e, kc=kc: e.tensor_scalar(
            out=hT[:, kc, col0:col0 + 128], in0=tp[:, kc * 128:(kc + 1) * 128],
            scalar1=C.Amod[:, a_idx, kc:kc + 1], scalar2=C.Bmod[:, a_idx, kc:kc + 1], op0=ALU.mult, op1=ALU.add),
            reads=[tptok, ("Amod", a_idx), ("Bmod", a_idx)], writes=[hTtok])


def load_weight_bf16(C, st, name, dst, dst_tok_fn, src_view, nchunks, chunk_shape, engines=("dve", "pool")):
    P, nc = C.P, C.nc
    ring = Ring(st, nc, name + "_stg", 2, chunk_shape, F32)
    for i in range(nchunks):
        t, tok = ring.next()
        P.dma("sp", t[:], src_view(i), writes=[tok])
        eng = engines[i % len(engines)]
        P.op(eng, lambda e, t=t, i=i: e.tensor_copy(out=dst(i), in_=t[:]), reads=[tok], writes=[dst_tok_fn(i)])


def phase0_setup(C):
    nc, P, sb, ps = C.nc, C.P, C.sb, C.ps
    cpp = sb("cpp", [128, 8, 2])
    sig = sb("sig", [128, 8, 2])
    sc = sb("sc", [128, 8, 2])
    C.scb = sb("scb", [128, 8, 2], BF16)
    C.bpp = sb("bpp", [128, 2, 48])
    P.dma("sp", cpp[:], C.cpp_d, writes=["cpp"])
    P.dma("sp", C.bpp[:], C.bmodpp_d.rearrange("l p f -> p l f"), writes=["bpp"])
    P.dma("sp", C.gmix[:], C.gmixpp_d.rearrange("l p f -> p l f"), writes=["gmix"])
    P.dma("sp", C.gffn[:], C.gffnpp_d.rearrange("l p f -> p l f"), writes=["gffn"])
    P.op("act", lambda e: e.activation(out=sig[:], in_=cpp[:], func=AF.Sigmoid), reads=["cpp"], writes=["sig"])
    P.op("dve", lambda e: e.tensor_tensor(out=sc[:], in0=cpp[:], in1=sig[:], op=ALU.mult),
         reads=["cpp", "sig"], writes=["sc"])
    P.op("dve", lambda e: e.tensor_copy(out=C.scb[:], in_=sc[:]), reads=["sc"], writes=["scb"])
    C.scbb = sb("scbb", [128, 8, 128], BF16)
    P.op("dve", lambda e: e.tensor_copy(out=C.scbb[:], in_=sc[:, :, 0:1].to_broadcast([128, 8, 128])),
         reads=["sc"], writes=["scbb"])


def mod_items(C, st, specs):
    nc, P, sb, ps = C.nc, C.P, C.sb, C.ps
    wring = Ring(st, nc, "wmod", 3, [128, 8, 512], BF16)
    bg_ps = ps("bg_ps", [128, 512], st=st)
    grow = Ring(st, nc, "grow", 2, [1, 512], F32)
    brow = Ring(st, nc, "browr", 3, [1, 512], F32)
    wv = C.wmod_d.rearrange("l (kc p) f -> l p kc f", p=128)
    items = []
    combos = {(0, 1): [(0, 0, 1, 0, C.gmix, 0), (1, 0, 1, 0, C.gmix, 1)], (0, 4): [(2, 0, 3, 2, C.gffn, 0)],
              (1, 1): [(3, 1, 1, 0, C.gmix, 0)], (1, 4): [(4, 1, 3, 2, C.gffn, 0)]}

    def make(l, m, half):
        stt = {}
        c0 = m * 1024 + half * 512

        def dma():
            stt["wt"], stt["wtok"] = wring.next()
            P.dma("pool", stt["wt"][:], wv[l, :, :, c0:c0 + 512], writes=[stt["wtok"]])
            if m in (2, 5):
                stt["br"], stt["brtok"] = brow.next()
                P.dma("sp", stt["br"][:], C.bmod_d[l:l + 1, c0:c0 + 512], writes=[stt["brtok"]])

        def item():
            wt, wtok = stt["wt"], stt["wtok"]
            if m in (2, 5):
                gi = l * 2 + (0 if m == 2 else 1)
                br, brtok = stt["br"], stt["brtok"]
                for kc in range(8):
                    P.op("pe", lambda e, kc=kc: e.matmul(bg_ps[:, :], lhsT=C.scbb[:, kc, :], rhs=wt[:, kc, :],
                                                         start=(kc == 0), stop=(kc == 7)),
                         reads=["scbb", wtok], writes=["bg_ps"])
                gr, grtok = grow.next()
                P.op("dve", lambda e: e.tensor_tensor(out=gr[:], in0=bg_ps[0:1, :], in1=br[:], op=ALU.add),
                     reads=["bg_ps", brtok], writes=[grtok])
                P.dma("sp", C.gates_d[gi:gi + 1, half * 512:(half + 1) * 512], gr[:], reads=[grtok],
                      writes=[("gates_d", gi, half)])
            else:
                mi = {0: 0, 1: 1, 3: 2, 4: 3}[m]
                for f4 in range(4):
                    for kc in range(8):
                        P.op("pe", lambda e, kc=kc, f4=f4: e.matmul(
                            bg_ps[:, 2 * f4:2 * f4 + 2], lhsT=wt[:, kc, f4 * 128:(f4 + 1) * 128], rhs=C.scb[:, kc, :],
                            start=(kc == 0), stop=(kc == 7)),
                            reads=["scb", wtok], writes=["bg_ps"])
                P.op("dve", lambda e: e.tensor_tensor(
                    out=C.modpp[:, l, mi, half * 4:(half + 1) * 4, :],
                    in0=bg_ps[:, 0:8].rearrange("p (f t) -> p f t", t=2),
                    in1=C.bpp[:, l, m * 8 + half * 4:m * 8 + half * 4 + 4].unsqueeze(2).to_broadcast([128, 4, 2]), op=ALU.add),
                    reads=["bg_ps", "bpp"], writes=[("modpp", l, mi, half)])
                if half == 1 and (l, m) in combos:
                    for idx, l_, m_sc, m_sh, g, col in combos[(l, m)]:
                        P.op("dve", lambda e, idx=idx, m_sc=m_sc, g=g, col=col: e.scalar_tensor_tensor(
                            out=C.Amod[:, idx, :], in0=C.modpp[:, l, m_sc, :, col], scalar=1.0, in1=g[:, l, :],
                            op0=ALU.add, op1=ALU.mult),
                            reads=[("modpp", l, m_sc, 0), ("modpp", l, m_sc, 1), "gmix", "gffn"], writes=[("Amod", idx)])
                        P.op("dve", lambda e, idx=idx, m_sh=m_sh, col=col: e.tensor_copy(
                            out=C.Bmod[:, idx, :], in_=C.modpp[:, l, m_sh, :, col]),
                            reads=[("modpp", l, m_sh, 0), ("modpp", l, m_sh, 1)], writes=[("Bmod", idx)])
        return (dma, item)

    for (l, m) in specs:
        for half in range(2):
            items.append(make(l, m, half))
    return items


def phase1(C):
    nc, P, sb, ps = C.nc, C.P, C.sb, C.ps
    with ExitStack() as st:
        all_items = mod_items(C, st, [(0, 0), (0, 1), (0, 3), (0, 4), (0, 2), (0, 5), (1, 0), (1, 1), (1, 3), (1, 4), (1, 2), (1, 5)])
        items_bg = None
        wqkv = sb("wqkv", [128, 8, 1536], BF16, st=st)
        cosb = sb("cosb", [128, NT, 64], BF16, st=st)
        sinb = sb("sinb", [128, NT, 64], BF16, st=st)
        gqk = sb("gqk", [128, 10, 128], st=st)
        for d_, _ in all_items[:3]:
            d_()
        ndma = [3]
        nrun = [0]

        def run_item():
            all_items[nrun[0]][1]()
            nrun[0] += 1
            if ndma[0] < len(all_items):
                all_items[ndma[0]][0]()
                ndma[0] += 1

        for _ in range(4):
            run_item()
        P.dma("pool", cosb[:], C.cos_d.rearrange("(t p) f -> p t f", p=128), writes=["cos"])
        P.dma("pool", sinb[:], C.sin_d.rearrange("(t p) f -> p t f", p=128), writes=["sin"])
        for h in range(10):
            P.dma("sp", gqk[:, h, :], (C.gq_d if h < 8 else C.gk_d).partition_broadcast(128), writes=["gqk"])
        wv = C.wqkv_d.rearrange("(kc p) f -> p kc f", p=128)
        for kc in range(8):
            P.dma("pool", wqkv[:, kc, :], wv[:, kc, :], writes=[("wqkv", kc)])
        xtr = Ring(st, nc, "xt", 3, [128, D], F32)
        ssr = Ring(st, nc, "ss", 3, [128, 4], F32)
        xnr = Ring(st, nc, "xn", 2, [128, D], BF16)
        tpr = Ring(st, nc, "tp", 2, [128, D], BF16, psum=True)
        junk = sb("junk", [128, D], BF16, st=st)
        hTr = Ring(st, nc, "hT", 2, [128, 8, 128], BF16)
        qkv_ps = [ps("qkv_ps%d" % i, [128, 512], st=st) for i in range(3)]
        qkvr = Ring(st, nc, "qkv_sb", 3, [128, 1536], F32)
        ssq = Ring(st, nc, "ssq", 3, [128, 3, 10], F32)
        qg = sb("qg", [128, 10, 128], BF16, st=st)
        t1 = sb("t1", [128, 10, 64], BF16, st=st)
        t2 = sb("t2", [128, 10, 64], BF16, st=st)
        t3 = sb("t3", [128, 10, 64], BF16, st=st)
        t4 = sb("t4", [128, 10, 64], BF16, st=st)
        qrr = Ring(st, nc, "qr", 2, [128, 10, 128], BF16)
        qT_ps = ps("qT_ps", [128, 1024], BF16, st=st)
        kT_ps = ps("kT_ps", [128, 1024], BF16, st=st)
        qTr = Ring(st, nc, "qT_sb", 2, [128, 1024], BF16)
        items_bg = all_items[4:]

        def run_bg():
            items_bg.pop(0)
            run_item()
        print("p1 sbuf remaining", nc.sbuf_bytes_remaining)
        info = {}
        lat = lambda T: T < NT

        def sA(T):
            src = C.x_d[T * 128:(T + 1) * 128, :] if lat(T) else C.ctx_d[(T - NT) * 128:(T - NT + 1) * 128, :]
            xt, xtok = xtr.next()
            P.dma("sp", xt[:], src, writes=[xtok])
            info[T] = dict(xt=xt, xtok=xtok)

        def sB(T):
            s = info[T]
            xt, xtok = s["xt"], s["xtok"]
            ss, sstok = ssr.next()
            xn, xntok = xnr.next()
            P.op("act", lambda e: e.activation(out=junk[:], in_=xt[:], func=AF.Square, accum_out=ss[:, 0:1]),
                 reads=[xtok], writes=["junk", (sstok, 0)])
            P.op("act", lambda e: e.activation(out=ss[:, 1:2], in_=ss[:, 0:1], func=AF.Ln, scale=1.0 / D, bias=C.eps[:, 0:1]),
                 reads=[(sstok, 0), "eps"], writes=[(sstok, 1)])
            P.op("act", lambda e: e.activation(out=ss[:, 2:3], in_=ss[:, 1:2], func=AF.Exp, scale=-0.5),
                 reads=[(sstok, 1)], writes=[(sstok, 2)])
            P.op("act", lambda e: e.activation(out=xn[:], in_=xt[:], func=AF.Identity, scale=ss[:, 2:3]),
                 reads=[xtok, (sstok, 2)], writes=[xntok])
            s.update(xn=xn, xntok=xntok)

        def sE(T):
            s = info[T]
            tp, tptok = tpr.next()
            for kc in range(8):
                P.op("pe", lambda e, kc=kc: e.transpose(out=tp[:, kc * 128:(kc + 1) * 128],
                                                         in_=s["xn"][:, kc * 128:(kc + 1) * 128], identity=C.ident[:]),
                     reads=[s["xntok"], "ident"], writes=[tptok])
            s.update(tp=tp, tptok=tptok)

        def sF(T):
            s = info[T]
            a_idx = 0 if lat(T) else 1
            hT, hTtok = hTr.next()
            for kc in range(8):
                P.op("dve", lambda e, kc=kc: e.tensor_scalar(
                    out=hT[:, kc, :], in0=s["tp"][:, kc * 128:(kc + 1) * 128],
                    scalar1=C.Amod[:, a_idx, kc:kc + 1], scalar2=C.Bmod[:, a_idx, kc:kc + 1], op0=ALU.mult, op1=ALU.add),
                    reads=[s["tptok"], ("Amod", a_idx), ("Bmod", a_idx)], writes=[hTtok])
            s.update(hT=hT, hTtok=hTtok)

        def sG(T):
            s = info[T]
            banks = (0, 1, 2) if lat(T) else (2,)
            for nb in banks:
                for kc in range(8):
                    P.op("pe", lambda e, nb=nb, kc=kc: e.matmul(
                        qkv_ps[nb][:], lhsT=s["hT"][:, kc, :], rhs=wqkv[:, kc, nb * 512:(nb + 1) * 512],
                        start=(kc == 0), stop=(kc == 7)),
                        reads=[s["hTtok"], ("wqkv", kc)], writes=[("qkv_ps", nb)])

        def sH(T):
            s = info[T]
            banks = (0, 1, 2) if lat(T) else (2,)
            h0 = 0 if lat(T) else 8
            qs, qstok = qkvr.next()
            for nb in banks:
                P.op("act", lambda e, nb=nb: e.activation(out=qs[:, nb * 512:(nb + 1) * 512], in_=qkv_ps[nb][:],
                                                          func=AF.Identity),
                     reads=[("qkv_ps", nb)], writes=[(qstok, nb)])
            sq3, sqtok = ssq.next()
            for h in range(h0, 10):
                P.op("act", lambda e, h=h: e.activation(out=junk[:, 0:128], in_=qs[:, h * 128:(h + 1) * 128], func=AF.Square,
                                                        accum_out=sq3[:, 0, h:h + 1]),
                     reads=[(qstok, h // 4)], writes=["junk", (sqtok, 0, h)])
            P.op("act", lambda e: e.activation(out=sq3[:, 1, h0:10], in_=sq3[:, 0, h0:10], func=AF.Ln,
                                               scale=1.0 / HD, bias=C.eps[:, 0:1]),
                 reads=[(sqtok, 0, h) for h in range(h0, 10)] + ["eps"], writes=[(sqtok, 1)])
            P.op("act", lambda e: e.activation(out=sq3[:, 2, h0:10], in_=sq3[:, 1, h0:10], func=AF.Exp, scale=-0.5),
                 reads=[(sqtok, 1)], writes=[(sqtok, 2)])
            s.update(qs=qs, qstok=qstok, sq3=sq3, sqtok=sqtok)

        def sK(T):
            s = info[T]
            qs, qstok, sq3, sqtok = s["qs"], s["qstok"], s["sq3"], s["sqtok"]
            h0 = 0 if lat(T) else 8
            rd = [(qstok, nb) for nb in ((0, 1, 2) if lat(T) else (2,))]
            P.op("pool", lambda e: e.tensor_copy(out=C.Vs[:, T, :], in_=qs[:, 1280:1536]), reads=[(qstok, 2)],
                 writes=[("V", T)])
            for h in range(h0, 10):
                P.op("dve", lambda e, h=h: e.scalar_tensor_tensor(
                    out=qg[:, h, :], in0=qs[:, h * 128:(h + 1) * 128], scalar=sq3[:, 2, h:h + 1], in1=gqk[:, h, :],
                    op0=ALU.mult, op1=ALU.mult),
                    reads=rd + [(sqtok, 2), "gqk"], writes=[("qg", h)])
            qr, qrtok = qrr.next()
            if lat(T):
                cb = cosb[:, T, :].unsqueeze(1).to_broadcast([128, 10, 64])
                sbb = sinb[:, T, :].unsqueeze(1).to_broadcast([128, 10, 64])
                x1 = qg[:, :, 0:64]
                x2 = qg[:, :, 64:128]
                qall = [("qg", h) for h in range(10)]
                P.op("dve", lambda e: e.tensor_tensor(out=t1[:], in0=x1, in1=cb, op=ALU.mult), reads=qall + ["cos"], writes=["t1"])
                P.op("dve", lambda e: e.tensor_tensor(out=t2[:], in0=x2, in1=sbb, op=ALU.mult), reads=qall + ["sin"], writes=["t2"])
                P.op("dve", lambda e: e.tensor_tensor(out=t3[:], in0=x1, in1=sbb, op=ALU.mult), reads=qall + ["sin"], writes=["t3"])
                P.op("dve", lambda e: e.tensor_tensor(out=t4[:], in0=x2, in1=cb, op=ALU.mult), reads=qall + ["cos"], writes=["t4"])
                P.op("dve", lambda e: e.tensor_tensor(out=qr[:, :, 0:64], in0=t1[:], in1=t2[:], op=ALU.subtract),
                     reads=["t1", "t2"], writes=[(qrtok, 0)])
                P.op("dve", lambda e: e.tensor_tensor(out=qr[:, :, 64:128], in0=t3[:], in1=t4[:], op=ALU.add),
                     reads=["t3", "t4"], writes=[(qrtok, 1)])
            else:
                P.op("dve", lambda e: e.tensor_copy(out=qr[:, 8:10, :], in_=qg[:, 8:10, :]),
                     reads=[("qg", 8), ("qg", 9)], writes=[(qrtok, 0), (qrtok, 1)])
            s.update(qr=qr, qrtok=qrtok)

        def sL(T):
            s = info[T]
            qr, rd = s["qr"], [(s["qrtok"], 0), (s["qrtok"], 1)]
            if lat(T):
                for h in range(8):
                    P.op("pe", lambda e, h=h: e.transpose(out=qT_ps[:, h * 128:(h + 1) * 128], in_=qr[:, h, :],
                                                           identity=C.ident[:]),
                         reads=rd + ["ident"], writes=["qT_ps"])
            for j in range(2):
                P.op("pe", lambda e, j=j: e.transpose(out=kT_ps[:, j * 128:(j + 1) * 128], in_=qr[:, 8 + j, :],
                                                       identity=C.ident[:]),
                     reads=rd + ["ident"], writes=["kT_ps"])

        def sM(T):
            if lat(T):
                qT, qTtok = qTr.next()
                P.op("act", lambda e: e.activation(out=qT[:], in_=qT_ps[:], func=AF.Identity), reads=["qT_ps"], writes=[qTtok])
                P.dma("pool", C.QT_d[T], qT[:], reads=[qTtok], writes=[("QT_d", T)])
            P.op("dve", lambda e: e.tensor_copy(
                out=C.KT[:, :, T * 128:(T + 1) * 128], in_=kT_ps[:, 0:256].rearrange("p (j t) -> p j t", t=128)),
                reads=["kT_ps"], writes=[("KT", T)])
            del info[T]

        order = [(sH, 5), (sM, 8), (sL, 7), (sK, 6), (sG, 4), (sF, 3), (sE, 2), (sB, 1), (sA, 0)]
        nit = NKT + 8
        for i in range(nit):
            for fn, skew in order:
                T = i - skew
                if 0 <= T < NKT:
                    fn(T)
            if items_bg and i >= 2:
                run_bg()
        while items_bg:
            run_bg()
        P.barrier()


def phase2(C):
    nc, P, sb, ps = C.nc, C.P, C.sb, C.ps
    NKP = NKT // 2
    with ExitStack() as st:
        wo = sb("wo", [128, 8, 1024], BF16, st=st)
        wv = C.wo_d.rearrange("(h p) f -> p h f", p=128)
        for h in range(8):
            P.dma("pool", wo[:, h, :], wv[:, h, :], writes=[("wo", h)])
        qTr = Ring(st, nc, "qT2", 2, [128, 1024], BF16)
        xtr = Ring(st, nc, "xt2", 3, [128, D], F32)
        PTr = Ring(st, nc, "PT", 10, [128, 1024], BF16)
        saccr = Ring(st, nc, "sacc", 2, [128, 2, 1024], BF16)
        Sr = Ring(st, nc, "S_ps", 2, [128, 1024], F32, psum=True)
        accr = Ring(st, nc, "acc_ps", 2, [128, 512], F32, psum=True)
        den_ps = ps("den_ps", [128, 512], st=st)
        wo_ps = ps("wo_ps", [128, 512], st=st)
        recr = Ring(st, nc, "rec", 2, [128, 512], F32)
        OTr = Ring(st, nc, "OT", 2, [128, 8, 128], BF16)
        x1r = Ring(st, nc, "x1t", 2, [128, D], F32)
        gate = sb("gate_p2", [128, D], st=st)
        P.dma("sp", gate[:], C.gates_d[0].partition_broadcast(128), writes=["gate"])

        steps = [(T, kvh, p) for T in range(NT) for kvh in range(NKV) for p in range(NKP)]
        state = {}
        pending = []

        def load_tile(T):
            qT, qTtok = qTr.next()
            xt, xtok = xtr.next()
            P.dma("sp", qT[:], C.QT_d[T], writes=[qTtok])
            P.dma("sp", xt[:], C.x_d[T * 128:(T + 1) * 128, :], writes=[xtok])
            OT, OTtok = OTr.next()
            state[T] = dict(qT=qT, qTtok=qTtok, xt=xt, xtok=xtok, OT=OT, OTtok=OTtok)

        def emit_S(i):
            T, kvh, p = steps[i]
            if kvh == 0 and p == 0:
                if T not in state:
                    load_tile(T)
                if T + 1 < NT and (T + 1) not in state:
                    load_tile(T + 1)
            s = state[T]
            S, Stok = Sr.next()
            for u in range(2):
                kt = 2 * p + u
                P.op("pe", lambda e, kt=kt, u=u: e.matmul(
                    S[:, u * 512:(u + 1) * 512], lhsT=C.KT[:, kvh, kt * 128:(kt + 1) * 128],
                    rhs=s["qT"][:, kvh * 512:(kvh + 1) * 512], start=True, stop=True),
                    reads=[("KT", kt), s["qTtok"]], writes=[(Stok, u)])
            PT, PTtok = PTr.next()
            P.op("act", lambda e: e.activation(out=PT[:], in_=S[:], func=AF.Exp, scale=SCALE),
                 reads=[(Stok, 0), (Stok, 1)], writes=[PTtok])
            return PT, PTtok

        def emit_PV(i, PT, PTtok):
            T, kvh, p = steps[i]
            s = state[T]
            if p == 0:
                s["acc"], s["acctok"] = accr.next()
                s["sacc"], s["sacctok"] = saccr.next()
            acc, acctok, sacc, sacctok = s["acc"], s["acctok"], s["sacc"], s["sacctok"]
            for u in range(2):
                kt = 2 * p + u
                P.op("pe", lambda e, kt=kt, u=u: e.matmul(
                    acc[:], lhsT=C.Vs[:, kt, kvh * 128:(kvh + 1) * 128], rhs=PT[:, u * 512:(u + 1) * 512],
                    start=(kt == 0), stop=(kt == NKT - 1)),
                    reads=[("V", kt), PTtok], writes=[acctok])
            a = p % 2
            if p < 2:
                P.op("dve", lambda e: e.tensor_copy(out=sacc[:, a, :], in_=PT[:]), reads=[PTtok], writes=[(sacctok, a)])
            else:
                P.op("dve", lambda e: e.tensor_tensor(out=sacc[:, a, :], in0=sacc[:, a, :], in1=PT[:], op=ALU.add),
                     reads=[PTtok, (sacctok, a)], writes=[(sacctok, a)])
            if p == NKP - 1:
                OT, OTtok = s["OT"], s["OTtok"]

                def fin(T=T, kvh=kvh, acc=acc, acctok=acctok, sacc=sacc, sacctok=sacctok, OT=OT, OTtok=OTtok):
                    for u in range(4):
                        P.op("pe", lambda e, u=u: e.matmul(den_ps[:], lhsT=C.ones[:],
                                                           rhs=sacc[:, u // 2, (u % 2) * 512:(u % 2 + 1) * 512],
                                                           start=(u == 0), stop=(u == 3)),
                             reads=["ones", (sacctok, 0), (sacctok, 1)], writes=["den_ps"])
                    rec, rectok = recr.next()
                    P.op("dve", lambda e: e.reciprocal(out=rec[:], in_=den_ps[:]), reads=["den_ps"], writes=[rectok])
                    P.op("dve", lambda e: e.tensor_tensor(
                        out=OT[:, kvh * 4:(kvh + 1) * 4, :].rearrange("p h q -> p (h q)"), in0=acc[:], in1=rec[:], op=ALU.mult),
                        reads=[acctok, rectok], writes=[(OTtok, kvh)])
                    if kvh == NKV - 1:
                        pending.append([5, lambda T=T: emit_wo(T)])

                pending.append([2, fin])

        def emit_wo(T):
            s = state[T]
            OT, OTtok = s["OT"], s["OTtok"]
            x1t, x1tok = x1r.next()

            def chunk(half, c):
                sl = slice(half * 512, (half + 1) * 512)
                for h in (2 * c, 2 * c + 1):
                    P.op("pe", lambda e, h=h: e.matmul(
                        wo_ps[:], lhsT=OT[:, h, :], rhs=wo[:, h, half * 512:(half + 1) * 512],
                        start=(h == 0), stop=(h == 7)),
                        reads=[(OTtok, h // 4), ("wo", h)], writes=["wo_ps"])
                if c == 3:
                    P.op("dve", lambda e: e.tensor_tensor(out=x1t[:, sl], in0=wo_ps[:], in1=gate[:, sl], op=ALU.mult),
                         reads=["wo_ps", "gate"], writes=[(x1tok, half)])
                    P.op("pool", lambda e: e.tensor_tensor(out=x1t[:, sl], in0=x1t[:, sl], in1=s["xt"][:, sl], op=ALU.add),
                         reads=[(x1tok, half), s["xtok"]], writes=[(x1tok, half)])
                    if half == 1:
                        P.dma("pool", C.x1_d[T * 128:(T + 1) * 128, :], x1t[:], reads=[(x1tok, 0), (x1tok, 1)],
                              writes=[("x1_d", T)])
                        del state[T]
                        return
                nh, ncn = (half, c + 1) if c < 3 else (half + 1, 0)
                pending.append([1, lambda: chunk(nh, ncn)])

            chunk(0, 0)

        def tick():
            for p in list(pending):
                p[0] -= 1
                if p[0] <= 0:
                    pending.remove(p)
                    p[1]()

        LA = 2
        q = [emit_S(i) for i in range(LA)]
        for i in range(len(steps)):
            if i + LA < len(steps):
                q.append(emit_S(i + LA))
            emit_PV(i, *q.pop(0))
            tick()
        while pending:
            tick()
        P.barrier()


def ffn_phase(C, layer, src_d, dst_d, final=False):
    nc, P, sb, ps = C.nc, C.P, C.sb, C.ps
    a_idx = 2 if layer == 0 else 4
    gi = layer * 2 + 1
    with ExitStack() as st:
        wgu = sb("wgu", [128, 8, 2 * DFF], BF16, st=st)
        wd = sb("wd", [128, NFC, D], BF16, st=st)
        wguv = C.wgu_d[layer].rearrange("(kc p) f -> p kc f", p=128)
        wdv = C.wd_d[layer].rearrange("(fc p) d -> p fc d", p=128)
        NG = (NFC + 3) // 4
        for g in range(NG):
            w = min(512, DFF - g * 512)
            for hh in range(2):
                c0 = hh * DFF + g * 512
                for kc in range(8):
                    P.dma("pool", wgu[:, kc, c0:c0 + w], wguv[:, kc, c0:c0 + w], writes=[("wgu", hh, g, kc)])
        for fc in range(NFC):
            P.dma("pool", wd[:, fc, :], wdv[:, fc, :], writes=[("wd", fc)])
        xrr = Ring(st, nc, "xres", 2, [128, D], F32)
        gate = xrr.tiles[1]
        P.dma("sp", gate[:], C.gates_d[gi].partition_broadcast(128), writes=[("xres", 1)])
        for fc in range(NFC):
            P.op("pool", lambda e, fc=fc: e.tensor_tensor(out=wd[:, fc, :], in0=wd[:, fc, :], in1=gate[:], op=ALU.mult),
                 reads=[("wd", fc), ("xres", 1)], writes=[("wdg", fc)])
        if final:
            gfin = sb("gfin", [128, D], st=st)
            P.dma("sp", gfin[:], C.gfin_d.partition_broadcast(128), writes=["gfin"])
        xtr = Ring(st, nc, "fxt", 2, [128, D], F32)
        ssr = Ring(st, nc, "fss", 4, [128, 4], F32)
        xnr = Ring(st, nc, "fxn", 4, [128, D], BF16)
        tpr = Ring(st, nc, "ftp", 2, [128, D], BF16, psum=True)
        junk = sb("fjunk", [128, D], BF16, st=st)
        hT2 = sb("hT2", [128, 8, 512], BF16, st=st)
        aT = sb("aT", [128, NFC, 512], BF16, st=st)
        Gr = Ring(st, nc, "G_ps", 2, [128, 512], F32, psum=True)
        Ur = Ring(st, nc, "U_ps", 2, [128, 512], F32, psum=True)
        Dr = Ring(st, nc, "D_ps", 2, [128, 512], F32, psum=True)
        sgr = Ring(st, nc, "sg", 2, [128, 512], F32)
        fss = Ring(st, nc, "finss", 2, [128, 4], F32)
        print("ffn sbuf remaining", nc.sbuf_bytes_remaining)
        NB = NT // 4
        xn_info = {}

        def rms_act(T):
            xt, xtok = xtr.next()
            P.dma("sp", xt[:], src_d[T * 128:(T + 1) * 128, :], writes=[xtok])
            ss, sstok = ssr.next()
            xn, xntok = xnr.next()
            P.op("act", lambda e: e.activation(out=junk[:], in_=xt[:], func=AF.Square, accum_out=ss[:, 0:1]),
                 reads=[xtok], writes=["junk", (sstok, 0)])
            P.op("act", lambda e: e.activation(out=ss[:, 1:2], in_=ss[:, 0:1], func=AF.Ln, scale=1.0 / D, bias=C.eps[:, 0:1]),
                 reads=[(sstok, 0), "eps"], writes=[(sstok, 1)])
            P.op("act", lambda e: e.activation(out=ss[:, 2:3], in_=ss[:, 1:2], func=AF.Exp, scale=-0.5),
                 reads=[(sstok, 1)], writes=[(sstok, 2)])
            P.op("act", lambda e: e.activation(out=xn[:], in_=xt[:], func=AF.Identity, scale=ss[:, 2:3]),
                 reads=[xtok, (sstok, 2)], writes=[xntok])
            xn_info[T] = (xn, xntok)

        def rms_pe(bk):
            for j in range(4):
                xn, xntok = xn_info.pop(bk * 4 + j)
                tp, tptok = tpr.next()
                for kc in range(8):
                    P.op("pe", lambda e, kc=kc: e.transpose(out=tp[:, kc * 128:(kc + 1) * 128], in_=xn[:, kc * 128:(kc + 1) * 128],
                                                             identity=C.ident[:]),
                         reads=[xntok, "ident"], writes=[tptok])
                for kc in range(8):
                    P.op("dve", lambda e, kc=kc: e.tensor_scalar(
                        out=hT2[:, kc, j * 128:(j + 1) * 128], in0=tp[:, kc * 128:(kc + 1) * 128],
                        scalar1=C.Amod[:, a_idx, kc:kc + 1], scalar2=C.Bmod[:, a_idx, kc:kc + 1], op0=ALU.mult, op1=ALU.add),
                        reads=[tptok, ("Amod", a_idx), ("Bmod", a_idx)], writes=[("hT2", j)])

        def emit_gu(bk):
            for fc in range(NFC):
                G, Gtok = Gr.next()
                U, Utok = Ur.next()
                for (ps_t, ps_tok, hh) in ((G, Gtok, 0), (U, Utok, 1)):
                    c0 = hh * DFF + fc * 128
                    for kc in range(8):
                        P.op("pe", lambda e, ps_t=ps_t, c0=c0, kc=kc: e.matmul(
                            ps_t[:], lhsT=wgu[:, kc, c0:c0 + 128], rhs=hT2[:, kc, :], start=(kc == 0), stop=(kc == 7)),
                            reads=[("wgu", hh, fc // 4, kc)] + [("hT2", j) for j in range(4)], writes=[ps_tok])
                sg, sgtok = sgr.next()
                P.op("act", lambda e, sg=sg, G=G: e.activation(out=sg[:], in_=G[:], func=AF.Silu), reads=[Gtok], writes=[sgtok])
                P.op("dve", lambda e, sg=sg, U=U, fc=fc: e.tensor_tensor(out=aT[:, fc, :], in0=U[:], in1=sg[:], op=ALU.mult),
                     reads=[Utok, sgtok], writes=[("aT", fc)])
                if bk + 1 < NB and fc in (2, 7, 12, 17):
                    rms_act((bk + 1) * 4 + (2, 7, 12, 17).index(fc))

        def emit_down(bk):
            for j in range(4):
                T = bk * 4 + j
                xr, xrtok = xrr.next()
                P.dma("sp", xr[:], src_d[T * 128:(T + 1) * 128, :], writes=[xrtok])
                for half in range(2):
                    Dp, Dtok = Dr.next()
                    sl = slice(half * 512, (half + 1) * 512)
                    for fc in range(NFC):
                        P.op("pe", lambda e, Dp=Dp, fc=fc, sl=sl, j=j: e.matmul(
                            Dp[:], lhsT=aT[:, fc, j * 128:(j + 1) * 128], rhs=wd[:, fc, sl],
                            start=(fc == 0), stop=(fc == NFC - 1)),
                            reads=[("aT", fc), ("wdg", fc)], writes=[Dtok])
                    P.op("dve", lambda e, Dp=Dp, sl=sl, xr=xr: e.tensor_tensor(out=xr[:, sl], in0=Dp[:], in1=xr[:, sl], op=ALU.add),
                         reads=[Dtok, xrtok], writes=[xrtok])
                if final:
                    fs, fstok = fss.next()
                    P.op("act", lambda e, xr=xr, fs=fs: e.activation(out=junk[:], in_=xr[:], func=AF.Square, accum_out=fs[:, 0:1]),
                         reads=[xrtok], writes=["junk", (fstok, 0)])
                    P.op("act", lambda e, fs=fs: e.activation(out=fs[:, 1:2], in_=fs[:, 0:1], func=AF.Ln, scale=1.0 / D, bias=C.eps[:, 0:1]),
                         reads=[(fstok, 0), "eps"], writes=[(fstok, 1)])
                    P.op("act", lambda e, fs=fs: e.activation(out=fs[:, 2:3], in_=fs[:, 1:2], func=AF.Exp, scale=-0.5),
                         reads=[(fstok, 1)], writes=[(fstok, 2)])
                    P.op("dve", lambda e, xr=xr, fs=fs: e.scalar_tensor_tensor(
                        out=xr[:], in0=xr[:], scalar=fs[:, 2:3], in1=gfin[:], op0=ALU.mult, op1=ALU.mult),
                        reads=[xrtok, (fstok, 2), "gfin"], writes=[xrtok])
                P.dma("pool", dst_d[T * 128:(T + 1) * 128, :], xr[:], reads=[xrtok], writes=[("dst", T)])

        for j in range(4):
            rms_act(j)
        rms_pe(0)
        for bk in range(NB):
            emit_gu(bk)
            if bk + 1 < NB:
                rms_pe(bk + 1)
            emit_down(bk)
        P.barrier()


def run_pipeline(order, ntiles, extra=None):
    maxskew = max(s for _, s in order)
    for i in range(ntiles + maxskew):
        for fn, skew in order:
            T = i - skew
            if 0 <= T < ntiles:
                fn(T)
        if extra is not None:
            extra(i)


def phase4(C):
    nc, P, sb, ps = C.nc, C.P, C.sb, C.ps
    with ExitStack() as st:
        Fc = sb("Fc", [128, 2, 512], BF16, st=st)
        P.dma("pool", Fc[:], C.Fc_d, writes=["Fc"])
        xtr = Ring(st, nc, "p4xt", 3, [128, D], F32)
        ssr = Ring(st, nc, "p4ss", 3, [128, 4], F32)
        xnr = Ring(st, nc, "p4xn", 2, [128, D], BF16)
        tpr = Ring(st, nc, "p4tp", 2, [128, D], BF16, psum=True)
        junk = sb("p4junk", [128, D], BF16, st=st)
        hTr = Ring(st, nc, "p4hT", 2, [128, 8, 128], BF16)
        Zp = [ps("Z_ps%d" % g, [128, 512], st=st) for g in range(4)]
        zsr = Ring(st, nc, "zs", 3, [128, 2, D], BF16)
        info = {}

        def sA(T):
            xt, xtok = xtr.next()
            P.dma("sp", xt[:], C.x2_d[T * 128:(T + 1) * 128, :], writes=[xtok])
            info[T] = dict(xt=xt, xtok=xtok)

        def sB(T):
            s = info[T]
            xt, xtok = s["xt"], s["xtok"]
            ss, sstok = ssr.next()
            xn, xntok = xnr.next()
            P.op("act", lambda e: e.activation(out=junk[:], in_=xt[:], func=AF.Square, accum_out=ss[:, 0:1]),
                 reads=[xtok], writes=["junk", (sstok, 0)])
            P.op("act", lambda e: e.activation(out=ss[:, 1:2], in_=ss[:, 0:1], func=AF.Ln, scale=1.0 / D, bias=C.eps[:, 0:1]),
                 reads=[(sstok, 0), "eps"], writes=[(sstok, 1)])
            P.op("act", lambda e: e.activation(out=ss[:, 2:3], in_=ss[:, 1:2], func=AF.Exp, scale=-0.5),
                 reads=[(sstok, 1)], writes=[(sstok, 2)])
            P.op("act", lambda e: e.activation(out=xn[:], in_=xt[:], func=AF.Identity, scale=ss[:, 2:3]),
                 reads=[xtok, (sstok, 2)], writes=[xntok])
            s.update(xn=xn, xntok=xntok)

        def sE(T):
            s = info[T]
            tp, tptok = tpr.next()
            for kc in range(8):
                P.op("pe", lambda e, kc=kc: e.transpose(out=tp[:, kc * 128:(kc + 1) * 128],
                                                         in_=s["xn"][:, kc * 128:(kc + 1) * 128], identity=C.ident[:]),
                     reads=[s["xntok"], "ident"], writes=[tptok])
            s.update(tp=tp, tptok=tptok)

        def sF(T):
            s = info[T]
            hT, hTtok = hTr.next()
            for kc in range(8):
                P.op("dve", lambda e, kc=kc: e.tensor_scalar(
                    out=hT[:, kc, :], in0=s["tp"][:, kc * 128:(kc + 1) * 128],
                    scalar1=C.Amod[:, 3, kc:kc + 1], scalar2=C.Bmod[:, 3, kc:kc + 1], op0=ALU.mult, op1=ALU.add),
                    reads=[s["tptok"], ("Amod", 3), ("Bmod", 3)], writes=[hTtok])
            s.update(hT=hT, hTtok=hTtok)

        def sG(T):
            s = info[T]
            for g in range(4):
                for cc in range(2):
                    P.op("pe", lambda e, g=g, cc=cc: e.matmul(Zp[g][:], lhsT=s["hT"][:, 2 * g + cc, :], rhs=Fc[:, cc, :],
                                                               start=(cc == 0), stop=(cc == 1)),
                         reads=[s["hTtok"], "Fc"], writes=[("Z_ps", g)])

        def sH(T):
            zs, zstok = zsr.next()
            for g in range(4):
                if g == 0:
                    P.op("act", lambda e, g=g: e.activation(
                        out=zs[:, :, g * 256:(g + 1) * 256], in_=Zp[g][:].rearrange("p (r c) -> p r c", r=2), func=AF.Identity),
                        reads=[("Z_ps", g)], writes=[(zstok, g)])
                else:
                    P.op("dve", lambda e, g=g: e.tensor_copy(
                        out=zs[:, :, g * 256:(g + 1) * 256], in_=Zp[g][:].rearrange("p (r c) -> p r c", r=2)),
                        reads=[("Z_ps", g)], writes=[(zstok, g)])
            P.dma("pool", C.Z_d[:, T * 128:(T + 1) * 128, :].rearrange("r t c -> t r c"), zs[:],
                  reads=[(zstok, g) for g in range(4)], writes=[("Z_d", T)])
            del info[T]

        run_pipeline([(sH, 5), (sG, 4), (sF, 3), (sE, 2), (sB, 1), (sA, 0)], NT)
        P.barrier()


def phase5(C):
    nc, P, sb, ps = C.nc, C.P, C.sb, C.ps
    NB2 = 8
    with ExitStack() as st:
        M1 = sb("M1", [128, 128], BF16, st=st)
        P.dma("pool", M1[:], C.M1_d, writes=["M1"])
        ztr = Ring(st, nc, "zt", 2, [128, NB2, D], BF16)
        t1r = Ring(st, nc, "t1s", 2, [128, NB2, D], BF16)
        Tr = Ring(st, nc, "T1_ps", 4, [128, 512], F32, psum=True)
        zv = C.Z_d.rearrange("r (n1 n2) c -> (r n1) n2 c", n2=64)
        k = 0
        for blk in range(64 // NB2):
            zt, zttok = ztr.next()
            P.dma("sp", zt[:], zv[:, blk * NB2:(blk + 1) * NB2, :], writes=[zttok])
            t1s, t1tok = t1r.next()
            for i in range(NB2):
                for half in range(2):
                    Tp, Ttok = Tr.next()
                    sl = slice(half * 512, (half + 1) * 512)
                    P.op("pe", lambda e, Tp=Tp, i=i, sl=sl: e.matmul(Tp[:], lhsT=M1[:], rhs=zt[:, i, sl], start=True, stop=True),
                         reads=["M1", zttok], writes=[Ttok])
                    if k % 2 == 0:
                        P.op("act", lambda e, Tp=Tp, i=i, sl=sl: e.activation(out=t1s[:, i, sl], in_=Tp[:], func=AF.Identity),
                             reads=[Ttok], writes=[(t1tok, i, half)])
                    else:
                        P.op("dve", lambda e, Tp=Tp, i=i, sl=sl: e.tensor_copy(out=t1s[:, i, sl], in_=Tp[:]),
                             reads=[Ttok], writes=[(t1tok, i, half)])
                    k += 1
            rd = [(t1tok, i, h) for i in range(NB2) for h in range(2)]
            for r in range(2):
                P.dma("pool", C.T1_d[r, blk * NB2:(blk + 1) * NB2, :, :].rearrange("n2 k1 c -> k1 n2 c"),
                      t1s[r * 64:(r + 1) * 64, :, :], reads=rd, writes=[("T1_d", blk, r)])
        P.barrier()


def phase6(C):
    nc, P, sb, ps = C.nc, C.P, C.sb, C.ps
    NB1 = 8
    with ExitStack() as st:
        M2 = sb("M2", [128, 64, 64], BF16, st=st)
        P.dma("pool", M2[:], C.M2_d, writes=["M2"])
        ttr = Ring(st, nc, "tt", 2, [128, NB1, D], BF16)
        fsr = Ring(st, nc, "fs", 2, [64, NB1, D], BF16)
        Fr = Ring(st, nc, "f_ps", 4, [64, 512], F32, psum=True)
        tv = C.T1_d.rearrange("r n2 k1 c -> (r n2) k1 c")
        fv = C.f_d.rearrange("(k2 k1) c -> k2 k1 c", k1=64)
        k = 0
        for blk in range(64 // NB1):
            tt, tttok = ttr.next()
            P.dma("sp", tt[:], tv[:, blk * NB1:(blk + 1) * NB1, :], writes=[tttok])
            fs, fstok = fsr.next()
            for i in range(NB1):
                k1 = blk * NB1 + i
                for half in range(2):
                    Fp, Ftok = Fr.next()
                    sl = slice(half * 512, (half + 1) * 512)
                    P.op("pe", lambda e, Fp=Fp, i=i, sl=sl, k1=k1: e.matmul(Fp[:], lhsT=M2[:, k1, :], rhs=tt[:, i, sl],
                                                                        start=True, stop=True),
                         reads=["M2", tttok], writes=[Ftok])
                    if k % 2 == 0:
                        P.op("act", lambda e, Fp=Fp, i=i, sl=sl: e.activation(out=fs[:, i, sl], in_=Fp[:], func=AF.Identity),
                             reads=[Ftok], writes=[(fstok, i, half)])
                    else:
                        P.op("dve", lambda e, Fp=Fp, i=i, sl=sl: e.tensor_copy(out=fs[:, i, sl], in_=Fp[:]),
                             reads=[Ftok], writes=[(fstok, i, half)])
                    k += 1
            P.dma("pool", fv[:, blk * NB1:(blk + 1) * NB1, :], fs[:],
                  reads=[(fstok, i, h) for i in range(NB1) for h in range(2)], writes=[("f_d", blk)])
        P.barrier()
    with ExitStack() as st:
        wf = sb("wf", [128, 8, D], BF16, st=st)
        wfg = sb("wfg", [128, 8, D], BF16, st=st)
        wfv = C.wf_d.rearrange("(kc p) f -> p kc f", p=128)
        for kc in range(8):
            P.dma("pool", wf[:, kc, :], wfv[:, kc, :], writes=[("wf", kc)])
        gate = sb("gate_p6", [128, D], st=st)
        bfr = sb("bf_row", [1, D], st=st)
        bfg = sb("bfg_row", [1, D], BF16, st=st)
        P.dma("sp", gate[:], C.gates_d[2].partition_broadcast(128), writes=["gate"])
        P.dma("sp", bfr[:], C.bf_d.partition_broadcast(1), writes=["bfr"])
        for kc in range(8):
            P.op("dve", lambda e, kc=kc: e.tensor_tensor(out=wfg[:, kc, :], in0=wf[:, kc, :], in1=gate[:], op=ALU.mult),
                 reads=[("wf", kc), "gate"], writes=[("wfg", kc)])
        P.op("dve", lambda e: e.tensor_tensor(out=bfg[:], in0=bfr[:], in1=gate[0:1, :], op=ALU.mult),
             reads=["bfr", "gate"], writes=["bfg"])
        ftr = Ring(st, nc, "ft", 3, [128, D], BF16)
        tpr = Ring(st, nc, "p6tp", 2, [128, D], BF16, psum=True)
        fTr = Ring(st, nc, "fT", 2, [128, 8, 128], BF16)
        xtr = Ring(st, nc, "p6xt", 5, [128, D], F32)
        xor_ = Ring(st, nc, "p6xo", 2, [128, D], F32)
        Wp = [ps("wf_ps%d" % i, [128, 512], st=st) for i in range(2)]
        info = {}

        def sA(T):
            ft, fttok = ftr.next()
            P.dma("sp", ft[:], C.f_d[T * 128:(T + 1) * 128, :], writes=[fttok])
            xt, xtok = xtr.next()
            P.dma("sp", xt[:], C.x2_d[T * 128:(T + 1) * 128, :], writes=[xtok])
            info[T] = dict(ft=ft, fttok=fttok, xt=xt, xtok=xtok)

        def sE(T):
            s = info[T]
            tp, tptok = tpr.next()
            for kc in range(8):
                P.op("pe", lambda e, kc=kc: e.transpose(out=tp[:, kc * 128:(kc + 1) * 128],
                                                         in_=s["ft"][:, kc * 128:(kc + 1) * 128], identity=C.ident[:]),
                     reads=[s["fttok"], "ident"], writes=[tptok])
            s.update(tp=tp, tptok=tptok)

        def sF(T):
            s = info[T]
            fT, fTtok = fTr.next()
            P.op("act", lambda e: e.activation(out=fT[:].rearrange("p k t -> p (k t)"), in_=s["tp"][:], func=AF.Identity),
                 reads=[s["tptok"]], writes=[fTtok])
            s.update(fT=fT, fTtok=fTtok)

        def sG(T):
            s = info[T]
            for half in range(2):
                sl = slice(half * 512, (half + 1) * 512)
                for kc in range(8):
                    P.op("pe", lambda e, half=half, kc=kc, sl=sl: e.matmul(Wp[half][:], lhsT=s["fT"][:, kc, :], rhs=wfg[:, kc, sl],
                                                                           start=(kc == 0), stop=False),
                         reads=[s["fTtok"], ("wfg", kc)], writes=[("wf_ps", half)])
                P.op("pe", lambda e, half=half, sl=sl: e.matmul(Wp[half][:], lhsT=C.ones[0:1, :], rhs=bfg[0:1, sl],
                                                                start=False, stop=True),
                     reads=["ones", "bfg"], writes=[("wf_ps", half)])

        def sH(T):
            s = info[T]
            xo, xotok = xor_.next()
            for half in range(2):
                sl = slice(half * 512, (half + 1) * 512)
                P.op("dve", lambda e, half=half, sl=sl: e.tensor_tensor(out=xo[:, sl], in0=Wp[half][:], in1=s["xt"][:, sl], op=ALU.add),
                     reads=[("wf_ps", half), s["xtok"]], writes=[(xotok, half)])
            P.dma("pool", C.x3_d[T * 128:(T + 1) * 128, :], xo[:], reads=[(xotok, 0), (xotok, 1)], writes=[("x3_d", T)])
            del info[T]

        run_pipeline([(sH, 4), (sG, 3), (sF, 2), (sE, 1), (sA, 0)], NT)
        P.barrier()


def _dft_tables():
    c = np.arange(256, dtype=np.float64)
    ang = 2 * np.pi * np.outer(c, c) / 256.0
    Cc, Sc = np.cos(ang) / 16.0, np.sin(ang) / 16.0
    fc = np.concatenate([Cc, -Sc], axis=1)
    fc = fc.reshape(2, 128, 512).transpose(1, 0, 2)
    n = np.arange(64, dtype=np.float64)
    a1 = 2 * np.pi * np.outer(n, n) / 64.0
    C1, S1 = np.cos(a1) / 8.0, np.sin(a1) / 8.0
    m1 = np.block([[C1, -S1], [S1, C1]])
    n2 = n[:, None, None]; k1 = n[None, :, None]; k2 = n[None, None, :]
    th = 2 * np.pi * (n2 * k2 / 64.0 + n2 * k1 / 4096.0)
    m2 = np.concatenate([np.cos(th), np.sin(th)], axis=0) / 8.0
    f32 = lambda a: np.ascontiguousarray(a.astype(np.float32))
    return f32(fc), f32(m1), f32(m2)


def _rope_tables():
    rows = SEQ // 64
    row = np.repeat(np.arange(rows), 64).astype(np.float32)
    col = np.tile(np.arange(64), rows).astype(np.float32)
    inv_freq = (np.float32(10000.0) ** (-np.arange(32, dtype=np.float32) / np.float32(32))).astype(np.float32)
    ang = np.concatenate([row[:, None] * inv_freq, col[:, None] * inv_freq], axis=-1).astype(np.float32)
    return np.cos(ang).astype(np.float32), np.sin(ang).astype(np.float32)


def _host_inputs(inputs):
    f = lambda a: np.ascontiguousarray(np.asarray(a, dtype=np.float32))
    x = f(inputs["x"]); c = f(inputs["c"]); ctx = f(inputs["ctx"]); c_ctx = f(inputs["c_ctx"])
    w_mod = f(inputs["w_mod"]); b_mod = f(inputs["b_mod"])
    pp = lambda v: np.ascontiguousarray(v.reshape(-1, 128).T)
    b_mod_pp = np.stack([pp(b_mod[l]) for l in range(2)])
    g_mix_pp = np.stack([pp(f(inputs["g_mix"])[l]) for l in range(2)])
    g_ffn_pp = np.stack([pp(f(inputs["g_ffn"])[l]) for l in range(2)])
    cos, sin = _rope_tables()
    shared = {
        "w_mod": w_mod, "b_mod_pp": b_mod_pp, "b_mod": b_mod, "g_mix_pp": g_mix_pp, "g_ffn_pp": g_ffn_pp,
        "w_qkv": f(inputs["w_qkv"])[0], "g_q": f(inputs["g_q"])[0], "g_k": f(inputs["g_k"])[0],
        "w_o": f(inputs["w_attn_out"])[0], "rope_cos": cos, "rope_sin": sin,
        "ident": np.eye(128, dtype=np.float32),
        "w_gate_up": f(inputs["w_gate_up"]), "w_down": f(inputs["w_down"]),
        "w_fourier": f(inputs["w_fourier"])[0], "b_fourier": f(inputs["b_fourier"])[0],
        "g_final": f(inputs["g_final"]),
    }
    shared["dft_fc"], shared["dft_m1"], shared["dft_m2"] = _dft_tables()
    maps = []
    for b in range(NCORES):
        c_pp = np.ascontiguousarray(np.stack([pp(c[b]), pp(c_ctx)], axis=-1))
        m = {"x": x[b], "ctx": ctx[b], "c_pp": c_pp}
        m.update(shared)
        maps.append(m)
    return maps


def kernel(**inputs):
    nc = build_program()
    maps = _host_inputs(inputs)
    res = run_bass_kernel_spmd(nc, maps, core_ids=list(range(NCORES)))
    return np.stack([np.asarray(r["out"], dtype=np.float32) for r in res.results], axis=0)
```
